# Optimizing a Trainium2 kernel written in Bass

```python
import jax
import jax.numpy as jnp
from jax import lax
import numpy as np

D_MODEL = 1024
BATCH = 32
SEQ = 2048
DEPTH = 1

GRID_W = 64
CTX_LEN = 256
CTX_CHUNK = 64
N_MOD = 6
EPS = 1e-6

HG_HEADS = 8
HG_HEAD_K = 128
HG_HEAD_V = D_MODEL // HG_HEADS
HG_KEY = HG_HEADS * HG_HEAD_K
HG_VAL = HG_HEADS * HG_HEAD_V

GLA_HEADS = 4
GLA_KEY = D_MODEL // 2
GLA_VAL = D_MODEL
GLA_HEAD_K = GLA_KEY // GLA_HEADS
GLA_HEAD_V = GLA_VAL // GLA_HEADS
GLA_GATE_RANK = 16
GLA_GATE_NORMALIZER = 16.0

IN_SIZES = (HG_KEY, HG_KEY, HG_KEY, HG_VAL, HG_VAL,
            GLA_KEY, GLA_KEY, GLA_VAL, GLA_VAL, GLA_GATE_RANK, GLA_GATE_RANK,
            D_MODEL, D_MODEL)
D_IN = 3 * HG_KEY + 2 * HG_VAL + 2 * GLA_KEY + 2 * GLA_VAL + 2 * GLA_GATE_RANK + 2 * D_MODEL

PEER_HEADS = 8
PEER_N_KEYS = 128
PEER_EXPERTS = PEER_N_KEYS * PEER_N_KEYS
PEER_QUERY_DIM = 256
PEER_SUB_DIM = PEER_QUERY_DIM // 2
PEER_TOPK = 16
PEER_BLOCK = 128

kernel_name = "hybrid_hgrn2_gla_peer_dit_layer"


def rmsnorm(x, g):
    xf = x.astype(jnp.float32)
    xf = xf * lax.rsqrt(jnp.mean(xf * xf, axis=-1, keepdims=True) + EPS)
    return (xf * g.astype(jnp.float32)).astype(x.dtype)


def modulate(h, shift, scale):
    return h * (1.0 + scale) + shift


def to_heads(a, n_heads):
    bsz, t, _ = a.shape
    return a.reshape(bsz, t, n_heads, -1).transpose(0, 2, 1, 3)


def flip_t(a):
    return a[:, :, ::-1]


def hgrn_lower_bounds(logits):
    p = jax.nn.softmax(logits.astype(jnp.float32), axis=0)
    return jnp.cumsum(p, axis=0)[:DEPTH]


def hgrn_forget(z, lb):
    f = lb + (1.0 - lb) * jax.nn.sigmoid(z.astype(jnp.float32))
    return 1.0 - f, jnp.log(f)


def chunk_gla(q, k, v, log_g, s0, n_chunks):
    bsz, nh, t, _ = q.shape
    dv = v.shape[-1]
    chunk = t // n_chunks

    def blocks(a):
        return jnp.moveaxis(a.reshape(bsz, nh, n_chunks, chunk, a.shape[-1]), 2, 0)

    qc, kc, vc = blocks(q), blocks(k), blocks(v)
    b = jnp.cumsum(blocks(log_g).astype(jnp.float32), axis=3)
    b_last = b[:, :, :, chunk - 1:chunk]
    b_ref = b[:, :, :, chunk // 2:chunk // 2 + 1]
    scores = jnp.einsum("nbhck,nbhsk->nbhcs", qc * jnp.exp(b - b_ref), kc * jnp.exp(b_ref - b))
    causal_in_scan = jnp.tril(jnp.ones((chunk, chunk), dtype=bool))
    o_intra = jnp.einsum("nbhcs,nbhsv->nbhcv", jnp.where(causal_in_scan, scores, 0.0), vc)
    q_dec = qc * jnp.exp(b)
    k_dec = kc * jnp.exp(b_last - b)

    def step(state, inp):
        q_i, k_i, v_i, bl = inp
        o_inter = jnp.einsum("bhck,bhkv->bhcv", q_i, state)
        state = jnp.exp(bl)[:, :, 0, :, None] * state + jnp.einsum("bhsk,bhsv->bhkv", k_i, v_i)
        return state, o_inter

    _, o_inter = lax.scan(step, s0, (q_dec, k_dec, vc, b_last))
    o = o_intra + o_inter
    return jnp.moveaxis(o, 0, 2).reshape(bsz, nh, t, dv)


def final_state(k, v, log_g):
    b = jnp.cumsum(log_g.astype(jnp.float32), axis=2)
    return jnp.einsum("bhtk,bhtv->bhkv", k * jnp.exp(b[:, :, -1:] - b), v)


def mixer_inputs(h, w_in, lb, gk_w, gk_b):
    splits = [int(s) for s in np.cumsum(IN_SIZES)[:-1]]
    (hq, hf_fwd, hf_bwd, hi, hg, gq, gk, gv, gg, gr_fwd, gr_bwd, m_hg, m_gla) = jnp.split(
        h @ w_in, splits, axis=-1)
    hk_fwd, hlf_fwd = hgrn_forget(hf_fwd, lb[0])
    hk_bwd, hlf_bwd = hgrn_forget(hf_bwd, lb[1])
    glf_fwd = jax.nn.log_sigmoid((gr_fwd @ gk_w[0] + gk_b[0]).astype(jnp.float32)) / GLA_GATE_NORMALIZER
    glf_bwd = jax.nn.log_sigmoid((gr_bwd @ gk_w[1] + gk_b[1]).astype(jnp.float32)) / GLA_GATE_NORMALIZER
    gla_k = to_heads(gk, GLA_HEADS)
    return {
        "hg_q": to_heads(hq, HG_HEADS) * HG_HEAD_K ** -0.5,
        "hg_v": to_heads(hi, HG_HEADS),
        "hg_k": (to_heads(hk_fwd, HG_HEADS), to_heads(hk_bwd, HG_HEADS)),
        "hg_lg": (to_heads(hlf_fwd, HG_HEADS), to_heads(hlf_bwd, HG_HEADS)),
        "hg_gate": hg,
        "gla_q": to_heads(gq, GLA_HEADS) * GLA_HEAD_K ** -0.5,
        "gla_v": to_heads(gv, GLA_HEADS),
        "gla_k": (gla_k, gla_k),
        "gla_lg": (to_heads(glf_fwd, GLA_HEADS), to_heads(glf_bwd, GLA_HEADS)),
        "gla_gate": gg,
        "merge": (m_hg, m_gla),
    }


def context_states(t):
    def both(k_pair, v, lg_pair):
        return (final_state(k_pair[0], v, lg_pair[0]),
                final_state(flip_t(k_pair[1]), flip_t(v), flip_t(lg_pair[1])))
    return {"hg": both(t["hg_k"], t["hg_v"], t["hg_lg"]),
            "gla": both(t["gla_k"], t["gla_v"], t["gla_lg"])}


def zero_states(bsz):
    hg = jnp.zeros((bsz, HG_HEADS, HG_HEAD_K, HG_HEAD_V), jnp.float32)
    gla = jnp.zeros((bsz, GLA_HEADS, GLA_HEAD_K, GLA_HEAD_V), jnp.float32)
    return {"hg": (hg, hg), "gla": (gla, gla)}


def bidir(q, k_pair, v, lg_pair, s_pair, n_chunks):
    o_fwd = chunk_gla(q, k_pair[0], v, lg_pair[0], s_pair[0], n_chunks)
    o_bwd = chunk_gla(flip_t(q), flip_t(k_pair[1]), flip_t(v), flip_t(lg_pair[1]), s_pair[1], n_chunks)
    return o_fwd + flip_t(o_bwd)


def gated_head_norm(o, gate, g):
    bsz, nh, t, dv = o.shape
    on = rmsnorm(o.transpose(0, 2, 1, 3), g)
    y = on * jax.nn.silu(gate.astype(jnp.float32)).reshape(bsz, t, nh, dv)
    return y.reshape(bsz, t, nh * dv).astype(gate.dtype)


def mixer_output(t, states, n_chunks, hg_norm_g, gla_norm_g, w_br_hg, w_br_gla, w_o):
    o_hg = bidir(t["hg_q"], t["hg_k"], t["hg_v"], t["hg_lg"], states["hg"], n_chunks)
    o_gla = bidir(t["gla_q"], t["gla_k"], t["gla_v"], t["gla_lg"], states["gla"], n_chunks)
    y_hg = gated_head_norm(o_hg, t["hg_gate"], hg_norm_g) @ w_br_hg
    y_gla = gated_head_norm(o_gla, t["gla_gate"], gla_norm_g) @ w_br_gla
    m_hg, m_gla = t["merge"]
    y = jax.nn.sigmoid(m_hg) * y_hg + jax.nn.sigmoid(m_gla) * y_gla
    return y @ w_o


def peer(h, wq, k1, k2, u, v):
    bsz, t, d = h.shape
    tok = h.reshape(bsz * t, d)
    q = (tok @ wq).reshape(-1, PEER_HEADS, 2, PEER_SUB_DIM)
    s1 = jnp.einsum("nhd,kd->nhk", q[:, :, 0], k1).astype(jnp.float32)
    s2 = jnp.einsum("nhd,kd->nhk", q[:, :, 1], k2).astype(jnp.float32)
    v1, i1 = lax.top_k(s1, PEER_TOPK)
    v2, i2 = lax.top_k(s2, PEER_TOPK)
    cand = (v1[..., :, None] + v2[..., None, :]).reshape(-1, PEER_HEADS, PEER_TOPK * PEER_TOPK)
    top_s, top_c = lax.top_k(cand, PEER_TOPK)
    expert = (jnp.take_along_axis(i1, top_c // PEER_TOPK, axis=-1) * PEER_N_KEYS
              + jnp.take_along_axis(i2, top_c % PEER_TOPK, axis=-1))
    gate = jax.nn.softmax(top_s, axis=-1)
    n_sel = PEER_HEADS * PEER_TOPK

    def expert_block(args):
        xb, eb, gb = args
        act = jax.nn.gelu(jnp.einsum("pd,ped->pe", xb, jnp.take(u, eb, axis=0)), approximate=False)
        return jnp.einsum("pe,ped->pd", (gb * act).astype(v.dtype), jnp.take(v, eb, axis=0))

    out = lax.map(expert_block, (tok.reshape(-1, PEER_BLOCK, d),
                                 expert.reshape(-1, PEER_BLOCK, n_sel),
                                 gate.reshape(-1, PEER_BLOCK, n_sel)))
    return out.reshape(bsz, t, d).astype(h.dtype)


def setup_inputs(seed: int = 0) -> dict:
    key = jax.random.key(seed)
    ks = jax.random.split(key, 24)

    def nrm(k, shape, scale):
        return jax.random.normal(k, shape, jnp.float32) * scale

    def gain(k, shape):
        return 1.0 + 0.1 * jax.random.normal(k, shape, jnp.float32)

    return {
        "x": nrm(ks[0], (BATCH, SEQ, D_MODEL), 1.0),
        "c": nrm(ks[1], (BATCH, D_MODEL), 1.0),
        "ctx": nrm(ks[2], (BATCH, CTX_LEN, D_MODEL), 1.0),
        "c_ctx": nrm(ks[3], (D_MODEL,), 1.0),
        "ada_w": nrm(ks[4], (DEPTH, D_MODEL, N_MOD * D_MODEL), 0.5 * D_MODEL ** -0.5),
        "ada_b": nrm(ks[5], (DEPTH, N_MOD * D_MODEL), 0.02),
        "norm_mix_g": gain(ks[6], (DEPTH, D_MODEL)),
        "w_in": nrm(ks[7], (DEPTH, D_MODEL, D_IN), D_MODEL ** -0.5),
        "hgrn_lb_logits": nrm(ks[8], (DEPTH + 1, 2, HG_KEY), 0.5),
        "hgrn_norm_g": gain(ks[9], (DEPTH, HG_HEAD_V)),
        "gla_gk_w": nrm(ks[10], (DEPTH, 2, GLA_GATE_RANK, GLA_KEY), GLA_GATE_RANK ** -0.5),
        "gla_gk_b": nrm(ks[11], (DEPTH, 2, GLA_KEY), 0.1),
        "gla_norm_g": gain(ks[12], (DEPTH, GLA_HEAD_V)),
        "w_branch_hgrn": nrm(ks[13], (DEPTH, HG_VAL, D_MODEL), HG_VAL ** -0.5),
        "w_branch_gla": nrm(ks[14], (DEPTH, GLA_VAL, D_MODEL), GLA_VAL ** -0.5),
        "w_out": nrm(ks[15], (DEPTH, D_MODEL, D_MODEL), D_MODEL ** -0.5),
        "norm_ffn_g": gain(ks[16], (DEPTH, D_MODEL)),
        "peer_wq": nrm(ks[17], (DEPTH, D_MODEL, PEER_HEADS * PEER_QUERY_DIM), D_MODEL ** -0.5),
        "peer_k1": nrm(ks[18], (DEPTH, PEER_N_KEYS, PEER_SUB_DIM), PEER_SUB_DIM ** -0.5),
        "peer_k2": nrm(ks[19], (DEPTH, PEER_N_KEYS, PEER_SUB_DIM), PEER_SUB_DIM ** -0.5),
        "peer_u": nrm(ks[20], (DEPTH, PEER_EXPERTS, D_MODEL), D_MODEL ** -0.5),
        "peer_v": nrm(ks[21], (DEPTH, PEER_EXPERTS, D_MODEL), (PEER_HEADS * PEER_TOPK) ** -0.5),
        "final_g": gain(ks[22], (D_MODEL,)),
    }


def reference(x, c, ctx, c_ctx, ada_w, ada_b, norm_mix_g, w_in, hgrn_lb_logits, hgrn_norm_g,
              gla_gk_w, gla_gk_b, gla_norm_g, w_branch_hgrn, w_branch_gla, w_out, norm_ffn_g,
              peer_wq, peer_k1, peer_k2, peer_u, peer_v, final_g):
    bsz, seq, _ = x.shape
    rows = seq // GRID_W
    ctx_chunks = ctx.shape[1] // CTX_CHUNK
    lower = hgrn_lower_bounds(hgrn_lb_logits)
    silu_c = jax.nn.silu(c)
    silu_cc = jax.nn.silu(c_ctx)[None]
    xc = ctx
    for l in range(DEPTH):
        mod = jnp.split((silu_c @ ada_w[l] + ada_b[l])[:, None, :], N_MOD, axis=-1)
        mod_c = jnp.split((silu_cc @ ada_w[l] + ada_b[l])[:, None, :], N_MOD, axis=-1)
        mix_w = (hgrn_norm_g[l], gla_norm_g[l], w_branch_hgrn[l], w_branch_gla[l], w_out[l])
        peer_w = (peer_wq[l], peer_k1[l], peer_k2[l], peer_u[l], peer_v[l])
        t_ctx = mixer_inputs(modulate(rmsnorm(xc, norm_mix_g[l]), mod_c[0], mod_c[1]),
                             w_in[l], lower[l], gla_gk_w[l], gla_gk_b[l])
        t_lat = mixer_inputs(modulate(rmsnorm(x, norm_mix_g[l]), mod[0], mod[1]),
                             w_in[l], lower[l], gla_gk_w[l], gla_gk_b[l])
        x = x + mod[2] * mixer_output(t_lat, context_states(t_ctx), rows, *mix_w)
        x = x + mod[5] * peer(modulate(rmsnorm(x, norm_ffn_g[l]), mod[3], mod[4]), *peer_w)
        if l < DEPTH - 1:
            xc = xc + mod_c[2] * mixer_output(t_ctx, zero_states(bsz), ctx_chunks, *mix_w)
            xc = xc + mod_c[5] * peer(modulate(rmsnorm(xc, norm_ffn_g[l]), mod_c[3], mod_c[4]), *peer_w)
    return rmsnorm(x, final_g)
```

```python
import contextlib
import numpy as np
import concourse.bass as bass
import concourse.mybir as mybir
from concourse.bass_utils import run_bass_kernel_spmd

F32 = mybir.dt.float32
BF16 = mybir.dt.bfloat16
I32 = mybir.dt.int32
U32 = mybir.dt.uint32
AF = mybir.ActivationFunctionType
ALU = mybir.AluOpType
AX = mybir.AxisListType

N_CORES = 8
D = 1024
SEQ = 2048
CTX = 256
NLT = SEQ // 128
NCT = CTX // 128
NT = NLT + NCT
TOK = SEQ + CTX
D_IN = 10272
EPS = 1e-6
QSCALE = 128 ** -0.5
NEG = -1e30


class _Op:
    __slots__ = ("eng", "fn", "deps", "signal", "tok_sem", "tok_val", "is_dma", "dma_key")

    def __init__(self, eng, fn, is_dma=False, dma_key=None):
        self.eng = eng
        self.fn = fn
        self.deps = []
        self.signal = False
        self.tok_sem = None
        self.tok_val = 0
        self.is_dma = is_dma
        self.dma_key = dma_key


class Sched:
    ENGS = ("pe", "act", "dve", "pool", "sp")

    def __init__(self, nc):
        self.nc = nc
        self.ops = {e: [] for e in self.ENGS}
        self.last_writer = {}
        self.readers = {}
        self.last_eng = {}
        self.last_key = {}
        self.bar_deps = []
        self.bar_need = set()

    def barrier(self):
        self.bar_deps = list(self.last_eng.values()) + list(self.last_key.values())
        self.bar_need = set(self.ENGS)

    def _add(self, op, reads, writes):
        excl = [k for k in reads if isinstance(k, str) and k.startswith("ps")]
        if excl:
            reads = [k for k in reads if k not in excl]
            writes = list(writes) + [k for k in excl if k not in writes]
        deps = []
        if op.eng in self.bar_need:
            deps.extend(self.bar_deps)
            self.bar_need.discard(op.eng)
        for r in reads:
            w = self.last_writer.get(r)
            if w is not None:
                deps.append(w)
        for wkey in writes:
            w = self.last_writer.get(wkey)
            if w is not None:
                deps.append(w)
            deps.extend(self.readers.get(wkey, ()))
        seen = set()
        for d in deps:
            if d is op or id(d) in seen:
                continue
            seen.add(id(d))
            if (not d.is_dma) and (not op.is_dma) and d.eng == op.eng and op.eng == "pe":
                continue
            op.deps.append(d)
            d.signal = True
        for r in reads:
            self.readers.setdefault(r, []).append(op)
        for wkey in writes:
            self.last_writer[wkey] = op
            self.readers[wkey] = []
        self.ops[op.eng].append(op)
        if op.is_dma:
            self.last_key[op.dma_key] = op
        else:
            self.last_eng[op.eng] = op
        return op

    def op(self, eng, fn, reads=(), writes=()):
        return self._add(_Op(eng, fn), list(reads), list(writes))

    def dma(self, eng, fn, key, reads=(), writes=()):
        o = _Op(eng, fn, is_dma=True, dma_key=key)
        o.signal = True
        return self._add(o, list(reads), list(writes))

    def emit(self, final_wait_keys=()):
        nc = self.nc
        dma_keys = []
        for e in self.ENGS:
            for o in self.ops[e]:
                if o.is_dma and o.dma_key not in dma_keys:
                    dma_keys.append(o.dma_key)
        with contextlib.ExitStack() as st:
            eng_sem = {e: st.enter_context(nc.semaphore("S_" + e)) for e in self.ENGS}
            key_sem = {k: st.enter_context(nc.semaphore("D_%d" % i)) for i, k in enumerate(dma_keys)}
            key_cnt = {k: 0 for k in dma_keys}
            key_eng = {}
            for e in self.ENGS:
                cnt = 0
                for o in self.ops[e]:
                    if o.is_dma:
                        assert key_eng.setdefault(o.dma_key, e) == e, "dma key used from two queues"
                        key_cnt[o.dma_key] += 16
                        o.tok_sem = key_sem[o.dma_key]
                        o.tok_val = key_cnt[o.dma_key]
                    elif o.signal:
                        cnt += 1
                        o.tok_sem = eng_sem[e]
                        o.tok_val = cnt
            blk = st.enter_context(nc.Block())
            engobj = {"pe": "tensor", "act": "scalar", "dve": "vector", "pool": "gpsimd", "sp": "sync"}

            self.stats = {}

            def make(e):
                def body(eng):
                    waited = {}
                    nw = 0
                    for o in self.ops[e]:
                        for d in o.deps:
                            s = d.tok_sem
                            if waited.get(id(s), 0) >= d.tok_val:
                                continue
                            waited[id(s)] = d.tok_val
                            eng.wait_ge(s, d.tok_val)
                            nw += 1
                        ins = o.fn(eng)
                        if o.is_dma:
                            ins.then_inc(o.tok_sem, 16)
                        elif o.signal:
                            ins.then_inc(o.tok_sem, 1)
                    self.stats[e] = (len(self.ops[e]), nw)
                    if e == "sp":
                        for k in final_wait_keys:
                            if k in key_sem:
                                eng.wait_ge(key_sem[k], key_cnt[k])
                return body

            for e in self.ENGS:
                getattr(blk, engobj[e])(make(e))


def build_program(NS, dbg=(), stop=None):
    nc = bass.Bass("TRN2", target_bir_lowering=False)

    def din(name, shape, dt=F32):
        return nc.dram_tensor(name, list(shape), dt, kind="ExternalInput").ap()

    x_d = din("x", [NS, SEQ, D])
    ctx_d = din("ctx", [NS, CTX, D])
    c_d = din("c", [NS, D])
    cctx_d = din("c_ctx", [1, D])
    adaw_d = din("ada_w", [D, 6 * D])
    adab_d = din("ada_b", [1, 6 * D])
    gmix_d = din("norm_mix_g", [1, D])
    win_d = din("w_in", [D, D_IN])
    lbl_d = din("lb_logits", [2, 2 * D])
    hgg_d = din("hgrn_norm_g", [1, 128])
    gkw_d = din("gla_gk_w", [2, 16, 512])
    gkb_d = din("gla_gk_b", [2, 512])
    glg_d = din("gla_norm_g", [1, 256])
    wbh_d = din("w_branch_hgrn", [D, D])
    wbg_d = din("w_branch_gla", [D, D])
    wo_d = din("w_out", [D, D])
    gffn_d = din("norm_ffn_g", [1, D])
    wq_d = din("peer_wq", [D, 2048])
    k1_d = din("peer_k1", [128, 128])
    k2_d = din("peer_k2", [128, 128])
    pu_d = din("peer_u", [16384, D])
    pv_d = din("peer_v", [16384, D])
    fg_d = din("final_g", [1, D])
    out_d = nc.dram_tensor("out", [NS, SEQ, D], F32, kind="ExternalOutput").ap()
    modscr = nc.dram_tensor("modscr", [NS + 1, 6 * D], F32, kind="Internal").ap()
    wps_d = nc.dram_tensor("wps", [128, 8, 2048], BF16, kind="Internal").ap()
    dbg_out = {}
    for name, shape, dt in dbg:
        dbg_out[name] = nc.dram_tensor("dbg_" + name, list(shape), dt, kind="ExternalOutput").ap()

    ARENA = 212000
    arena = nc.alloc_sbuf_tensor("arena", [128, ARENA // 4], F32)
    base = nc.lookup_mloc(arena).addr
    cur = {"p": base, "limit": base + ARENA}

    def _sz(shape, dt):
        n = int(np.prod(shape[1:])) * (2 if dt == BF16 else 4)
        return (n + 63) // 64 * 64

    def alloc(name, shape, dt, at=None):
        n = _sz(shape, dt)
        if at is None:
            off = cur["p"]
            cur["p"] += n
            assert cur["p"] <= cur["limit"], ("SBUF overflow", name, cur["p"] - base)
        else:
            off = at[0]
            at[0] += n
            assert at[0] <= cur["limit"], ("SBUF phase overflow", name, at[0] - base)
        return nc.alloc_sbuf_tensor_at(name, list(shape), dt, offset=off)

    identF = alloc("identF", [128, 128], F32)
    identB = alloc("identB", [128, 128], BF16)
    TRI = [alloc("TRIf", [128, 128], F32), alloc("TRIb", [128, 128], F32)]
    M1 = [alloc("M1f", [128, 128], F32), alloc("M1b", [128, 128], F32)]
    UU = [alloc("Uf", [128, 128], F32), alloc("Ub", [128, 128], F32)]
    hgg_bc = alloc("hgg_bc", [128, 128], F32)
    glg_bc = alloc("glg_bc", [128, 256], F32)
    modsm = alloc("modsm", [128, 6, 8], F32)
    small = alloc("small", [128, 16], F32)
    yThg = alloc("yThg", [128, 8, SEQ], BF16)
    wgr = alloc("wgr", [128, 8, 64], BF16)
    gkw = alloc("gkw", [64, 512], F32)
    grT = alloc("grT", [64, TOK], F32)
    iota16 = alloc("iota16", [128, 16], F32)
    PH = cur["p"]

    a = [PH]
    hT = alloc("hT", [128, 8, TOK], BF16, a)
    yTgl = alloc("yTgl", [128, 8, SEQ], BF16, a)
    MX = a[0]
    a = [MX]
    qT = alloc("qT", [128, SEQ], F32, a)
    vst = alloc("vst", [128, NT, 256], BF16, a)
    ofw = alloc("ofw", [128, NLT, 256], F32, a)
    whd = [alloc("whd0", [128, 8, 768], BF16, a), alloc("whd1", [128, 8, 768], BF16, a)]
    oml = alloc("oml", [128, 2, 128], F32, a)
    omt = alloc("omt", [128, 2, 2, 128], F32, a)
    t_e = alloc("t_e", [128, 256], F32, a)
    t_r = alloc("t_r", [128, 256], F32, a)
    t_k = alloc("t_k", [128, 128], F32, a)
    t_f = alloc("t_f", [128, 128], F32, a)
    t_lg = alloc("t_lg", [128, 128], F32, a)
    t_P = alloc("t_P", [128, 4, 128], F32, a)
    t_qd = alloc("t_qd", [128, 128], BF16, a)
    t_kdT = alloc("t_kdT", [128, 128], BF16, a)
    t_qdec = alloc("t_qdec", [128, 128], BF16, a)
    t_kdec = alloc("t_kdec", [128, 128], BF16, a)
    t_scm = [alloc("t_scm0", [128, 128], BF16, a), alloc("t_scm1", [128, 128], BF16, a)]
    St = alloc("St", [128, 256], F32, a)
    Sbf = alloc("Sbf", [128, 256], BF16, a)
    t_o = alloc("t_o", [128, 256], F32, a)
    t_gate = alloc("t_gate", [128, 256], F32, a)
    t_junk = alloc("t_junk", [128, 256], F32, a)
    t_y = alloc("t_y", [128, 256], BF16, a)
    a = [MX]
    xin = [alloc("xin0", [128, D], F32, a), alloc("xin1", [128, D], F32, a)]
    xs = alloc("xs", [128, D], F32, a)
    a = [MX]
    wbh = alloc("wbh", [128, 8, D], BF16, a)
    wbg = alloc("wbg", [128, 8, D], BF16, a)
    wm = alloc("wm", [128, 8, 2 * D], BF16, a)
    c_e = [alloc("c_e0", [128, 512], F32, a), alloc("c_e1", [128, 512], F32, a)]
    c_t = [alloc("c_t0", [128, 512], F32, a), alloc("c_t1", [128, 512], F32, a)]
    c_ym = alloc("c_ym", [128, D], BF16, a)
    a = [PH]
    s_l = alloc("s_l", [128, 2, 2 * D], F32, a)
    adaw = [alloc("adaw0", [128, 8, 512], F32, a), alloc("adaw1", [128, 8, 512], F32, a)]
    cT = alloc("cT", [128, 8, NS + 1], F32, a)
    scT = alloc("scT", [128, 8, NS + 1], F32, a)
    s_tmp = alloc("s_tmp", [128, 8, NS + 1], F32, a)
    modrows = alloc("modrows", [NS + 1, 6 * D], F32, a)
    adab_sb = alloc("adab_sb", [NS + 1, 6 * D], F32, a)
    wqb = [alloc("wqb0", [128, 8, 128], F32, a), alloc("wqb1", [128, 8, 128], F32, a)]
    kraw = alloc("kraw", [128, 2, 128], F32, a)
    kT = alloc("kT", [128, 2, 128], F32, a)
    wqT = [alloc("wqT0", [128, 128], F32, a), alloc("wqT1", [128, 128], F32, a)]
    wpb = [alloc("wpb0", [128, 8, 128], BF16, a), alloc("wpb1", [128, 8, 128], BF16, a)]
    a = [PH]
    wps = alloc("wps_sb", [128, 8, 2048], BF16, a)
    wo = alloc("wo", [128, 8, D], BF16, a)
    G1 = alloc("G1", [128, D], F32, a)
    A2 = alloc("A2", [128, D], F32, a)
    B2 = alloc("B2", [128, D], F32, a)
    G2 = alloc("G2", [128, D], F32, a)
    FG = alloc("FG", [128, D], F32, a)
    NGB = 8
    gb = [alloc("gb%d" % i, [128, D], F32, a) for i in range(NGB)]
    sc = alloc("sc", [128, 2048], F32, a)
    scw = alloc("scw", [128, 256], F32, a)
    dxin = alloc("dxin", [128, D], F32, a)
    x1 = alloc("x1", [128, D], F32, a)
    h2 = alloc("h2", [128, D], F32, a)
    h2T = alloc("h2T", [128, 8, 128], BF16, a)
    acc = alloc("acc", [128, D], F32, a)
    djunk = alloc("djunk", [128, D], F32, a)
    v16 = alloc("v16", [128, 16, 16], F32, a)
    i16 = alloc("i16", [128, 16, 16], U32, a)
    i16f = alloc("i16f", [128, 16, 16], F32, a)
    cand = alloc("cand", [128, 8, 256], F32, a)
    ts = alloc("ts", [128, 8, 16], F32, a)
    pos = alloc("pos", [128, 8, 16], U32, a)
    pa = alloc("pa", [128, 2, 8, 16], I32, a)
    paf = alloc("paf", [128, 2, 8, 16], F32, a)
    oh = alloc("oh", [128, 8, 256], F32, a)
    isel = alloc("isel", [128, 2, 128], F32, a)
    eidx = alloc("eidx", [128, 128], I32, a)
    pgate = alloc("pgate", [128, 128], F32, a)
    pact = alloc("pact", [128, 128], F32, a)
    pex = alloc("pex", [128, 128], F32, a)
    psm = alloc("psm", [128, 16], F32, a)

    psA = nc.alloc_psum_tensor("psA", [128, 1024], F32)
    psB = nc.alloc_psum_tensor("psB", [128, 1024], F32)
    psC = nc.alloc_psum_tensor("psC", [128, 1024], F32)
    psD = nc.alloc_psum_tensor("psD", [128, 512], F32)
    psE = nc.alloc_psum_tensor("psE", [128, 1024], BF16)

    S = Sched(nc)

    def MM(out, lhsT, rhs, start=True, stop=True, r=(), w=()):
        S.op("pe", lambda e: e.matmul(out, lhsT=lhsT, rhs=rhs, start=start, stop=stop), r, w)

    def TR(out, in_, ident, r=(), w=()):
        S.op("pe", lambda e: e.transpose(out=out, in_=in_, identity=ident), r, w)

    def ACT(out, in_, func, r=(), w=(), bias=0.0, scale=1.0, accum=None):
        if accum is None:
            S.op("act", lambda e: e.activation(out=out, in_=in_, func=func, bias=bias, scale=scale), r, w)
        else:
            S.op("act", lambda e: e.activation(out=out, in_=in_, func=func, bias=bias, scale=scale, accum_out=accum), r, w)

    def ACP(out, in_, r=(), w=()):
        S.op("act", lambda e: e.copy(out=out, in_=in_), r, w)

    def AMUL(out, in_, mul, r=(), w=()):
        S.op("act", lambda e: e.mul(out=out, in_=in_, mul=mul), r, w)

    def TT(eng, out, in0, in1, op, r=(), w=()):
        S.op(eng, lambda e: e.tensor_tensor(out=out, in0=in0, in1=in1, op=op), r, w)

    def TS(eng, out, in0, s1, s2, op0, op1=None, r=(), w=()):
        if op1 is None:
            S.op(eng, lambda e: e.tensor_scalar(out=out, in0=in0, scalar1=s1, scalar2=None, op0=op0), r, w)
        else:
            S.op(eng, lambda e: e.tensor_scalar(out=out, in0=in0, scalar1=s1, scalar2=s2, op0=op0, op1=op1), r, w)

    def STT(out, in0, scalar, in1, op0, op1, r=(), w=(), accum=None):
        if accum is None:
            S.op("dve", lambda e: e.scalar_tensor_tensor(out=out, in0=in0, scalar=scalar, in1=in1, op0=op0, op1=op1), r, w)
        else:
            S.op("dve", lambda e: e.scalar_tensor_tensor(out=out, in0=in0, scalar=scalar, in1=in1, op0=op0, op1=op1, accum_out=accum), r, w)

    def CP(eng, out, in_, r=(), w=()):
        S.op(eng, lambda e: e.tensor_copy(out=out, in_=in_), r, w)

    def RCP(out, in_, r=(), w=()):
        S.op("dve", lambda e: e.reciprocal(out=out, in_=in_), r, w)

    def MSET(eng, ap, val, w=()):
        S.op(eng, lambda e: e.memset(ap, val), (), w)

    def DMA(eng, out, in_, key, r=(), w=(), slow=False):
        if slow:
            S.dma(eng, lambda e: e.dma_start(out=out, in_=in_, allow_slow_non_contiguous=True), key, r, w)
        else:
            S.dma(eng, lambda e: e.dma_start(out=out, in_=in_), key, r, w)

    def ASEL(out, pattern, cmp, base_, cm, r=(), w=()):
        S.op("pool", lambda e: e.affine_select(out=out, in_=out, pattern=pattern, compare_op=cmp, fill=0.0,
                                               base=base_, channel_multiplier=cm), r, w)

    def rstd_from_ss(ss, out, n, keyr, keyw):
        ACT(out, ss, AF.Ln, r=keyr, w=[keyw], bias=EPS, scale=1.0 / n)
        ACT(out, out, AF.Exp, r=[keyw], w=[keyw], scale=-0.5)

    def tri_const(t, pattern, cmp, base_, cm, key):
        MSET("pool", t[:], 1.0, w=[key])
        ASEL(t[:], pattern, cmp, base_, cm, r=[key], w=[key])

    tri_const(identF, [[-1, 128]], ALU.is_equal, 0, 1, "identF")
    CP("dve", identB[:], identF[:], r=["identF"], w=["identB"])
    tri_const(TRI[0], [[1, 128]], ALU.is_ge, 0, -1, "TRI0")
    tri_const(TRI[1], [[-1, 128]], ALU.is_ge, 0, 1, "TRI1")
    tri_const(UU[0], [[-1, 128]], ALU.is_gt, 0, 1, "UU0")
    tri_const(UU[1], [[1, 128]], ALU.is_gt, 0, -1, "UU1")
    tri_const(M1[0], [[0, 128]], ALU.is_ge, 64, -1, "M10")
    tri_const(M1[1], [[0, 128]], ALU.is_ge, -63, 1, "M11")
    TT("dve", M1[0][:], TRI[0][:], M1[0][:], ALU.subtract, r=["TRI0", "M10"], w=["M10"])
    TT("dve", M1[1][:], TRI[1][:], M1[1][:], ALU.subtract, r=["TRI1", "M11"], w=["M11"])
    S.op("pool", lambda e: e.iota(iota16[:], pattern=[[1, 16]], base=0, channel_multiplier=0,
                                  allow_small_or_imprecise_dtypes=True), (), ["iota16"])
    if stop == "s1":
        S.emit(final_wait_keys=[])
        return nc
    DMA("sp", hgg_bc[:], hgg_d.partition_broadcast(128), "su", w=["hgg_bc"])
    DMA("sp", glg_bc[:], glg_d.partition_broadcast(128), "su", w=["glg_bc"])
    MSET("pool", wgr[:], 0.0, w=["wgr"])
    winv = win_d.rearrange("(c p) n -> p c n", p=128)
    DMA("pool", wgr[:, :, 0:16], winv[:, :, 8192:8208], "pw", r=["wgr"], w=["wgr"])
    DMA("pool", wgr[:, :, 32:48], winv[:, :, 8208:8224], "pw", r=["wgr"], w=["wgr"])
    MSET("dve", grT[:], 1.0, w=["grT"])
    MSET("dve", gkw[:], 0.0, w=["gkw"])
    DMA("sp", gkw[0:16, :], gkw_d[0], "su", r=["gkw"], w=["gkw"])
    DMA("sp", gkw[16:17, :], gkb_d[0:1, :], "su", w=["gkw"])
    DMA("sp", gkw[32:48, :], gkw_d[1], "su", w=["gkw"])
    DMA("sp", gkw[48:49, :], gkb_d[1:2, :], "su", w=["gkw"])
    if stop == "s2":
        S.emit(final_wait_keys=[])
        return nc
    for b_ in range(NS):
        DMA("sp", cT[:, :, b_:b_ + 1], c_d[b_:b_ + 1, :].rearrange("b (c p) -> p c b", p=128), "su", w=["cT"], slow=True)
    DMA("sp", cT[:, :, NS:NS + 1], cctx_d.rearrange("b (c p) -> p c b", p=128), "su", w=["cT"], slow=True)
    DMA("sp", adab_sb[:], adab_d.partition_broadcast(NS + 1), "su", w=["adab"])
    ACT(s_tmp[:], cT[:], AF.Exp, r=["cT"], w=["s_tmp"], scale=-1.0)
    TS("dve", s_tmp[:], s_tmp[:], 1.0, None, ALU.add, r=["s_tmp"], w=["s_tmp"])
    RCP(s_tmp[:], s_tmp[:], r=["s_tmp"], w=["s_tmp"])
    TT("dve", scT[:], cT[:], s_tmp[:], ALU.mult, r=["cT", "s_tmp"], w=["scT"])
    if stop == "s3":
        S.emit(final_wait_keys=[])
        return nc
    adawv = adaw_d.rearrange("(c p) n -> p c n", p=128)
    for n in range(12):
        ab = adaw[n % 2]
        DMA("sp", ab[:], adawv[:, :, n * 512:(n + 1) * 512], "aw", w=["adaw%d" % (n % 2)])
        for c in range(8):
            MM(psA[0:NS + 1, 0:512], scT[:, c, :], ab[:, c, :], start=(c == 0), stop=(c == 7),
               r=["scT", "adaw%d" % (n % 2)], w=["psA0"])
        TT("dve", modrows[:, n * 512:(n + 1) * 512], psA[0:NS + 1, 0:512], adab_sb[:, n * 512:(n + 1) * 512],
           ALU.add, r=["psA0", "adab"], w=["modrows"])
    if stop == "s4":
        S.emit(final_wait_keys=[])
        return nc
    DMA("sp", modscr, modrows[:], "ms", r=["modrows"], w=["modscr"])
    DMA("sp", modsm[:, 0, :], gmix_d.rearrange("o (c p) -> p (o c)", p=128), "ms2", w=["gmixT"], slow=True)
    DMA("sp", modsm[:, 2, :], modscr[NS:NS + 1, 0:D].rearrange("o (c p) -> p (o c)", p=128), "ms2",
        r=["modscr"], w=["B1Tc"], slow=True)
    DMA("sp", modsm[:, 5, :], modscr[NS:NS + 1, D:2 * D].rearrange("o (c p) -> p (o c)", p=128), "ms2",
        r=["modscr"], w=["mtmp"], slow=True)
    STT(modsm[:, 1, :], modsm[:, 5, :], 1.0, modsm[:, 0, :], ALU.add, ALU.mult, r=["mtmp", "gmixT"], w=["A1Tc"])
    if stop == "s5":
        S.emit(final_wait_keys=[])
        return nc
    DMA("sp", kraw[:, 0, :], k1_d, "su", w=["kraw"])
    DMA("sp", kraw[:, 1, :], k2_d, "su", w=["kraw"])
    for hf in range(2):
        TR(psB[:, hf * 128:(hf + 1) * 128], kraw[:, hf, :], identF[:], r=["kraw", "identF"], w=["psB0"])
    CP("dve", kT[:].rearrange("p a b -> p (a b)"), psB[:, 0:256], r=["psB0"], w=["kT"])
    wqv = wq_d.rearrange("(c p) n -> p c n", p=128)
    if stop == "s6":
        S.emit(final_wait_keys=[])
        return nc
    import os
    WST = int(os.environ.get("WST", "9"))
    for g in range(16 if stop != "s7" else 1):
        wb_ = wqb[g % 2]
        DMA("sp", wb_[:], wqv[:, :, g * 128:(g + 1) * 128], "wq", w=["wqb%d" % (g % 2)])
        for c in range(8):
            i = (g * 8 + c) % 2
            TR(psC[:, i * 512:i * 512 + 128], wb_[:, c, :], identF[:], r=["wqb%d" % (g % 2), "identF"], w=["psC%d" % i])
            if i == 0:
                CP("dve", wqT[i][:], psC[:, i * 512:i * 512 + 128], r=["psC%d" % i], w=["wqT%d" % i])
            else:
                ACP(wqT[i][:], psC[:, i * 512:i * 512 + 128], r=["psC%d" % i], w=["wqT%d" % i])
            MM(psA[:, i * 512:i * 512 + 128], wqT[i][:], kT[:, g % 2, :], r=["wqT%d" % i, "kT"], w=["psA%d" % i])
            CP("dve", wpb[g % 2][:, c, :], psA[:, i * 512:i * 512 + 128], r=["psA%d" % i], w=["wpb%d" % (g % 2)])
        DMA("sp", wps_d[:, :, g * 128:(g + 1) * 128], wpb[g % 2][:], "wpo", r=["wpb%d" % (g % 2)], w=["wps_d"])
    S.barrier()
    if stop == "s7":
        S.emit(final_wait_keys=[])
        return nc
    if stop == "setup":
        if "modrows" in dbg_out:
            DMA("sp", dbg_out["modrows"], modscr, "dbg", r=["modscr"])
            DMA("sp", dbg_out["wps"], wps_d, "dbg", r=["wps_d"])
        S.emit(final_wait_keys=["dbg"] if dbg_out else [])
        return nc

    def phase_A(s):
        for t in range(NT):
            xb = xin[t % 2]
            xk = "xin%d" % (t % 2)
            src = ctx_d[s, t * 128:(t + 1) * 128, :] if t < NCT else x_d[s, (t - NCT) * 128:(t - NCT + 1) * 128, :]
            DMA("sp", xb[:], src, "xin", w=[xk])
            ACT(xs[:], xb[:], AF.Square, r=[xk], w=["xs", "ssA"], accum=small[:, 0:1])
            rstd_from_ss(small[:, 0:1], small[:, 1:2], D, ["ssA"], "rsA")
            AMUL(xs[:], xb[:], small[:, 1:2], r=[xk, "rsA"], w=["xs"])
            for c in range(8):
                TR(psA[:, c * 128:(c + 1) * 128], xs[:, c * 128:(c + 1) * 128], identF[:], r=["xs", "identF"], w=["psA%d" % (c // 4)])
            ai, bi = (1, 2) if t < NCT else (3, 4)
            an, bn = ("A1Tc", "B1Tc") if t < NCT else ("A1Ts", "B1Ts")
            for c in range(8):
                TS("dve", hT[:, c, t * 128:(t + 1) * 128], psA[:, c * 128:(c + 1) * 128], modsm[:, ai, c:c + 1], modsm[:, bi, c:c + 1],
                   ALU.mult, ALU.add, r=["psA%d" % (c // 4), an, bn], w=[("hT", t)])

    def load_seq_mod(s):
        DMA("sp", modsm[:, 4, :], modscr[s:s + 1, 0:D].rearrange("o (c p) -> p (o c)", p=128), "ms2",
            r=["modscr"], w=["B1Ts"], slow=True)
        DMA("sp", modsm[:, 5, :], modscr[s:s + 1, D:2 * D].rearrange("o (c p) -> p (o c)", p=128), "ms2",
            r=["modscr"], w=["mtmp"], slow=True)
        STT(modsm[:, 3, :], modsm[:, 5, :], 1.0, modsm[:, 0, :], ALU.add, ALU.mult, r=["mtmp", "gmixT"], w=["A1Ts"])

    def gr_prepass():
        for g0 in range(0, TOK, 512):
            n = min(512, TOK - g0)
            for c in range(8):
                MM(psC[0:64, 512:512 + n], wgr[:, c, :], hT[:, c, g0:g0 + n], start=(c == 0), stop=(c == 7),
                   r=["wgr"] + [("hT", t) for t in range(g0 // 128, (g0 + n) // 128)], w=["psC1"])
            CP("dve", grT[0:16, g0:g0 + n], psC[0:16, 512:512 + n], r=["psC1"], w=["grT"])
            ACP(grT[32:48, g0:g0 + n], psC[32:48, 512:512 + n], r=["psC1"], w=["grT"])

    def load_head_w(kind, hh, buf):
        wt = whd[buf]
        k = "whd%d" % buf
        if kind == "hg":
            cols = [(hh * 128, 128), (D + hh * 128, 128), (3 * D + hh * 128, 128), (2 * D + hh * 128, 128), (4 * D + hh * 128, 128)]
        else:
            cols = [(5120 + hh * 128, 128), (6144 + hh * 256, 256), (5632 + hh * 128, 128), (7168 + hh * 256, 256)]
        o = 0
        for c0, n in cols:
            DMA("pool", wt[:, :, o:o + n], winv[:, :, c0:c0 + n], "pw", w=[k])
            o += n

    def scan_tile(kind, hh, buf, d, t, V, escale):
        import os
        wt = whd[buf]
        wk = "whd%d" % buf
        is_ctx = t < NCT
        lt = t - NCT
        tok = slice(t * 128, (t + 1) * 128)
        if kind == "hg":
            c0, ncol = (128, 256) if d == 0 else (384, 256)
        else:
            c0, ncol = (128, 384) if d == 0 else (384, 384)
        for c in range(8):
            MM(psB[:, 0:ncol], hT[:, c, tok], wt[:, c, c0:c0 + ncol], start=(c == 0), stop=(c == 7),
               r=[("hT", t), wk], w=["psB0"])
        if kind == "hg":
            z = psB[:, 0:128]
            ACT(t_e[:, 0:128], z, AF.Exp, r=["psB0"], w=["t_e"], scale=-1.0)
            TS("dve", t_e[:, 0:128], t_e[:, 0:128], 1.0, None, ALU.add, r=["t_e"], w=["t_e"])
            RCP(t_r[:, 0:128], t_e[:, 0:128], r=["t_e"], w=["t_r"])
            TS("dve", t_r[:, 0:128], t_r[:, 0:128], -1.0, 1.0, ALU.mult, ALU.add, r=["t_r"], w=["t_r"])
            TT("dve", t_k[:], t_r[:, 0:128], oml[:, d, :], ALU.mult, r=["t_r", "oml"], w=["t_k"])
            TS("dve", t_f[:], t_k[:], -1.0, 1.0, ALU.mult, ALU.add, r=["t_k"], w=["t_f"])
            ACT(t_lg[:], t_f[:], AF.Ln, r=["t_f"], w=["t_lg"])
            if d == 0 and not (os.environ.get("NOV17") and t == 17):
                ACP(vst[:, t, 0:128], psB[:, 128:256], r=["psB0"], w=[("vst", t)])
            gsrc = psB[:, 128:256]
        else:
            if d == 0:
                ACP(vst[:, t, :], psB[:, 0:256], r=["psB0"], w=[("vst", t)])
                CP("dve", t_k[:], psB[:, 256:384], r=["psB0"], w=["t_k"])
            else:
                CP("dve", t_k[:], psB[:, 0:128], r=["psB0"], w=["t_k"])
            gsrc = psB[:, 128:384]
            pb = 32 * d
            MM(psC[:, 640:768], grT[pb:pb + 32, tok], gkw[pb:pb + 32, hh * 128:(hh + 1) * 128], r=["grT", "gkw"], w=["psC1"])
            ACT(t_e[:, 0:128], psC[:, 640:768], AF.Exp, r=["psC1"], w=["t_e"], scale=-1.0)
            ACT(t_lg[:], t_e[:, 0:128], AF.Ln, r=["t_e"], w=["t_lg"], bias=1.0)
        if d == 1 and not is_ctx:
            ACT(t_e[:, 0:V], gsrc, AF.Exp, r=["psB0"], w=["t_e"], scale=-1.0)
            TS("dve", t_e[:, 0:V], t_e[:, 0:V], 1.0, None, ALU.add, r=["t_e"], w=["t_e"])
            RCP(t_r[:, 0:V], t_e[:, 0:V], r=["t_e"], w=["t_r"])
            TT("dve", t_gate[:, 0:V], gsrc, t_r[:, 0:V], ALU.mult, r=["psB0", "t_r"], w=["t_gate"])
            gbc = hgg_bc if kind == "hg" else glg_bc
            TT("dve", t_gate[:, 0:V], t_gate[:, 0:V], gbc[:, 0:V], ALU.mult, r=["t_gate", "hgg_bc", "glg_bc"], w=["t_gate"])
        HST = int(os.environ.get("HST", "9"))
        if HST < 2:
            return
        HSK = os.environ.get("HSK", "").split(",")
        if not is_ctx and "a" not in HSK:
            MM(psC[:, 0:128], t_lg[:], M1[d][:], r=["t_lg", "M1%d" % d], w=["psC0"])
        if "b" not in HSK:
            MM(psC[:, 128:256], t_lg[:], TRI[d][:], r=["t_lg", "TRI%d" % d], w=["psC0"])
        if "c" not in HSK:
            MM(psC[:, 256:384], UU[d][:], t_lg[:], r=["t_lg", "UU%d" % d], w=["psC0"])
        if not is_ctx:
            if "d" not in HSK:
                TR(psC[:, 384:512], t_k[:], identF[:], r=["t_k", "identF"], w=["psC0"])
            if "e" not in HSK:
                ACT(t_P[:, 0, :], psC[:, 0:128], AF.Exp, r=["psC0"], w=["P1"], scale=escale)
                ACT(t_P[:, 1, :], psC[:, 0:128], AF.Exp, r=["psC0"], w=["P2"], scale=-escale)
        if "e" not in HSK:
            ACT(t_P[:, 2, :], psC[:, 128:256], AF.Exp, r=["psC0"], w=["P3"], scale=escale)
            ACT(t_P[:, 3, :], psC[:, 256:384], AF.Exp, r=["psC0"], w=["P4"], scale=escale)
        if HST < 3:
            return
        PEN = os.environ.get("PEN", "dve")
        if "f" not in HSK:
            TT(PEN, t_kdec[:], t_k[:], t_P[:, 3, :], ALU.mult, r=["t_k", "P4"], w=["t_kdec"])
        if not is_ctx:
            qs = qT[:, lt * 128:(lt + 1) * 128]
            if "g" not in HSK:
                TT("dve", t_qd[:], qs, t_P[:, 0, :], ALU.mult, r=["qT", "P1"], w=["t_qd"])
            if "h" not in HSK:
                TT("dve", t_kdT[:], psC[:, 384:512], t_P[:, 1, :], ALU.mult, r=["psC0", "P2"], w=["t_kdT"])
            if "i" not in HSK:
                TT(PEN, t_qdec[:], qs, t_P[:, 2, :], ALU.mult, r=["qT", "P3"], w=["t_qdec"])
            if "j" not in HSK:
                MM(psC[:, 512:640], t_kdT[:], t_qd[:], r=["t_kdT", "t_qd"], w=["psC1"])
            if "k" not in HSK:
                S.op("dve", lambda e, d=d: e.copy_predicated(out=t_scm[d][:], mask=TRI[d][:].bitcast(U32), data=psC[:, 512:640]),
                     ["psC1", "TRI%d" % d], ["t_scm%d" % d])
            if "l" not in HSK:
                MM(psD[:, 0:V], t_scm[d][:], vst[:, t, 0:V], start=True, stop=False, r=["t_scm%d" % d, ("vst", t)], w=["psD"])
                MM(psD[:, 0:V], t_qdec[:], Sbf[:, 0:V], start=False, stop=True, r=["t_qdec", "Sbf"], w=["psD"])
        if HST < 4:
            return
        MM(psD[:, 256:256 + V], t_kdec[:], vst[:, t, 0:V], r=["t_kdec", ("vst", t)], w=["psD"])
        dcol = 127 if d == 0 else 0
        STT(St[:, 0:V], St[:, 0:V], t_P[:, 2, dcol:dcol + 1], psD[:, 256:256 + V], ALU.mult, ALU.add,
            r=["St", "P3", "psD"], w=["St"])
        ACP(Sbf[:, 0:V], St[:, 0:V], r=["St"], w=["Sbf"])
        if is_ctx:
            return
        if d == 0:
            ACP(ofw[:, lt, 0:V], psD[:, 0:V], r=["psD"], w=[("ofw", lt)])
            return
        if HST < 5:
            return
        TT("dve", t_o[:, 0:V], psD[:, 0:V], ofw[:, lt, 0:V], ALU.add, r=["psD", ("ofw", lt)], w=["t_o"])
        ACT(t_junk[:, 0:V], t_o[:, 0:V], AF.Square, r=["t_o"], w=["t_junk", "ssH"], accum=small[:, 2:3])
        rstd_from_ss(small[:, 2:3], small[:, 3:4], V, ["ssH"], "rsH")
        STT(t_y[:, 0:V], t_o[:, 0:V], small[:, 3:4], t_gate[:, 0:V], ALU.mult, ALU.mult, r=["t_o", "rsH", "t_gate"], w=["t_y"])
        dst = yThg if kind == "hg" else yTgl
        dk = "yThg" if kind == "hg" else "yTgl"
        for j in range(V // 128):
            TR(psE[:, j * 128:(j + 1) * 128], t_y[:, j * 128:(j + 1) * 128], identB[:], r=["t_y", "identB"], w=["psE"])
        for j in range(V // 128):
            ch = hh * (V // 128) + j
            ACP(dst[:, ch, lt * 128:(lt + 1) * 128], psE[:, j * 128:(j + 1) * 128], r=["psE"], w=[(dk, lt)])

    def head(kind, hh, buf, s):
        V = 128 if kind == "hg" else 256
        escale = 1.0 if kind == "hg" else -1.0 / 16.0
        wt = whd[buf]
        wk = "whd%d" % buf
        if kind == "hg":
            for d in range(2):
                for l in range(2):
                    DMA("sp", omt[:, d, l, :], lbl_d[l:l + 1, d * D + hh * 128:d * D + (hh + 1) * 128].partition_broadcast(128),
                        "su", w=["omt"])
            TT("dve", omt[:, :, 0, :], omt[:, :, 1, :], omt[:, :, 0, :], ALU.subtract, r=["omt"], w=["omt"])
            ACT(omt[:, :, 0, :], omt[:, :, 0, :], AF.Exp, r=["omt"], w=["omt"])
            TS("dve", omt[:, :, 1, :], omt[:, :, 0, :], 1.0, None, ALU.add, r=["omt"], w=["omt"])
            RCP(omt[:, :, 1, :], omt[:, :, 1, :], r=["omt"], w=["omt"])
            TT("dve", oml[:], omt[:, :, 0, :], omt[:, :, 1, :], ALU.mult, r=["omt"], w=["oml"])
        for g in range(4):
            for c in range(8):
                MM(psB[:, 512:1024], wt[:, c, 0:128], hT[:, c, CTX + g * 512:CTX + (g + 1) * 512], start=(c == 0), stop=(c == 7),
                   r=[wk] + [("hT", NCT + g * 4 + i) for i in range(4)], w=["psB1"])
            AMUL(qT[:, g * 512:(g + 1) * 512], psB[:, 512:1024], QSCALE, r=["psB1"], w=["qT"])
        import os
        if int(os.environ.get("HST", "9")) < 1:
            return
        for d in range(2):
            MSET("dve", t_scm[d][:], 0.0, w=["t_scm%d" % d])
            MSET("dve", St[:], 0.0, w=["St"])
            MSET("dve", Sbf[:], 0.0, w=["Sbf"])
            order = list(range(NT)) if d == 0 else [1, 0] + list(range(NT - 1, NCT - 1, -1))
            order = order[:int(os.environ.get("HTL%d" % d, "99"))]
            for t in order:
                scan_tile(kind, hh, buf, d, t, V, escale)

    def phase_C1(s):
        DMA("pool", wbh[:], wbh_d.rearrange("(c p) n -> p c n", p=128), "pw", w=["wbh"])
        DMA("pool", wbg[:], wbg_d.rearrange("(c p) n -> p c n", p=128), "pw", w=["wbg"])
        DMA("pool", wm[:], winv[:, :, 8224:8224 + 2 * D], "pw", w=["wm"])
        for lt in range(NLT):
            t = lt + NCT
            tok = slice(lt * 128, (lt + 1) * 128)
            for hf in range(2):
                fs = slice(hf * 512, (hf + 1) * 512)
                for c in range(8):
                    MM(psA[:, 0:512], yThg[:, c, tok], wbh[:, c, fs], start=(c == 0), stop=(c == 7), r=[("yThg", lt), "wbh"], w=["psA0"])
                for c in range(8):
                    MM(psA[:, 512:1024], yTgl[:, c, tok], wbg[:, c, fs], start=(c == 0), stop=(c == 7), r=[("yTgl", lt), "wbg"], w=["psA1"])
                for c in range(8):
                    MM(psB[:, 0:512], hT[:, c, t * 128:(t + 1) * 128], wm[:, c, fs], start=(c == 0), stop=(c == 7), r=[("hT", t), "wm"], w=["psB0"])
                for c in range(8):
                    MM(psB[:, 512:1024], hT[:, c, t * 128:(t + 1) * 128], wm[:, c, D + hf * 512:D + (hf + 1) * 512],
                       start=(c == 0), stop=(c == 7), r=[("hT", t), "wm"], w=["psB1"])
                for i, (pm, py, pmk, pyk) in enumerate(((psB[:, 0:512], psA[:, 0:512], "psB0", "psA0"),
                                                        (psB[:, 512:1024], psA[:, 512:1024], "psB1", "psA1"))):
                    ACT(c_e[i][:], pm, AF.Exp, r=[pmk], w=["c_e%d" % i], scale=-1.0)
                    TS("dve", c_e[i][:], c_e[i][:], 1.0, None, ALU.add, r=["c_e%d" % i], w=["c_e%d" % i])
                    RCP(c_e[i][:], c_e[i][:], r=["c_e%d" % i], w=["c_e%d" % i])
                    TT("dve", c_t[i][:], py, c_e[i][:], ALU.mult, r=[pyk, "c_e%d" % i], w=["c_t%d" % i])
                TT("dve", c_ym[:, fs], c_t[0][:], c_t[1][:], ALU.add, r=["c_t0", "c_t1"], w=["c_ym"])
            for c in range(8):
                TR(psE[:, c * 128:(c + 1) * 128], c_ym[:, c * 128:(c + 1) * 128], identB[:], r=["c_ym", "identB"], w=["psE"])
            ACP(yThg[:, :, tok], psE[:].rearrange("p (c t) -> p c t", c=8), r=["psE"], w=[("yThg", lt)])

    def peer_topk():
        for g in range(16):
            sg = sc[:, g * 128:(g + 1) * 128]
            S.op("dve", lambda e, g=g, sg=sg: e.max(out=v16[:, g, 0:8], in_=sg), ["sc"], [("v16", g)])
            S.op("dve", lambda e, g=g, sg=sg: e.max_index(out=i16[:, g, 0:8], in_max=v16[:, g, 0:8], in_values=sg), ["sc", ("v16", g)], [("i16", g)])
            S.op("dve", lambda e, g=g, sg=sg: e.match_replace(out=scw[:, 0:128], in_to_replace=v16[:, g, 0:8], in_values=sg, imm_value=NEG),
                 ["sc", ("v16", g)], ["scw"])
            S.op("dve", lambda e, g=g: e.max(out=v16[:, g, 8:16], in_=scw[:, 0:128]), ["scw"], [("v16b", g)])
            S.op("dve", lambda e, g=g: e.max_index(out=i16[:, g, 8:16], in_max=v16[:, g, 8:16], in_values=scw[:, 0:128]),
                 ["scw", ("v16b", g)], [("i16b", g)])
        allv = [("v16", g) for g in range(16)] + [("v16b", g) for g in range(16)]
        alli = [("i16", g) for g in range(16)] + [("i16b", g) for g in range(16)]
        v16v = v16[:].rearrange("p (h f) k -> p h f k", f=2)
        for h in range(8):
            TT("dve", cand[:, h, :].rearrange("p (a b) -> p a b", a=16),
               v16[:, 2 * h, :].unsqueeze(2).to_broadcast([128, 16, 16]),
               v16[:, 2 * h + 1, :].unsqueeze(1).to_broadcast([128, 16, 16]), ALU.add, r=allv, w=[("cand", h)])
        for h in range(8):
            ch = cand[:, h, :]
            S.op("dve", lambda e, h=h, ch=ch: e.max(out=ts[:, h, 0:8], in_=ch), [("cand", h)], [("ts", h)])
            S.op("dve", lambda e, h=h, ch=ch: e.max_index(out=pos[:, h, 0:8], in_max=ts[:, h, 0:8], in_values=ch), [("cand", h), ("ts", h)], [("pos", h)])
            S.op("dve", lambda e, h=h, ch=ch: e.match_replace(out=scw[:, 0:256], in_to_replace=ts[:, h, 0:8], in_values=ch, imm_value=NEG),
                 [("cand", h), ("ts", h)], ["scw"])
            S.op("dve", lambda e, h=h: e.max(out=ts[:, h, 8:16], in_=scw[:, 0:256]), ["scw"], [("tsb", h)])
            S.op("dve", lambda e, h=h: e.max_index(out=pos[:, h, 8:16], in_max=ts[:, h, 8:16], in_values=scw[:, 0:256]),
                 ["scw", ("tsb", h)], [("posb", h)])
        allts = [("ts", h) for h in range(8)] + [("tsb", h) for h in range(8)]
        allpos = [("pos", h) for h in range(8)] + [("posb", h) for h in range(8)]
        pex3 = pex[:].rearrange("p (h k) -> p h k", h=8)
        TT("dve", pex3, ts[:], ts[:, :, 0:1].to_broadcast([128, 8, 16]), ALU.subtract, r=allts, w=["pex"])
        ACT(pex[:], pex[:], AF.Exp, r=["pex"], w=["pex"])
        S.op("dve", lambda e: e.tensor_reduce(out=psm[:, 0:8], in_=pex3, axis=AX.X, op=ALU.add), ["pex"], ["psm"])
        RCP(psm[:, 8:16], psm[:, 0:8], r=["psm"], w=["psm"])
        TT("dve", pgate[:].rearrange("p (h k) -> p h k", h=8), pex3, psm[:, 8:16].unsqueeze(2).to_broadcast([128, 8, 16]),
           ALU.mult, r=["pex", "psm"], w=["pgate"])
        posi = pos[:].bitcast(I32)
        S.op("dve", lambda e: e.tensor_single_scalar(out=pa[:, 0], in_=posi, scalar=4, op=ALU.logical_shift_right), allpos, ["pa0"])
        S.op("dve", lambda e: e.tensor_single_scalar(out=pa[:, 1], in_=posi, scalar=15, op=ALU.bitwise_and), allpos, ["pa1"])
        CP("dve", paf[:], pa[:], r=["pa0", "pa1"], w=["paf"])
        CP("dve", i16f[:], i16[:], r=alli, w=["i16f"])
        i16fv = i16f[:].rearrange("p (h f) k -> p h f k", f=2)
        for f in range(2):
            for h in range(8):
                ohh = oh[:, h, :].rearrange("p (r a) -> p r a", r=16)
                TT("dve", ohh, paf[:, f, h, :].unsqueeze(2).to_broadcast([128, 16, 16]),
                   iota16[:].unsqueeze(1).to_broadcast([128, 16, 16]), ALU.is_equal, r=["paf", "iota16"], w=[("oh", h)])
                TT("dve", ohh, ohh, i16f[:, 2 * h + f, :].unsqueeze(1).to_broadcast([128, 16, 16]), ALU.mult,
                   r=[("oh", h), "i16f"], w=[("oh", h)])
            S.op("dve", lambda e, f=f: e.tensor_reduce(out=isel[:, f, :], in_=oh[:].rearrange("p h (r a) -> p (h r) a", r=16),
                                                       axis=AX.X, op=ALU.add), [("oh", h) for h in range(8)], [("isel", f)])
        STT(eidx[:], isel[:, 0, :], 128.0, isel[:, 1, :], ALU.mult, ALU.add, r=[("isel", 0), ("isel", 1)], w=["eidx"])

    gctr = [0]

    def phase_D(s):
        DMA("sp", wps[:], wps_d, "dw", r=["wps_d"], w=["wps"])
        DMA("pool", wo[:], wo_d.rearrange("(c p) n -> p c n", p=128), "pw", w=["wo"])
        DMA("sp", G1[:], modscr[s:s + 1, 2 * D:3 * D].partition_broadcast(128), "dw", r=["modscr"], w=["G1"])
        DMA("sp", B2[:], modscr[s:s + 1, 3 * D:4 * D].partition_broadcast(128), "dw", r=["modscr"], w=["B2"])
        DMA("sp", A2[:], modscr[s:s + 1, 4 * D:5 * D].partition_broadcast(128), "dw", r=["modscr"], w=["A2"])
        DMA("sp", G2[:], modscr[s:s + 1, 5 * D:6 * D].partition_broadcast(128), "dw", r=["modscr"], w=["G2"])
        DMA("sp", FG[:], gffn_d.partition_broadcast(128), "dw", w=["FG"])
        STT(A2[:], A2[:], 1.0, FG[:], ALU.add, ALU.mult, r=["A2", "FG"], w=["A2"])
        DMA("sp", FG[:], fg_d.partition_broadcast(128), "dw", r=["A2"], w=["FG"])
        for lt in range(NLT):
            tok = slice(lt * 128, (lt + 1) * 128)
            DMA("sp", dxin[:], x_d[s, tok, :], "dx", w=["dxin"])
            for hf in range(2):
                fs = slice(hf * 512, (hf + 1) * 512)
                for c in range(8):
                    MM(psA[:, fs], yThg[:, c, tok], wo[:, c, fs], start=(c == 0), stop=(c == 7), r=[("yThg", lt), "wo"], w=["psA%d" % hf])
                TT("dve", x1[:, fs], psA[:, fs], G1[:, fs], ALU.mult, r=["psA%d" % hf, "G1"], w=[("x1", hf)])
                TT("dve", x1[:, fs], x1[:, fs], dxin[:, fs], ALU.add, r=[("x1", hf), "dxin"], w=[("x1", hf)])
            x1k = [("x1", 0), ("x1", 1)]
            if "x1" in dbg_out:
                DMA("sp", dbg_out["x1"][lt * 128:(lt + 1) * 128, :], x1[:], "dbg", r=x1k)
            ACT(djunk[:], x1[:], AF.Square, r=x1k, w=["djunk", "ssD"], accum=small[:, 4:5])
            rstd_from_ss(small[:, 4:5], small[:, 5:6], D, ["ssD"], "rsD")
            STT(h2[:], x1[:], small[:, 5:6], A2[:], ALU.mult, ALU.mult, r=x1k + ["rsD", "A2"], w=["h2"])
            TT("dve", h2[:], h2[:], B2[:], ALU.add, r=["h2", "B2"], w=["h2"])
            for c in range(8):
                TR(psA[:, c * 128:(c + 1) * 128], h2[:, c * 128:(c + 1) * 128], identF[:], r=["h2", "identF"], w=["psA%d" % (c // 4)])
            ACP(h2T[:].rearrange("p c t -> p (c t)"), psA[:], r=["psA0", "psA1"], w=["h2T"])
            for q in range(4):
                pt = (psB, psC)[q // 2]
                pk = ("psB0", "psB1", "psC0", "psC1")[q]
                for c in range(8):
                    MM(pt[:, (q % 2) * 512:(q % 2 + 1) * 512], h2T[:, c, :], wps[:, c, q * 512:(q + 1) * 512], start=(c == 0), stop=(c == 7),
                       r=["h2T", "wps"], w=[pk])
                if q % 2 == 0:
                    ACP(sc[:, q * 512:(q + 1) * 512], pt[:, 0:512], r=[pk], w=["sc"])
                else:
                    CP("dve", sc[:, q * 512:(q + 1) * 512], pt[:, 512:1024], r=[pk], w=["sc"])
            peer_topk()
            for j in range(128):
                k = gctr[0] % NGB
                gctr[0] += 1
                S.dma("pool", lambda e, k=k, j=j: e.indirect_dma_start(out=gb[k][:], out_offset=None, in_=pu_d,
                                                                       in_offset=bass.IndirectOffsetOnAxis(ap=eidx[:, j:j + 1], axis=0)),
                      "gb%d" % k, ["eidx"], [("gb", k)])
                STT(djunk[:], gb[k][:], 1.0, h2[:], ALU.mult, ALU.mult, r=[("gb", k), "h2"], w=["djunk", ("pact", j)], accum=pact[:, j:j + 1])
            ACT(pact[:], pact[:], AF.Gelu, r=[("pact", j) for j in range(128)], w=["pact"])
            TT("dve", pact[:], pact[:], pgate[:], ALU.mult, r=["pact", "pgate"], w=["pact"])
            for j in range(128):
                k = gctr[0] % NGB
                gctr[0] += 1
                S.dma("pool", lambda e, k=k, j=j: e.indirect_dma_start(out=gb[k][:], out_offset=None, in_=pv_d,
                                                                       in_offset=bass.IndirectOffsetOnAxis(ap=eidx[:, j:j + 1], axis=0)),
                      "gb%d" % k, ["eidx"], [("gb", k)])
                if j == 0:
                    TS("dve", acc[:], gb[k][:], pact[:, 0:1], None, ALU.mult, r=[("gb", k), "pact"], w=["acc"])
                else:
                    STT(acc[:], gb[k][:], pact[:, j:j + 1], acc[:], ALU.mult, ALU.add, r=[("gb", k), "pact", "acc"], w=["acc"])
            TT("dve", acc[:], acc[:], G2[:], ALU.mult, r=["acc", "G2"], w=["acc"])
            TT("dve", acc[:], acc[:], x1[:], ALU.add, r=["acc"] + x1k, w=["acc"])
            ACT(djunk[:], acc[:], AF.Square, r=["acc"], w=["djunk", "ssF"], accum=small[:, 6:7])
            rstd_from_ss(small[:, 6:7], small[:, 7:8], D, ["ssF"], "rsF")
            STT(h2[:], acc[:], small[:, 7:8], FG[:], ALU.mult, ALU.mult, r=["acc", "rsF", "FG"], w=["h2"])
            DMA("sp", out_d[s, tok, :], h2[:], "out", r=["h2"])

    def dump_mixer():
        S.barrier()
        if "yThg" in dbg_out:
            DMA("sp", dbg_out["yThg"], yThg[:], "dbg", r=[("yThg", lt) for lt in range(NLT)])
            DMA("sp", dbg_out["yTgl"], yTgl[:], "dbg", r=[("yTgl", lt) for lt in range(NLT)])
            DMA("sp", dbg_out["hT"], hT[:], "dbg", r=[("hT", t) for t in range(NT)])
        if "grT" in dbg_out:
            DMA("sp", dbg_out["grT"], grT[:], "dbg", r=["grT"])

    def finish():
        S.emit(final_wait_keys=["out"] + (["dbg"] if dbg_out else []))
        return nc

    for s in range(NS):
        load_seq_mod(s)
        phase_A(s)
        S.barrier()
        if stop == "A":
            dump_mixer()
            S.emit(final_wait_keys=["dbg"])
            return nc
        gr_prepass()
        S.barrier()
        if stop == "gr":
            dump_mixer()
            S.emit(final_wait_keys=["dbg"])
            return nc
        hi = 0
        heads = [("hg", h) for h in range(8)] + [("gl", h) for h in range(4)]
        if stop == "head0":
            heads = [("hg", 0)]
        if stop == "head8":
            heads = [("gl", 0)]
        load_head_w(heads[0][0], heads[0][1], 0)
        for hi, (kind, hh) in enumerate(heads):
            if hi + 1 < len(heads):
                load_head_w(heads[hi + 1][0], heads[hi + 1][1], (hi + 1) % 2)
            head(kind, hh, hi % 2, s)
        if s == 0 and (dbg_out or stop in ("head0", "head8", "heads")):
            dump_mixer()
        if stop in ("head0", "head8", "heads"):
            S.emit(final_wait_keys=["dbg"])
            return nc
        S.barrier()
        phase_C1(s)
        S.barrier()
        phase_D(s)
        S.barrier()
    S.emit(final_wait_keys=["out"] + (["dbg"] if dbg_out else []))
    return nc


_CACHE = {}


def _in_maps(inputs, NS, cores):
    f = lambda a: np.ascontiguousarray(a, dtype=np.float32)
    shared = {
        "c_ctx": f(inputs["c_ctx"]).reshape(1, D),
        "ada_w": f(inputs["ada_w"]).reshape(D, 6 * D),
        "ada_b": f(inputs["ada_b"]).reshape(1, 6 * D),
        "norm_mix_g": f(inputs["norm_mix_g"]).reshape(1, D),
        "w_in": f(inputs["w_in"]).reshape(D, D_IN),
        "lb_logits": f(inputs["hgrn_lb_logits"]).reshape(2, 2 * D),
        "hgrn_norm_g": f(inputs["hgrn_norm_g"]).reshape(1, 128),
        "gla_gk_w": f(inputs["gla_gk_w"]).reshape(2, 16, 512),
        "gla_gk_b": f(inputs["gla_gk_b"]).reshape(2, 512),
        "gla_norm_g": f(inputs["gla_norm_g"]).reshape(1, 256),
        "w_branch_hgrn": f(inputs["w_branch_hgrn"]).reshape(D, D),
        "w_branch_gla": f(inputs["w_branch_gla"]).reshape(D, D),
        "w_out": f(inputs["w_out"]).reshape(D, D),
        "norm_ffn_g": f(inputs["norm_ffn_g"]).reshape(1, D),
        "peer_wq": f(inputs["peer_wq"]).reshape(D, 2048),
        "peer_k1": f(inputs["peer_k1"]).reshape(128, 128),
        "peer_k2": f(inputs["peer_k2"]).reshape(128, 128),
        "peer_u": f(inputs["peer_u"]).reshape(16384, D),
        "peer_v": f(inputs["peer_v"]).reshape(16384, D),
        "final_g": f(inputs["final_g"]).reshape(1, D),
    }
    x = f(inputs["x"])
    ctx = f(inputs["ctx"])
    c = f(inputs["c"])
    maps = []
    for i in range(cores):
        m = dict(shared)
        m["x"] = x[i * NS:(i + 1) * NS]
        m["ctx"] = ctx[i * NS:(i + 1) * NS]
        m["c"] = c[i * NS:(i + 1) * NS]
        maps.append(m)
    return maps


def kernel(**inputs):
    B = inputs["x"].shape[0]
    NS = B // N_CORES
    if NS not in _CACHE:
        _CACHE[NS] = build_program(NS)
    nc = _CACHE[NS]
    maps = _in_maps(inputs, NS, N_CORES)
    res = run_bass_kernel_spmd(nc, maps, core_ids=list(range(N_CORES)))
    return np.concatenate([r["out"] for r in res.results], axis=0).astype(np.float32)
```

```python
import contextlib
import numpy as np
import concourse.bass as bass
import concourse.mybir as mybir
from concourse.bass_utils import run_bass_kernel_spmd

F32 = mybir.dt.float32
BF16 = mybir.dt.bfloat16
I32 = mybir.dt.int32
U32 = mybir.dt.uint32
AF = mybir.ActivationFunctionType
ALU = mybir.AluOpType
AX = mybir.AxisListType

N_CORES = 8
D = 1024
SEQ = 2048
CTX = 256
NLT = SEQ // 128
NCT = CTX // 128
NT = NLT + NCT
TOK = SEQ + CTX
D_IN = 10272
EPS = 1e-6
QSCALE = 128 ** -0.5
NEG = -1e30


class _Op:
    __slots__ = ("eng", "fn", "deps", "signal", "tok_sem", "tok_val", "is_dma", "dma_key")

    def __init__(self, eng, fn, is_dma=False, dma_key=None):
        self.eng = eng
        self.fn = fn
        self.deps = []
        self.signal = False
        self.tok_sem = None
        self.tok_val = 0
        self.is_dma = is_dma
        self.dma_key = dma_key


class Sched:
    ENGS = ("pe", "act", "dve", "pool", "sp")

    def __init__(self, nc):
        self.nc = nc
        self.ops = {e: [] for e in self.ENGS}
        self.last_writer = {}
        self.readers = {}
        self.last_eng = {}
        self.last_key = {}
        self.bar_deps = []
        self.bar_need = set()

    def barrier(self):
        self.bar_deps = list(self.last_eng.values()) + list(self.last_key.values())
        self.bar_need = set(self.ENGS)

    def _add(self, op, reads, writes):
        excl = [k for k in reads if isinstance(k, str) and k.startswith("ps")]
        if excl:
            reads = [k for k in reads if k not in excl]
            writes = list(writes) + [k for k in excl if k not in writes]
        deps = []
        if op.eng in self.bar_need:
            deps.extend(self.bar_deps)
            self.bar_need.discard(op.eng)
        for r in reads:
            w = self.last_writer.get(r)
            if w is not None:
                deps.append(w)
        for wkey in writes:
            w = self.last_writer.get(wkey)
            if w is not None:
                deps.append(w)
            deps.extend(self.readers.get(wkey, ()))
        seen = set()
        for d in deps:
            if d is op or id(d) in seen:
                continue
            seen.add(id(d))
            if (not d.is_dma) and (not op.is_dma) and d.eng == op.eng and op.eng == "pe":
                continue
            op.deps.append(d)
            d.signal = True
        for r in reads:
            self.readers.setdefault(r, []).append(op)
        for wkey in writes:
            self.last_writer[wkey] = op
            self.readers[wkey] = []
        self.ops[op.eng].append(op)
        if op.is_dma:
            self.last_key[op.dma_key] = op
        else:
            self.last_eng[op.eng] = op
        return op

    def op(self, eng, fn, reads=(), writes=()):
        return self._add(_Op(eng, fn), list(reads), list(writes))

    def dma(self, eng, fn, key, reads=(), writes=()):
        o = _Op(eng, fn, is_dma=True, dma_key=key)
        o.signal = True
        return self._add(o, list(reads), list(writes))

    def emit(self, final_wait_keys=()):
        nc = self.nc
        dma_keys = []
        for e in self.ENGS:
            for o in self.ops[e]:
                if o.is_dma and o.dma_key not in dma_keys:
                    dma_keys.append(o.dma_key)
        with contextlib.ExitStack() as st:
            eng_sem = {e: st.enter_context(nc.semaphore("S_" + e)) for e in self.ENGS}
            key_sem = {k: st.enter_context(nc.semaphore("D_%d" % i)) for i, k in enumerate(dma_keys)}
            key_cnt = {k: 0 for k in dma_keys}
            key_eng = {}
            for e in self.ENGS:
                cnt = 0
                for o in self.ops[e]:
                    if o.is_dma:
                        assert key_eng.setdefault(o.dma_key, e) == e, "dma key used from two queues"
                        key_cnt[o.dma_key] += 16
                        o.tok_sem = key_sem[o.dma_key]
                        o.tok_val = key_cnt[o.dma_key]
                    elif o.signal:
                        cnt += 1
                        o.tok_sem = eng_sem[e]
                        o.tok_val = cnt
            blk = st.enter_context(nc.Block())
            engobj = {"pe": "tensor", "act": "scalar", "dve": "vector", "pool": "gpsimd", "sp": "sync"}

            self.stats = {}

            def make(e):
                def body(eng):
                    waited = {}
                    nw = 0
                    for o in self.ops[e]:
                        for d in o.deps:
                            s = d.tok_sem
                            if waited.get(id(s), 0) >= d.tok_val:
                                continue
                            waited[id(s)] = d.tok_val
                            eng.wait_ge(s, d.tok_val)
                            nw += 1
                        ins = o.fn(eng)
                        if o.is_dma:
                            ins.then_inc(o.tok_sem, 16)
                        elif o.signal:
                            ins.then_inc(o.tok_sem, 1)
                    self.stats[e] = (len(self.ops[e]), nw)
                    if e == "sp":
                        for k in final_wait_keys:
                            if k in key_sem:
                                eng.wait_ge(key_sem[k], key_cnt[k])
                return body

            for e in self.ENGS:
                getattr(blk, engobj[e])(make(e))


def build_program(NS, dbg=(), stop=None):
    nc = bass.Bass("TRN2", target_bir_lowering=False)

    def din(name, shape, dt=F32):
        return nc.dram_tensor(name, list(shape), dt, kind="ExternalInput").ap()

    x_d = din("x", [NS, SEQ, D])
    ctx_d = din("ctx", [NS, CTX, D])
    c_d = din("c", [NS, D])
    cctx_d = din("c_ctx", [1, D])
    adaw_d = din("ada_w", [D, 6 * D])
    adab_d = din("ada_b", [1, 6 * D])
    gmix_d = din("norm_mix_g", [1, D])
    win_d = din("w_in", [D, D_IN])
    lbl_d = din("lb_logits", [2, 2 * D])
    hgg_d = din("hgrn_norm_g", [1, 128])
    gkw_d = din("gla_gk_w", [2, 16, 512])
    gkb_d = din("gla_gk_b", [2, 512])
    glg_d = din("gla_norm_g", [1, 256])
    wbh_d = din("w_branch_hgrn", [D, D])
    wbg_d = din("w_branch_gla", [D, D])
    wo_d = din("w_out", [D, D])
    gffn_d = din("norm_ffn_g", [1, D])
    wq_d = din("peer_wq", [D, 2048])
    k1_d = din("peer_k1", [128, 128])
    k2_d = din("peer_k2", [128, 128])
    pu_d = din("peer_u", [16384, D])
    pv_d = din("peer_v", [16384, D])
    fg_d = din("final_g", [1, D])
    out_d = nc.dram_tensor("out", [NS, SEQ, D], F32, kind="ExternalOutput").ap()
    modscr = nc.dram_tensor("modscr", [NS + 1, 6 * D], F32, kind="Internal").ap()
    wps_d = nc.dram_tensor("wps", [128, 8, 2048], BF16, kind="Internal").ap()
    pub_d = nc.dram_tensor("pu_bf", [16384, D], BF16, kind="Internal").ap()
    pvb_d = nc.dram_tensor("pv_bf", [16384, D], BF16, kind="Internal").ap()
    dbg_out = {}
    for name, shape, dt in dbg:
        dbg_out[name] = nc.dram_tensor("dbg_" + name, list(shape), dt, kind="ExternalOutput").ap()

    ARENA = 212000
    arena = nc.alloc_sbuf_tensor("arena", [128, ARENA // 4], F32)
    base = nc.lookup_mloc(arena).addr
    cur = {"p": base, "limit": base + ARENA}

    def _sz(shape, dt):
        n = int(np.prod(shape[1:])) * (2 if dt == BF16 else 4)
        return (n + 63) // 64 * 64

    def alloc(name, shape, dt, at=None):
        n = _sz(shape, dt)
        if at is None:
            off = cur["p"]
            cur["p"] += n
            assert cur["p"] <= cur["limit"], ("SBUF overflow", name, cur["p"] - base)
        else:
            off = at[0]
            at[0] += n
            assert at[0] <= cur["limit"], ("SBUF phase overflow", name, at[0] - base)
        return nc.alloc_sbuf_tensor_at(name, list(shape), dt, offset=off)

    identF = alloc("identF", [128, 128], F32)
    identB = alloc("identB", [128, 128], BF16)
    TRI = [alloc("TRIf", [128, 128], F32), alloc("TRIb", [128, 128], F32)]
    M1 = [alloc("M1f", [128, 128], F32), alloc("M1b", [128, 128], F32)]
    UU = [alloc("Uf", [128, 128], F32), alloc("Ub", [128, 128], F32)]
    hgg_bc = alloc("hgg_bc", [128, 128], F32)
    glg_bc = alloc("glg_bc", [128, 256], F32)
    modsm = alloc("modsm", [128, 6, 8], F32)
    small = alloc("small", [128, 16], F32)
    yThg = alloc("yThg", [128, 8, SEQ], BF16)
    wgr = alloc("wgr", [128, 8, 64], BF16)
    gkw = alloc("gkw", [64, 512], F32)
    grT = alloc("grT", [64, TOK], F32)
    iota16 = alloc("iota16", [128, 16], F32)
    PH = cur["p"]

    a = [PH]
    hT = alloc("hT", [128, 8, TOK], BF16, a)
    yTgl = alloc("yTgl", [128, 8, SEQ], BF16, a)
    MX = a[0]
    a = [MX]
    qT = alloc("qT", [128, SEQ], F32, a)
    vst = alloc("vst", [128, NT, 256], BF16, a)
    ofw = alloc("ofw", [128, NLT, 256], F32, a)
    whd = [alloc("whd0", [128, 8, 768], BF16, a), alloc("whd1", [128, 8, 768], BF16, a)]
    oml = alloc("oml", [128, 2, 128], F32, a)
    omt = alloc("omt", [128, 2, 2, 128], F32, a)
    t_e = alloc("t_e", [128, 256], F32, a)
    t_r = alloc("t_r", [128, 256], F32, a)
    t_k = alloc("t_k", [128, 128], F32, a)
    t_f = alloc("t_f", [128, 128], F32, a)
    t_lg = alloc("t_lg", [128, 128], F32, a)
    t_P = alloc("t_P", [128, 4, 128], F32, a)
    t_qd = alloc("t_qd", [128, 128], BF16, a)
    t_kdT = alloc("t_kdT", [128, 128], BF16, a)
    t_qdec = alloc("t_qdec", [128, 128], BF16, a)
    t_kdec = alloc("t_kdec", [128, 128], BF16, a)
    t_scm = [alloc("t_scm0", [128, 128], BF16, a), alloc("t_scm1", [128, 128], BF16, a)]
    St = alloc("St", [128, 256], F32, a)
    Sbf = alloc("Sbf", [128, 256], BF16, a)
    t_o = alloc("t_o", [128, 256], F32, a)
    t_gate = alloc("t_gate", [128, 256], F32, a)
    t_junk = alloc("t_junk", [128, 256], F32, a)
    t_y = alloc("t_y", [128, 256], BF16, a)
    a = [MX]
    xin = [alloc("xin0", [128, D], F32, a), alloc("xin1", [128, D], F32, a)]
    xs = alloc("xs", [128, D], F32, a)
    a = [MX]
    wbh = alloc("wbh", [128, 8, D], BF16, a)
    wbg = alloc("wbg", [128, 8, D], BF16, a)
    wm = alloc("wm", [128, 8, 2 * D], BF16, a)
    c_e = [alloc("c_e0", [128, 512], F32, a), alloc("c_e1", [128, 512], F32, a)]
    c_t = [alloc("c_t0", [128, 512], F32, a), alloc("c_t1", [128, 512], F32, a)]
    c_ym = alloc("c_ym", [128, D], BF16, a)
    a = [PH]
    s_l = alloc("s_l", [128, 2, 2 * D], F32, a)
    adaw = [alloc("adaw0", [128, 8, 512], F32, a), alloc("adaw1", [128, 8, 512], F32, a)]
    cT = alloc("cT", [128, 8, NS + 1], F32, a)
    scT = alloc("scT", [128, 8, NS + 1], F32, a)
    s_tmp = alloc("s_tmp", [128, 8, NS + 1], F32, a)
    modrows = alloc("modrows", [NS + 1, 6 * D], F32, a)
    adab_sb = alloc("adab_sb", [NS + 1, 6 * D], F32, a)
    wqb = [alloc("wqb0", [128, 8, 128], F32, a), alloc("wqb1", [128, 8, 128], F32, a)]
    kraw = alloc("kraw", [128, 2, 128], F32, a)
    kT = alloc("kT", [128, 2, 128], F32, a)
    wqT = [alloc("wqT0", [128, 128], F32, a), alloc("wqT1", [128, 128], F32, a)]
    wpb = [alloc("wpb0", [128, 8, 128], BF16, a), alloc("wpb1", [128, 8, 128], BF16, a)]
    a = [PH]
    cst_f = [alloc("cst_f0", [128, 4096], F32, a), alloc("cst_f1", [128, 4096], F32, a)]
    cst_b = [alloc("cst_b0", [128, 4096], BF16, a), alloc("cst_b1", [128, 4096], BF16, a)]
    a = [PH]
    wps = alloc("wps_sb", [128, 8, 2048], BF16, a)
    wo = alloc("wo", [128, 8, D], BF16, a)
    G1 = alloc("G1", [128, D], F32, a)
    A2 = alloc("A2", [128, D], F32, a)
    B2 = alloc("B2", [128, D], F32, a)
    G2 = alloc("G2", [128, D], F32, a)
    FG = alloc("FG", [128, D], F32, a)
    NGB = 16
    gb = [alloc("gb%d" % i, [128, D], BF16, a) for i in range(NGB)]
    NDG = 4
    dg = [alloc("dg%d" % i, [128, 128], BF16, a) for i in range(NDG)]
    sc = alloc("sc", [128, 2048], F32, a)
    scw = alloc("scw", [128, 256], F32, a)
    dxin = alloc("dxin", [128, D], F32, a)
    x1 = alloc("x1", [128, D], F32, a)
    h2 = alloc("h2", [128, D], F32, a)
    h2T = alloc("h2T", [128, 8, 128], BF16, a)
    acc = alloc("acc", [128, D], F32, a)
    djunk = alloc("djunk", [128, D], F32, a)
    v16 = alloc("v16", [128, 16, 16], F32, a)
    i16 = alloc("i16", [128, 16, 16], U32, a)
    i16f = alloc("i16f", [128, 16, 16], F32, a)
    cand = alloc("cand", [128, 8, 256], F32, a)
    ts = alloc("ts", [128, 8, 16], F32, a)
    pos = alloc("pos", [128, 8, 16], U32, a)
    pa = alloc("pa", [128, 2, 8, 16], I32, a)
    paf = alloc("paf", [128, 2, 8, 16], F32, a)
    oh = alloc("oh", [128, 8, 256], F32, a)
    isel = alloc("isel", [128, 2, 128], F32, a)
    eidx = alloc("eidx", [128, 128], I32, a)
    pgate = alloc("pgate", [128, 128], F32, a)
    pact = alloc("pact", [128, 128], F32, a)
    pex = alloc("pex", [128, 128], F32, a)
    psm = alloc("psm", [128, 16], F32, a)

    psA = nc.alloc_psum_tensor("psA", [128, 1024], F32)
    psB = nc.alloc_psum_tensor("psB", [128, 1024], F32)
    psC = nc.alloc_psum_tensor("psC", [128, 1024], F32)
    psD = nc.alloc_psum_tensor("psD", [128, 512], F32)
    psE = nc.alloc_psum_tensor("psE", [128, 1024], BF16)

    S = Sched(nc)

    def MM(out, lhsT, rhs, start=True, stop=True, r=(), w=()):
        S.op("pe", lambda e: e.matmul(out, lhsT=lhsT, rhs=rhs, start=start, stop=stop), r, w)

    def TR(out, in_, ident, r=(), w=()):
        S.op("pe", lambda e: e.transpose(out=out, in_=in_, identity=ident), r, w)

    def ACT(out, in_, func, r=(), w=(), bias=0.0, scale=1.0, accum=None):
        if accum is None:
            S.op("act", lambda e: e.activation(out=out, in_=in_, func=func, bias=bias, scale=scale), r, w)
        else:
            S.op("act", lambda e: e.activation(out=out, in_=in_, func=func, bias=bias, scale=scale, accum_out=accum), r, w)

    def ACP(out, in_, r=(), w=()):
        S.op("act", lambda e: e.copy(out=out, in_=in_), r, w)

    def AMUL(out, in_, mul, r=(), w=()):
        S.op("act", lambda e: e.mul(out=out, in_=in_, mul=mul), r, w)

    def TT(eng, out, in0, in1, op, r=(), w=()):
        S.op(eng, lambda e: e.tensor_tensor(out=out, in0=in0, in1=in1, op=op), r, w)

    def TS(eng, out, in0, s1, s2, op0, op1=None, r=(), w=()):
        if op1 is None:
            S.op(eng, lambda e: e.tensor_scalar(out=out, in0=in0, scalar1=s1, scalar2=None, op0=op0), r, w)
        else:
            S.op(eng, lambda e: e.tensor_scalar(out=out, in0=in0, scalar1=s1, scalar2=s2, op0=op0, op1=op1), r, w)

    def STT(out, in0, scalar, in1, op0, op1, r=(), w=(), accum=None):
        if accum is None:
            S.op("dve", lambda e: e.scalar_tensor_tensor(out=out, in0=in0, scalar=scalar, in1=in1, op0=op0, op1=op1), r, w)
        else:
            S.op("dve", lambda e: e.scalar_tensor_tensor(out=out, in0=in0, scalar=scalar, in1=in1, op0=op0, op1=op1, accum_out=accum), r, w)

    def CP(eng, out, in_, r=(), w=()):
        S.op(eng, lambda e: e.tensor_copy(out=out, in_=in_), r, w)

    def RCP(out, in_, r=(), w=()):
        S.op("dve", lambda e: e.reciprocal(out=out, in_=in_), r, w)

    def MSET(eng, ap, val, w=()):
        S.op(eng, lambda e: e.memset(ap, val), (), w)

    def DMA(eng, out, in_, key, r=(), w=(), slow=False):
        if slow:
            S.dma(eng, lambda e: e.dma_start(out=out, in_=in_, allow_slow_non_contiguous=True), key, r, w)
        else:
            S.dma(eng, lambda e: e.dma_start(out=out, in_=in_), key, r, w)

    def ASEL(out, pattern, cmp, base_, cm, r=(), w=()):
        S.op("pool", lambda e: e.affine_select(out=out, in_=out, pattern=pattern, compare_op=cmp, fill=0.0,
                                               base=base_, channel_multiplier=cm), r, w)

    def rstd_from_ss(ss, out, n, keyr, keyw):
        ACT(out, ss, AF.Ln, r=keyr, w=[keyw], bias=EPS, scale=1.0 / n)
        ACT(out, out, AF.Exp, r=[keyw], w=[keyw], scale=-0.5)

    def tri_const(t, pattern, cmp, base_, cm, key):
        MSET("pool", t[:], 1.0, w=[key])
        ASEL(t[:], pattern, cmp, base_, cm, r=[key], w=[key])

    tri_const(identF, [[-1, 128]], ALU.is_equal, 0, 1, "identF")
    CP("dve", identB[:], identF[:], r=["identF"], w=["identB"])
    tri_const(TRI[0], [[1, 128]], ALU.is_ge, 0, -1, "TRI0")
    tri_const(TRI[1], [[-1, 128]], ALU.is_ge, 0, 1, "TRI1")
    tri_const(UU[0], [[-1, 128]], ALU.is_gt, 0, 1, "UU0")
    tri_const(UU[1], [[1, 128]], ALU.is_gt, 0, -1, "UU1")
    tri_const(M1[0], [[0, 128]], ALU.is_ge, 64, -1, "M10")
    tri_const(M1[1], [[0, 128]], ALU.is_ge, -63, 1, "M11")
    TT("dve", M1[0][:], TRI[0][:], M1[0][:], ALU.subtract, r=["TRI0", "M10"], w=["M10"])
    TT("dve", M1[1][:], TRI[1][:], M1[1][:], ALU.subtract, r=["TRI1", "M11"], w=["M11"])
    S.op("pool", lambda e: e.iota(iota16[:], pattern=[[1, 16]], base=0, channel_multiplier=0,
                                  allow_small_or_imprecise_dtypes=True), (), ["iota16"])
    if stop == "s1":
        S.emit(final_wait_keys=[])
        return nc
    DMA("sp", hgg_bc[:], hgg_d.partition_broadcast(128), "su", w=["hgg_bc"])
    DMA("sp", glg_bc[:], glg_d.partition_broadcast(128), "su", w=["glg_bc"])
    MSET("pool", wgr[:], 0.0, w=["wgr"])
    winv = win_d.rearrange("(c p) n -> p c n", p=128)
    DMA("pool", wgr[:, :, 0:16], winv[:, :, 8192:8208], "pw", r=["wgr"], w=["wgr"])
    DMA("pool", wgr[:, :, 32:48], winv[:, :, 8208:8224], "pw", r=["wgr"], w=["wgr"])
    MSET("dve", grT[:], 1.0, w=["grT"])
    MSET("dve", gkw[:], 0.0, w=["gkw"])
    DMA("sp", gkw[0:16, :], gkw_d[0], "su", r=["gkw"], w=["gkw"])
    DMA("sp", gkw[16:17, :], gkb_d[0:1, :], "su", w=["gkw"])
    DMA("sp", gkw[32:48, :], gkw_d[1], "su", w=["gkw"])
    DMA("sp", gkw[48:49, :], gkb_d[1:2, :], "su", w=["gkw"])
    if stop == "s2":
        S.emit(final_wait_keys=[])
        return nc
    for b_ in range(NS):
        DMA("sp", cT[:, :, b_:b_ + 1], c_d[b_:b_ + 1, :].rearrange("b (c p) -> p c b", p=128), "su", w=["cT"], slow=True)
    DMA("sp", cT[:, :, NS:NS + 1], cctx_d.rearrange("b (c p) -> p c b", p=128), "su", w=["cT"], slow=True)
    DMA("sp", adab_sb[:], adab_d.partition_broadcast(NS + 1), "su", w=["adab"])
    ACT(s_tmp[:], cT[:], AF.Exp, r=["cT"], w=["s_tmp"], scale=-1.0)
    TS("dve", s_tmp[:], s_tmp[:], 1.0, None, ALU.add, r=["s_tmp"], w=["s_tmp"])
    RCP(s_tmp[:], s_tmp[:], r=["s_tmp"], w=["s_tmp"])
    TT("dve", scT[:], cT[:], s_tmp[:], ALU.mult, r=["cT", "s_tmp"], w=["scT"])
    if stop == "s3":
        S.emit(final_wait_keys=[])
        return nc
    adawv = adaw_d.rearrange("(c p) n -> p c n", p=128)
    for n in range(12):
        ab = adaw[n % 2]
        DMA("sp", ab[:], adawv[:, :, n * 512:(n + 1) * 512], "aw", w=["adaw%d" % (n % 2)])
        for c in range(8):
            MM(psA[0:NS + 1, 0:512], scT[:, c, :], ab[:, c, :], start=(c == 0), stop=(c == 7),
               r=["scT", "adaw%d" % (n % 2)], w=["psA0"])
        TT("dve", modrows[:, n * 512:(n + 1) * 512], psA[0:NS + 1, 0:512], adab_sb[:, n * 512:(n + 1) * 512],
           ALU.add, r=["psA0", "adab"], w=["modrows"])
    if stop == "s4":
        S.emit(final_wait_keys=[])
        return nc
    DMA("sp", modscr, modrows[:], "ms", r=["modrows"], w=["modscr"])
    DMA("sp", modsm[:, 0, :], gmix_d.rearrange("o (c p) -> p (o c)", p=128), "ms2", w=["gmixT"], slow=True)
    DMA("sp", modsm[:, 2, :], modscr[NS:NS + 1, 0:D].rearrange("o (c p) -> p (o c)", p=128), "ms2",
        r=["modscr"], w=["B1Tc"], slow=True)
    DMA("sp", modsm[:, 5, :], modscr[NS:NS + 1, D:2 * D].rearrange("o (c p) -> p (o c)", p=128), "ms2",
        r=["modscr"], w=["mtmp"], slow=True)
    STT(modsm[:, 1, :], modsm[:, 5, :], 1.0, modsm[:, 0, :], ALU.add, ALU.mult, r=["mtmp", "gmixT"], w=["A1Tc"])
    if stop == "s5":
        S.emit(final_wait_keys=[])
        return nc
    DMA("sp", kraw[:, 0, :], k1_d, "su", w=["kraw"])
    DMA("sp", kraw[:, 1, :], k2_d, "su", w=["kraw"])
    for hf in range(2):
        TR(psB[:, hf * 128:(hf + 1) * 128], kraw[:, hf, :], identF[:], r=["kraw", "identF"], w=["psB0"])
    CP("dve", kT[:].rearrange("p a b -> p (a b)"), psB[:, 0:256], r=["psB0"], w=["kT"])
    wqv = wq_d.rearrange("(c p) n -> p c n", p=128)
    if stop == "s6":
        S.emit(final_wait_keys=[])
        return nc
    import os
    WST = int(os.environ.get("WST", "9"))
    for g in range(16 if stop != "s7" else 1):
        wb_ = wqb[g % 2]
        DMA("sp", wb_[:], wqv[:, :, g * 128:(g + 1) * 128], "wq", w=["wqb%d" % (g % 2)])
        for c in range(8):
            i = (g * 8 + c) % 2
            TR(psC[:, i * 512:i * 512 + 128], wb_[:, c, :], identF[:], r=["wqb%d" % (g % 2), "identF"], w=["psC%d" % i])
            if i == 0:
                CP("dve", wqT[i][:], psC[:, i * 512:i * 512 + 128], r=["psC%d" % i], w=["wqT%d" % i])
            else:
                ACP(wqT[i][:], psC[:, i * 512:i * 512 + 128], r=["psC%d" % i], w=["wqT%d" % i])
            MM(psA[:, i * 512:i * 512 + 128], wqT[i][:], kT[:, g % 2, :], r=["wqT%d" % i, "kT"], w=["psA%d" % i])
            CP("dve", wpb[g % 2][:, c, :], psA[:, i * 512:i * 512 + 128], r=["psA%d" % i], w=["wpb%d" % (g % 2)])
        DMA("sp", wps_d[:, :, g * 128:(g + 1) * 128], wpb[g % 2][:], "wpo", r=["wpb%d" % (g % 2)], w=["wps_d"])
    S.barrier()
    if stop == "s7":
        S.emit(final_wait_keys=[])
        return nc
    ci = 0
    for src_t, dst_t in ((pu_d, pub_d), (pv_d, pvb_d)):
        sv = src_t.rearrange("(n p r) d -> n p (r d)", p=128, r=4)
        dv = dst_t.rearrange("(n p r) d -> n p (r d)", p=128, r=4)
        for n in range(32):
            i = ci % 2
            ci += 1
            DMA("sp", cst_f[i][:], sv[n], "cin", w=["cst_f%d" % i])
            if i == 0:
                ACP(cst_b[i][:], cst_f[i][:], r=["cst_f%d" % i], w=["cst_b%d" % i])
            else:
                CP("dve", cst_b[i][:], cst_f[i][:], r=["cst_f%d" % i], w=["cst_b%d" % i])
            DMA("sp", dv[n], cst_b[i][:], "cout", r=["cst_b%d" % i], w=["ptab"])
    S.barrier()
    if stop == "setup":
        if "modrows" in dbg_out:
            DMA("sp", dbg_out["modrows"], modscr, "dbg", r=["modscr"])
            DMA("sp", dbg_out["wps"], wps_d, "dbg", r=["wps_d"])
        S.emit(final_wait_keys=["dbg"] if dbg_out else [])
        return nc

    def phase_A(s):
        for t in range(NT):
            xb = xin[t % 2]
            xk = "xin%d" % (t % 2)
            src = ctx_d[s, t * 128:(t + 1) * 128, :] if t < NCT else x_d[s, (t - NCT) * 128:(t - NCT + 1) * 128, :]
            DMA("sp", xb[:], src, "xin", w=[xk])
            ACT(xs[:], xb[:], AF.Square, r=[xk], w=["xs", "ssA"], accum=small[:, 0:1])
            rstd_from_ss(small[:, 0:1], small[:, 1:2], D, ["ssA"], "rsA")
            AMUL(xs[:], xb[:], small[:, 1:2], r=[xk, "rsA"], w=["xs"])
            for c in range(8):
                TR(psA[:, c * 128:(c + 1) * 128], xs[:, c * 128:(c + 1) * 128], identF[:], r=["xs", "identF"], w=["psA%d" % (c // 4)])
            ai, bi = (1, 2) if t < NCT else (3, 4)
            an, bn = ("A1Tc", "B1Tc") if t < NCT else ("A1Ts", "B1Ts")
            for c in range(8):
                TS("dve", hT[:, c, t * 128:(t + 1) * 128], psA[:, c * 128:(c + 1) * 128], modsm[:, ai, c:c + 1], modsm[:, bi, c:c + 1],
                   ALU.mult, ALU.add, r=["psA%d" % (c // 4), an, bn], w=[("hT", t)])

    def load_seq_mod(s):
        DMA("sp", modsm[:, 4, :], modscr[s:s + 1, 0:D].rearrange("o (c p) -> p (o c)", p=128), "ms2",
            r=["modscr"], w=["B1Ts"], slow=True)
        DMA("sp", modsm[:, 5, :], modscr[s:s + 1, D:2 * D].rearrange("o (c p) -> p (o c)", p=128), "ms2",
            r=["modscr"], w=["mtmp"], slow=True)
        STT(modsm[:, 3, :], modsm[:, 5, :], 1.0, modsm[:, 0, :], ALU.add, ALU.mult, r=["mtmp", "gmixT"], w=["A1Ts"])

    def gr_prepass():
        for g0 in range(0, TOK, 512):
            n = min(512, TOK - g0)
            for c in range(8):
                MM(psC[0:64, 512:512 + n], wgr[:, c, :], hT[:, c, g0:g0 + n], start=(c == 0), stop=(c == 7),
                   r=["wgr"] + [("hT", t) for t in range(g0 // 128, (g0 + n) // 128)], w=["psC1"])
            CP("dve", grT[0:16, g0:g0 + n], psC[0:16, 512:512 + n], r=["psC1"], w=["grT"])
            ACP(grT[32:48, g0:g0 + n], psC[32:48, 512:512 + n], r=["psC1"], w=["grT"])

    def load_head_w(kind, hh, buf):
        wt = whd[buf]
        k = "whd%d" % buf
        if kind == "hg":
            cols = [(hh * 128, 128), (D + hh * 128, 128), (3 * D + hh * 128, 128), (2 * D + hh * 128, 128), (4 * D + hh * 128, 128)]
        else:
            cols = [(5120 + hh * 128, 128), (6144 + hh * 256, 256), (5632 + hh * 128, 128), (7168 + hh * 256, 256)]
        o = 0
        for c0, n in cols:
            DMA("pool", wt[:, :, o:o + n], winv[:, :, c0:c0 + n], "pw", w=[k])
            o += n

    def scan_tile(kind, hh, buf, d, t, V, escale):
        import os
        wt = whd[buf]
        wk = "whd%d" % buf
        is_ctx = t < NCT
        lt = t - NCT
        tok = slice(t * 128, (t + 1) * 128)
        if kind == "hg":
            c0, ncol = (128, 256) if d == 0 else (384, 256)
        else:
            c0, ncol = (128, 384) if d == 0 else (384, 384)
        for c in range(8):
            MM(psB[:, 0:ncol], hT[:, c, tok], wt[:, c, c0:c0 + ncol], start=(c == 0), stop=(c == 7),
               r=[("hT", t), wk], w=["psB0"])
        if kind == "hg":
            z = psB[:, 0:128]
            ACT(t_e[:, 0:128], z, AF.Exp, r=["psB0"], w=["t_e"], scale=-1.0)
            TS("dve", t_e[:, 0:128], t_e[:, 0:128], 1.0, None, ALU.add, r=["t_e"], w=["t_e"])
            RCP(t_r[:, 0:128], t_e[:, 0:128], r=["t_e"], w=["t_r"])
            TS("dve", t_r[:, 0:128], t_r[:, 0:128], -1.0, 1.0, ALU.mult, ALU.add, r=["t_r"], w=["t_r"])
            TT("dve", t_k[:], t_r[:, 0:128], oml[:, d, :], ALU.mult, r=["t_r", "oml"], w=["t_k"])
            TS("dve", t_f[:], t_k[:], -1.0, 1.0, ALU.mult, ALU.add, r=["t_k"], w=["t_f"])
            ACT(t_lg[:], t_f[:], AF.Ln, r=["t_f"], w=["t_lg"])
            if d == 0 and not (os.environ.get("NOV17") and t == 17):
                ACP(vst[:, t, 0:128], psB[:, 128:256], r=["psB0"], w=[("vst", t)])
            gsrc = psB[:, 128:256]
        else:
            if d == 0:
                ACP(vst[:, t, :], psB[:, 0:256], r=["psB0"], w=[("vst", t)])
                CP("dve", t_k[:], psB[:, 256:384], r=["psB0"], w=["t_k"])
            else:
                CP("dve", t_k[:], psB[:, 0:128], r=["psB0"], w=["t_k"])
            gsrc = psB[:, 128:384]
            pb = 32 * d
            MM(psC[:, 640:768], grT[pb:pb + 32, tok], gkw[pb:pb + 32, hh * 128:(hh + 1) * 128], r=["grT", "gkw"], w=["psC1"])
            ACT(t_e[:, 0:128], psC[:, 640:768], AF.Exp, r=["psC1"], w=["t_e"], scale=-1.0)
            ACT(t_lg[:], t_e[:, 0:128], AF.Ln, r=["t_e"], w=["t_lg"], bias=1.0)
        if d == 1 and not is_ctx:
            ACT(t_e[:, 0:V], gsrc, AF.Exp, r=["psB0"], w=["t_e"], scale=-1.0)
            TS("dve", t_e[:, 0:V], t_e[:, 0:V], 1.0, None, ALU.add, r=["t_e"], w=["t_e"])
            RCP(t_r[:, 0:V], t_e[:, 0:V], r=["t_e"], w=["t_r"])
            TT("dve", t_gate[:, 0:V], gsrc, t_r[:, 0:V], ALU.mult, r=["psB0", "t_r"], w=["t_gate"])
            gbc = hgg_bc if kind == "hg" else glg_bc
            TT("dve", t_gate[:, 0:V], t_gate[:, 0:V], gbc[:, 0:V], ALU.mult, r=["t_gate", "hgg_bc", "glg_bc"], w=["t_gate"])
        HST = int(os.environ.get("HST", "9"))
        if HST < 2:
            return
        HSK = os.environ.get("HSK", "").split(",")
        if not is_ctx and "a" not in HSK:
            MM(psC[:, 0:128], t_lg[:], M1[d][:], r=["t_lg", "M1%d" % d], w=["psC0"])
        if "b" not in HSK:
            MM(psC[:, 128:256], t_lg[:], TRI[d][:], r=["t_lg", "TRI%d" % d], w=["psC0"])
        if "c" not in HSK:
            MM(psC[:, 256:384], UU[d][:], t_lg[:], r=["t_lg", "UU%d" % d], w=["psC0"])
        if not is_ctx:
            if "d" not in HSK:
                TR(psC[:, 384:512], t_k[:], identF[:], r=["t_k", "identF"], w=["psC0"])
            if "e" not in HSK:
                ACT(t_P[:, 0, :], psC[:, 0:128], AF.Exp, r=["psC0"], w=["P1"], scale=escale)
                ACT(t_P[:, 1, :], psC[:, 0:128], AF.Exp, r=["psC0"], w=["P2"], scale=-escale)
        if "e" not in HSK:
            ACT(t_P[:, 2, :], psC[:, 128:256], AF.Exp, r=["psC0"], w=["P3"], scale=escale)
            ACT(t_P[:, 3, :], psC[:, 256:384], AF.Exp, r=["psC0"], w=["P4"], scale=escale)
        if HST < 3:
            return
        PEN = os.environ.get("PEN", "dve")
        if "f" not in HSK:
            TT(PEN, t_kdec[:], t_k[:], t_P[:, 3, :], ALU.mult, r=["t_k", "P4"], w=["t_kdec"])
        if not is_ctx:
            qs = qT[:, lt * 128:(lt + 1) * 128]
            if "g" not in HSK:
                TT("dve", t_qd[:], qs, t_P[:, 0, :], ALU.mult, r=["qT", "P1"], w=["t_qd"])
            if "h" not in HSK:
                TT("dve", t_kdT[:], psC[:, 384:512], t_P[:, 1, :], ALU.mult, r=["psC0", "P2"], w=["t_kdT"])
            if "i" not in HSK:
                TT(PEN, t_qdec[:], qs, t_P[:, 2, :], ALU.mult, r=["qT", "P3"], w=["t_qdec"])
            if "j" not in HSK:
                MM(psC[:, 512:640], t_kdT[:], t_qd[:], r=["t_kdT", "t_qd"], w=["psC1"])
            if "k" not in HSK:
                S.op("dve", lambda e, d=d: e.copy_predicated(out=t_scm[d][:], mask=TRI[d][:].bitcast(U32), data=psC[:, 512:640]),
                     ["psC1", "TRI%d" % d], ["t_scm%d" % d])
            if "l" not in HSK:
                MM(psD[:, 0:V], t_scm[d][:], vst[:, t, 0:V], start=True, stop=False, r=["t_scm%d" % d, ("vst", t)], w=["psD"])
                MM(psD[:, 0:V], t_qdec[:], Sbf[:, 0:V], start=False, stop=True, r=["t_qdec", "Sbf"], w=["psD"])
        if HST < 4:
            return
        MM(psD[:, 256:256 + V], t_kdec[:], vst[:, t, 0:V], r=["t_kdec", ("vst", t)], w=["psD"])
        dcol = 127 if d == 0 else 0
        STT(St[:, 0:V], St[:, 0:V], t_P[:, 2, dcol:dcol + 1], psD[:, 256:256 + V], ALU.mult, ALU.add,
            r=["St", "P3", "psD"], w=["St"])
        ACP(Sbf[:, 0:V], St[:, 0:V], r=["St"], w=["Sbf"])
        if is_ctx:
            return
        if d == 0:
            ACP(ofw[:, lt, 0:V], psD[:, 0:V], r=["psD"], w=[("ofw", lt)])
            return
        if HST < 5:
            return
        TT("dve", t_o[:, 0:V], psD[:, 0:V], ofw[:, lt, 0:V], ALU.add, r=["psD", ("ofw", lt)], w=["t_o"])
        ACT(t_junk[:, 0:V], t_o[:, 0:V], AF.Square, r=["t_o"], w=["t_junk", "ssH"], accum=small[:, 2:3])
        rstd_from_ss(small[:, 2:3], small[:, 3:4], V, ["ssH"], "rsH")
        STT(t_y[:, 0:V], t_o[:, 0:V], small[:, 3:4], t_gate[:, 0:V], ALU.mult, ALU.mult, r=["t_o", "rsH", "t_gate"], w=["t_y"])
        dst = yThg if kind == "hg" else yTgl
        dk = "yThg" if kind == "hg" else "yTgl"
        for j in range(V // 128):
            TR(psE[:, j * 128:(j + 1) * 128], t_y[:, j * 128:(j + 1) * 128], identB[:], r=["t_y", "identB"], w=["psE"])
        for j in range(V // 128):
            ch = hh * (V // 128) + j
            ACP(dst[:, ch, lt * 128:(lt + 1) * 128], psE[:, j * 128:(j + 1) * 128], r=["psE"], w=[(dk, lt)])

    def head(kind, hh, buf, s):
        V = 128 if kind == "hg" else 256
        escale = 1.0 if kind == "hg" else -1.0 / 16.0
        wt = whd[buf]
        wk = "whd%d" % buf
        if kind == "hg":
            for d in range(2):
                for l in range(2):
                    DMA("sp", omt[:, d, l, :], lbl_d[l:l + 1, d * D + hh * 128:d * D + (hh + 1) * 128].partition_broadcast(128),
                        "su", w=["omt"])
            TT("dve", omt[:, :, 0, :], omt[:, :, 1, :], omt[:, :, 0, :], ALU.subtract, r=["omt"], w=["omt"])
            ACT(omt[:, :, 0, :], omt[:, :, 0, :], AF.Exp, r=["omt"], w=["omt"])
            TS("dve", omt[:, :, 1, :], omt[:, :, 0, :], 1.0, None, ALU.add, r=["omt"], w=["omt"])
            RCP(omt[:, :, 1, :], omt[:, :, 1, :], r=["omt"], w=["omt"])
            TT("dve", oml[:], omt[:, :, 0, :], omt[:, :, 1, :], ALU.mult, r=["omt"], w=["oml"])
        for g in range(4):
            for c in range(8):
                MM(psB[:, 512:1024], wt[:, c, 0:128], hT[:, c, CTX + g * 512:CTX + (g + 1) * 512], start=(c == 0), stop=(c == 7),
                   r=[wk] + [("hT", NCT + g * 4 + i) for i in range(4)], w=["psB1"])
            AMUL(qT[:, g * 512:(g + 1) * 512], psB[:, 512:1024], QSCALE, r=["psB1"], w=["qT"])
        import os
        if int(os.environ.get("HST", "9")) < 1:
            return
        for d in range(2):
            MSET("dve", t_scm[d][:], 0.0, w=["t_scm%d" % d])
            MSET("dve", St[:], 0.0, w=["St"])
            MSET("dve", Sbf[:], 0.0, w=["Sbf"])
            order = list(range(NT)) if d == 0 else [1, 0] + list(range(NT - 1, NCT - 1, -1))
            order = order[:int(os.environ.get("HTL%d" % d, "99"))]
            for t in order:
                scan_tile(kind, hh, buf, d, t, V, escale)

    def phase_C1(s):
        DMA("pool", wbh[:], wbh_d.rearrange("(c p) n -> p c n", p=128), "pw", w=["wbh"])
        DMA("pool", wbg[:], wbg_d.rearrange("(c p) n -> p c n", p=128), "pw", w=["wbg"])
        DMA("pool", wm[:], winv[:, :, 8224:8224 + 2 * D], "pw", w=["wm"])
        for lt in range(NLT):
            t = lt + NCT
            tok = slice(lt * 128, (lt + 1) * 128)
            for hf in range(2):
                fs = slice(hf * 512, (hf + 1) * 512)
                for c in range(8):
                    MM(psA[:, 0:512], yThg[:, c, tok], wbh[:, c, fs], start=(c == 0), stop=(c == 7), r=[("yThg", lt), "wbh"], w=["psA0"])
                for c in range(8):
                    MM(psA[:, 512:1024], yTgl[:, c, tok], wbg[:, c, fs], start=(c == 0), stop=(c == 7), r=[("yTgl", lt), "wbg"], w=["psA1"])
                for c in range(8):
                    MM(psB[:, 0:512], hT[:, c, t * 128:(t + 1) * 128], wm[:, c, fs], start=(c == 0), stop=(c == 7), r=[("hT", t), "wm"], w=["psB0"])
                for c in range(8):
                    MM(psB[:, 512:1024], hT[:, c, t * 128:(t + 1) * 128], wm[:, c, D + hf * 512:D + (hf + 1) * 512],
                       start=(c == 0), stop=(c == 7), r=[("hT", t), "wm"], w=["psB1"])
                for i, (pm, py, pmk, pyk) in enumerate(((psB[:, 0:512], psA[:, 0:512], "psB0", "psA0"),
                                                        (psB[:, 512:1024], psA[:, 512:1024], "psB1", "psA1"))):
                    ACT(c_e[i][:], pm, AF.Exp, r=[pmk], w=["c_e%d" % i], scale=-1.0)
                    TS("dve", c_e[i][:], c_e[i][:], 1.0, None, ALU.add, r=["c_e%d" % i], w=["c_e%d" % i])
                    RCP(c_e[i][:], c_e[i][:], r=["c_e%d" % i], w=["c_e%d" % i])
                    TT("dve", c_t[i][:], py, c_e[i][:], ALU.mult, r=[pyk, "c_e%d" % i], w=["c_t%d" % i])
                TT("dve", c_ym[:, fs], c_t[0][:], c_t[1][:], ALU.add, r=["c_t0", "c_t1"], w=["c_ym"])
            for c in range(8):
                TR(psE[:, c * 128:(c + 1) * 128], c_ym[:, c * 128:(c + 1) * 128], identB[:], r=["c_ym", "identB"], w=["psE"])
            ACP(yThg[:, :, tok], psE[:].rearrange("p (c t) -> p c t", c=8), r=["psE"], w=[("yThg", lt)])

    def peer_topk():
        for g in range(16):
            sg = sc[:, g * 128:(g + 1) * 128]
            S.op("dve", lambda e, g=g, sg=sg: e.max(out=v16[:, g, 0:8], in_=sg), ["sc"], [("v16", g)])
            S.op("dve", lambda e, g=g, sg=sg: e.max_index(out=i16[:, g, 0:8], in_max=v16[:, g, 0:8], in_values=sg), ["sc", ("v16", g)], [("i16", g)])
            S.op("dve", lambda e, g=g, sg=sg: e.match_replace(out=scw[:, 0:128], in_to_replace=v16[:, g, 0:8], in_values=sg, imm_value=NEG),
                 ["sc", ("v16", g)], ["scw"])
            S.op("dve", lambda e, g=g: e.max(out=v16[:, g, 8:16], in_=scw[:, 0:128]), ["scw"], [("v16b", g)])
            S.op("dve", lambda e, g=g: e.max_index(out=i16[:, g, 8:16], in_max=v16[:, g, 8:16], in_values=scw[:, 0:128]),
                 ["scw", ("v16b", g)], [("i16b", g)])
        allv = [("v16", g) for g in range(16)] + [("v16b", g) for g in range(16)]
        alli = [("i16", g) for g in range(16)] + [("i16b", g) for g in range(16)]
        v16v = v16[:].rearrange("p (h f) k -> p h f k", f=2)
        for h in range(8):
            TT("dve", cand[:, h, :].rearrange("p (a b) -> p a b", a=16),
               v16[:, 2 * h, :].unsqueeze(2).to_broadcast([128, 16, 16]),
               v16[:, 2 * h + 1, :].unsqueeze(1).to_broadcast([128, 16, 16]), ALU.add, r=allv, w=[("cand", h)])
        for h in range(8):
            ch = cand[:, h, :]
            S.op("dve", lambda e, h=h, ch=ch: e.max(out=ts[:, h, 0:8], in_=ch), [("cand", h)], [("ts", h)])
            S.op("dve", lambda e, h=h, ch=ch: e.max_index(out=pos[:, h, 0:8], in_max=ts[:, h, 0:8], in_values=ch), [("cand", h), ("ts", h)], [("pos", h)])
            S.op("dve", lambda e, h=h, ch=ch: e.match_replace(out=scw[:, 0:256], in_to_replace=ts[:, h, 0:8], in_values=ch, imm_value=NEG),
                 [("cand", h), ("ts", h)], ["scw"])
            S.op("dve", lambda e, h=h: e.max(out=ts[:, h, 8:16], in_=scw[:, 0:256]), ["scw"], [("tsb", h)])
            S.op("dve", lambda e, h=h: e.max_index(out=pos[:, h, 8:16], in_max=ts[:, h, 8:16], in_values=scw[:, 0:256]),
                 ["scw", ("tsb", h)], [("posb", h)])
        allts = [("ts", h) for h in range(8)] + [("tsb", h) for h in range(8)]
        allpos = [("pos", h) for h in range(8)] + [("posb", h) for h in range(8)]
        pex3 = pex[:].rearrange("p (h k) -> p h k", h=8)
        TT("dve", pex3, ts[:], ts[:, :, 0:1].to_broadcast([128, 8, 16]), ALU.subtract, r=allts, w=["pex"])
        ACT(pex[:], pex[:], AF.Exp, r=["pex"], w=["pex"])
        S.op("dve", lambda e: e.tensor_reduce(out=psm[:, 0:8], in_=pex3, axis=AX.X, op=ALU.add), ["pex"], ["psm"])
        RCP(psm[:, 8:16], psm[:, 0:8], r=["psm"], w=["psm"])
        TT("dve", pgate[:].rearrange("p (h k) -> p h k", h=8), pex3, psm[:, 8:16].unsqueeze(2).to_broadcast([128, 8, 16]),
           ALU.mult, r=["pex", "psm"], w=["pgate"])
        posi = pos[:].bitcast(I32)
        S.op("dve", lambda e: e.tensor_single_scalar(out=pa[:, 0], in_=posi, scalar=4, op=ALU.logical_shift_right), allpos, ["pa0"])
        S.op("dve", lambda e: e.tensor_single_scalar(out=pa[:, 1], in_=posi, scalar=15, op=ALU.bitwise_and), allpos, ["pa1"])
        CP("dve", paf[:], pa[:], r=["pa0", "pa1"], w=["paf"])
        CP("dve", i16f[:], i16[:], r=alli, w=["i16f"])
        i16fv = i16f[:].rearrange("p (h f) k -> p h f k", f=2)
        for f in range(2):
            for h in range(8):
                ohh = oh[:, h, :].rearrange("p (r a) -> p r a", r=16)
                TT("dve", ohh, paf[:, f, h, :].unsqueeze(2).to_broadcast([128, 16, 16]),
                   iota16[:].unsqueeze(1).to_broadcast([128, 16, 16]), ALU.is_equal, r=["paf", "iota16"], w=[("oh", h)])
                TT("dve", ohh, ohh, i16f[:, 2 * h + f, :].unsqueeze(1).to_broadcast([128, 16, 16]), ALU.mult,
                   r=[("oh", h), "i16f"], w=[("oh", h)])
            S.op("dve", lambda e, f=f: e.tensor_reduce(out=isel[:, f, :], in_=oh[:].rearrange("p h (r a) -> p (h r) a", r=16),
                                                       axis=AX.X, op=ALU.add), [("oh", h) for h in range(8)], [("isel", f)])
        STT(eidx[:], isel[:, 0, :], 128.0, isel[:, 1, :], ALU.mult, ALU.add, r=[("isel", 0), ("isel", 1)], w=["eidx"])

    gctr = [0]

    def phase_D(s):
        DMA("sp", wps[:], wps_d, "dw", r=["wps_d"], w=["wps"])
        DMA("pool", wo[:], wo_d.rearrange("(c p) n -> p c n", p=128), "pw", w=["wo"])
        DMA("sp", G1[:], modscr[s:s + 1, 2 * D:3 * D].partition_broadcast(128), "dw", r=["modscr"], w=["G1"])
        DMA("sp", B2[:], modscr[s:s + 1, 3 * D:4 * D].partition_broadcast(128), "dw", r=["modscr"], w=["B2"])
        DMA("sp", A2[:], modscr[s:s + 1, 4 * D:5 * D].partition_broadcast(128), "dw", r=["modscr"], w=["A2"])
        DMA("sp", G2[:], modscr[s:s + 1, 5 * D:6 * D].partition_broadcast(128), "dw", r=["modscr"], w=["G2"])
        DMA("sp", FG[:], gffn_d.partition_broadcast(128), "dw", w=["FG"])
        STT(A2[:], A2[:], 1.0, FG[:], ALU.add, ALU.mult, r=["A2", "FG"], w=["A2"])
        DMA("sp", FG[:], fg_d.partition_broadcast(128), "dw", r=["A2"], w=["FG"])
        for lt in range(NLT):
            tok = slice(lt * 128, (lt + 1) * 128)
            DMA("sp", dxin[:], x_d[s, tok, :], "dx", w=["dxin"])
            for hf in range(2):
                fs = slice(hf * 512, (hf + 1) * 512)
                for c in range(8):
                    MM(psA[:, fs], yThg[:, c, tok], wo[:, c, fs], start=(c == 0), stop=(c == 7), r=[("yThg", lt), "wo"], w=["psA%d" % hf])
                TT("dve", x1[:, fs], psA[:, fs], G1[:, fs], ALU.mult, r=["psA%d" % hf, "G1"], w=[("x1", hf)])
                TT("dve", x1[:, fs], x1[:, fs], dxin[:, fs], ALU.add, r=[("x1", hf), "dxin"], w=[("x1", hf)])
            x1k = [("x1", 0), ("x1", 1)]
            if "x1" in dbg_out:
                DMA("sp", dbg_out["x1"][lt * 128:(lt + 1) * 128, :], x1[:], "dbg", r=x1k)
            ACT(djunk[:], x1[:], AF.Square, r=x1k, w=["djunk", "ssD"], accum=small[:, 4:5])
            rstd_from_ss(small[:, 4:5], small[:, 5:6], D, ["ssD"], "rsD")
            STT(h2[:], x1[:], small[:, 5:6], A2[:], ALU.mult, ALU.mult, r=x1k + ["rsD", "A2"], w=["h2"])
            TT("dve", h2[:], h2[:], B2[:], ALU.add, r=["h2", "B2"], w=["h2"])
            for c in range(8):
                TR(psA[:, c * 128:(c + 1) * 128], h2[:, c * 128:(c + 1) * 128], identF[:], r=["h2", "identF"], w=["psA%d" % (c // 4)])
            ACP(h2T[:].rearrange("p c t -> p (c t)"), psA[:], r=["psA0", "psA1"], w=["h2T"])
            for q in range(4):
                pt = (psB, psC)[q // 2]
                pk = ("psB0", "psB1", "psC0", "psC1")[q]
                for c in range(8):
                    MM(pt[:, (q % 2) * 512:(q % 2 + 1) * 512], h2T[:, c, :], wps[:, c, q * 512:(q + 1) * 512], start=(c == 0), stop=(c == 7),
                       r=["h2T", "wps"], w=[pk])
                if q % 2 == 0:
                    ACP(sc[:, q * 512:(q + 1) * 512], pt[:, 0:512], r=[pk], w=["sc"])
                else:
                    CP("dve", sc[:, q * 512:(q + 1) * 512], pt[:, 512:1024], r=[pk], w=["sc"])
            peer_topk()
            for j in range(128):
                k = gctr[0] % NGB
                gctr[0] += 1
                S.dma("pool", lambda e, k=k, j=j: e.indirect_dma_start(out=gb[k][:], out_offset=None, in_=pub_d,
                                                                       in_offset=bass.IndirectOffsetOnAxis(ap=eidx[:, j:j + 1], axis=0)),
                      "gb%d" % (k % 8), ["eidx", "ptab"], [("gb", k)])
                STT(djunk[:], gb[k][:], 1.0, h2[:], ALU.mult, ALU.mult, r=[("gb", k), "h2"], w=["djunk", ("pact", j)], accum=pact[:, j:j + 1])
            ACT(pact[:], pact[:], AF.Gelu, r=[("pact", j) for j in range(128)], w=["pact"])
            TT("dve", pact[:], pact[:], pgate[:], ALU.mult, r=["pact", "pgate"], w=["pact"])
            for j in range(128):
                k = gctr[0] % NGB
                gctr[0] += 1
                S.dma("pool", lambda e, k=k, j=j: e.indirect_dma_start(out=gb[k][:], out_offset=None, in_=pvb_d,
                                                                       in_offset=bass.IndirectOffsetOnAxis(ap=eidx[:, j:j + 1], axis=0)),
                      "gb%d" % (k % 8), ["eidx", "ptab"], [("gb", k)])
                dk = j % NDG
                AMUL(dg[dk][:], identF[:], pact[:, j:j + 1], r=["identF", "pact"], w=[("dg", dk)])
                for hf in range(2):
                    MM(psA[:, hf * 512:(hf + 1) * 512], dg[dk][:], gb[k][:, hf * 512:(hf + 1) * 512], start=(j == 0), stop=(j == 127),
                       r=[("dg", dk), ("gb", k)], w=["psA%d" % hf])
            for hf in range(2):
                fs = slice(hf * 512, (hf + 1) * 512)
                TT("dve", acc[:, fs], psA[:, fs], G2[:, fs], ALU.mult, r=["psA%d" % hf, "G2", "acc"], w=["acc"])
                TT("dve", acc[:, fs], acc[:, fs], x1[:, fs], ALU.add, r=["acc", ("x1", hf)], w=["acc"])
            ACT(djunk[:], acc[:], AF.Square, r=["acc"], w=["djunk", "ssF"], accum=small[:, 6:7])
            rstd_from_ss(small[:, 6:7], small[:, 7:8], D, ["ssF"], "rsF")
            STT(h2[:], acc[:], small[:, 7:8], FG[:], ALU.mult, ALU.mult, r=["acc", "rsF", "FG"], w=["h2"])
            DMA("sp", out_d[s, tok, :], h2[:], "out", r=["h2"])

    def dump_mixer():
        S.barrier()
        if "yThg" in dbg_out:
            DMA("sp", dbg_out["yThg"], yThg[:], "dbg", r=[("yThg", lt) for lt in range(NLT)])
            DMA("sp", dbg_out["yTgl"], yTgl[:], "dbg", r=[("yTgl", lt) for lt in range(NLT)])
            DMA("sp", dbg_out["hT"], hT[:], "dbg", r=[("hT", t) for t in range(NT)])
        if "grT" in dbg_out:
            DMA("sp", dbg_out["grT"], grT[:], "dbg", r=["grT"])

    def finish():
        S.emit(final_wait_keys=["out"] + (["dbg"] if dbg_out else []))
        return nc

    for s in range(NS):
        load_seq_mod(s)
        phase_A(s)
        S.barrier()
        if stop == "A":
            dump_mixer()
            S.emit(final_wait_keys=["dbg"])
            return nc
        gr_prepass()
        S.barrier()
        if stop == "gr":
            dump_mixer()
            S.emit(final_wait_keys=["dbg"])
            return nc
        hi = 0
        heads = [("hg", h) for h in range(8)] + [("gl", h) for h in range(4)]
        if stop == "head0":
            heads = [("hg", 0)]
        if stop == "head8":
            heads = [("gl", 0)]
        load_head_w(heads[0][0], heads[0][1], 0)
        for hi, (kind, hh) in enumerate(heads):
            if hi + 1 < len(heads):
                load_head_w(heads[hi + 1][0], heads[hi + 1][1], (hi + 1) % 2)
            head(kind, hh, hi % 2, s)
        if s == 0 and (dbg_out or stop in ("head0", "head8", "heads")):
            dump_mixer()
        if stop in ("head0", "head8", "heads"):
            S.emit(final_wait_keys=["dbg"])
            return nc
        S.barrier()
        phase_C1(s)
        S.barrier()
        phase_D(s)
        S.barrier()
    S.emit(final_wait_keys=["out"] + (["dbg"] if dbg_out else []))
    return nc


_CACHE = {}


def _in_maps(inputs, NS, cores):
    f = lambda a: np.ascontiguousarray(a, dtype=np.float32)
    shared = {
        "c_ctx": f(inputs["c_ctx"]).reshape(1, D),
        "ada_w": f(inputs["ada_w"]).reshape(D, 6 * D),
        "ada_b": f(inputs["ada_b"]).reshape(1, 6 * D),
        "norm_mix_g": f(inputs["norm_mix_g"]).reshape(1, D),
        "w_in": f(inputs["w_in"]).reshape(D, D_IN),
        "lb_logits": f(inputs["hgrn_lb_logits"]).reshape(2, 2 * D),
        "hgrn_norm_g": f(inputs["hgrn_norm_g"]).reshape(1, 128),
        "gla_gk_w": f(inputs["gla_gk_w"]).reshape(2, 16, 512),
        "gla_gk_b": f(inputs["gla_gk_b"]).reshape(2, 512),
        "gla_norm_g": f(inputs["gla_norm_g"]).reshape(1, 256),
        "w_branch_hgrn": f(inputs["w_branch_hgrn"]).reshape(D, D),
        "w_branch_gla": f(inputs["w_branch_gla"]).reshape(D, D),
        "w_out": f(inputs["w_out"]).reshape(D, D),
        "norm_ffn_g": f(inputs["norm_ffn_g"]).reshape(1, D),
        "peer_wq": f(inputs["peer_wq"]).reshape(D, 2048),
        "peer_k1": f(inputs["peer_k1"]).reshape(128, 128),
        "peer_k2": f(inputs["peer_k2"]).reshape(128, 128),
        "peer_u": f(inputs["peer_u"]).reshape(16384, D),
        "peer_v": f(inputs["peer_v"]).reshape(16384, D),
        "final_g": f(inputs["final_g"]).reshape(1, D),
    }
    x = f(inputs["x"])
    ctx = f(inputs["ctx"])
    c = f(inputs["c"])
    maps = []
    for i in range(cores):
        m = dict(shared)
        m["x"] = x[i * NS:(i + 1) * NS]
        m["ctx"] = ctx[i * NS:(i + 1) * NS]
        m["c"] = c[i * NS:(i + 1) * NS]
        maps.append(m)
    return maps


def kernel(**inputs):
    B = inputs["x"].shape[0]
    NS = B // N_CORES
    if NS not in _CACHE:
        _CACHE[NS] = build_program(NS)
    nc = _CACHE[NS]
    maps = _in_maps(inputs, NS, N_CORES)
    res = run_bass_kernel_spmd(nc, maps, core_ids=list(range(N_CORES)))
    return np.concatenate([r["out"] for r in res.results], axis=0).astype(np.float32)
```

```python
import contextlib
import numpy as np
import concourse.bass as bass
import concourse.mybir as mybir
from concourse.bass_utils import run_bass_kernel_spmd

F32 = mybir.dt.float32
BF16 = mybir.dt.bfloat16
I32 = mybir.dt.int32
U32 = mybir.dt.uint32
AF = mybir.ActivationFunctionType
ALU = mybir.AluOpType
AX = mybir.AxisListType

N_CORES = 8
D = 1024
SEQ = 2048
CTX = 256
NLT = SEQ // 128
NCT = CTX // 128
NT = NLT + NCT
TOK = SEQ + CTX
D_IN = 10272
EPS = 1e-6
QSCALE = 128 ** -0.5
NEG = -1e30


class _Op:
    __slots__ = ("eng", "fn", "deps", "signal", "tok_sem", "tok_val", "is_dma", "dma_key")

    def __init__(self, eng, fn, is_dma=False, dma_key=None):
        self.eng = eng
        self.fn = fn
        self.deps = []
        self.signal = False
        self.tok_sem = None
        self.tok_val = 0
        self.is_dma = is_dma
        self.dma_key = dma_key


class Sched:
    ENGS = ("pe", "act", "dve", "pool", "sp")

    def __init__(self, nc):
        self.nc = nc
        self.ops = {e: [] for e in self.ENGS}
        self.last_writer = {}
        self.readers = {}
        self.last_eng = {}
        self.last_key = {}
        self.bar_deps = []
        self.bar_need = set()

    def barrier(self):
        self.bar_deps = list(self.last_eng.values()) + list(self.last_key.values())
        self.bar_need = set(self.ENGS)

    def _add(self, op, reads, writes):
        excl = [k for k in reads if isinstance(k, str) and k.startswith("ps")]
        if excl:
            reads = [k for k in reads if k not in excl]
            writes = list(writes) + [k for k in excl if k not in writes]
        deps = []
        if op.eng in self.bar_need:
            deps.extend(self.bar_deps)
            self.bar_need.discard(op.eng)
        for r in reads:
            w = self.last_writer.get(r)
            if w is not None:
                deps.append(w)
        for wkey in writes:
            w = self.last_writer.get(wkey)
            if w is not None:
                deps.append(w)
            deps.extend(self.readers.get(wkey, ()))
        seen = set()
        for d in deps:
            if d is op or id(d) in seen:
                continue
            seen.add(id(d))
            if (not d.is_dma) and (not op.is_dma) and d.eng == op.eng and op.eng == "pe":
                continue
            op.deps.append(d)
            d.signal = True
        for r in reads:
            self.readers.setdefault(r, []).append(op)
        for wkey in writes:
            self.last_writer[wkey] = op
            self.readers[wkey] = []
        self.ops[op.eng].append(op)
        if op.is_dma:
            self.last_key[op.dma_key] = op
        else:
            self.last_eng[op.eng] = op
        return op

    def op(self, eng, fn, reads=(), writes=()):
        return self._add(_Op(eng, fn), list(reads), list(writes))

    def dma(self, eng, fn, key, reads=(), writes=()):
        o = _Op(eng, fn, is_dma=True, dma_key=key)
        o.signal = True
        return self._add(o, list(reads), list(writes))

    def emit(self, final_wait_keys=()):
        nc = self.nc
        dma_keys = []
        for e in self.ENGS:
            for o in self.ops[e]:
                if o.is_dma and o.dma_key not in dma_keys:
                    dma_keys.append(o.dma_key)
        with contextlib.ExitStack() as st:
            eng_sem = {e: st.enter_context(nc.semaphore("S_" + e)) for e in self.ENGS}
            key_sem = {k: st.enter_context(nc.semaphore("D_%d" % i)) for i, k in enumerate(dma_keys)}
            key_cnt = {k: 0 for k in dma_keys}
            key_eng = {}
            for e in self.ENGS:
                cnt = 0
                for o in self.ops[e]:
                    if o.is_dma:
                        assert key_eng.setdefault(o.dma_key, e) == e, "dma key used from two queues"
                        key_cnt[o.dma_key] += 16
                        o.tok_sem = key_sem[o.dma_key]
                        o.tok_val = key_cnt[o.dma_key]
                    elif o.signal:
                        cnt += 1
                        o.tok_sem = eng_sem[e]
                        o.tok_val = cnt
            blk = st.enter_context(nc.Block())
            engobj = {"pe": "tensor", "act": "scalar", "dve": "vector", "pool": "gpsimd", "sp": "sync"}

            self.stats = {}

            def make(e):
                def body(eng):
                    waited = {}
                    nw = 0
                    for o in self.ops[e]:
                        for d in o.deps:
                            s = d.tok_sem
                            if waited.get(id(s), 0) >= d.tok_val:
                                continue
                            waited[id(s)] = d.tok_val
                            eng.wait_ge(s, d.tok_val)
                            nw += 1
                        ins = o.fn(eng)
                        if o.is_dma:
                            ins.then_inc(o.tok_sem, 16)
                        elif o.signal:
                            ins.then_inc(o.tok_sem, 1)
                    self.stats[e] = (len(self.ops[e]), nw)
                    if e == "sp":
                        for k in final_wait_keys:
                            if k in key_sem:
                                eng.wait_ge(key_sem[k], key_cnt[k])
                return body

            for e in self.ENGS:
                getattr(blk, engobj[e])(make(e))


def build_program(NS, dbg=(), stop=None):
    nc = bass.Bass("TRN2", target_bir_lowering=False)

    def din(name, shape, dt=F32):
        return nc.dram_tensor(name, list(shape), dt, kind="ExternalInput").ap()

    x_d = din("x", [NS, SEQ, D])
    ctx_d = din("ctx", [NS, CTX, D])
    c_d = din("c", [NS, D])
    cctx_d = din("c_ctx", [1, D])
    adaw_d = din("ada_w", [D, 6 * D])
    adab_d = din("ada_b", [1, 6 * D])
    gmix_d = din("norm_mix_g", [1, D])
    win_d = din("w_in", [D, D_IN])
    lbl_d = din("lb_logits", [2, 2 * D])
    hgg_d = din("hgrn_norm_g", [1, 128])
    gkw_d = din("gla_gk_w", [2, 16, 512])
    gkb_d = din("gla_gk_b", [2, 512])
    glg_d = din("gla_norm_g", [1, 256])
    wbh_d = din("w_branch_hgrn", [D, D])
    wbg_d = din("w_branch_gla", [D, D])
    wo_d = din("w_out", [D, D])
    gffn_d = din("norm_ffn_g", [1, D])
    wq_d = din("peer_wq", [D, 2048])
    k1_d = din("peer_k1", [128, 128])
    k2_d = din("peer_k2", [128, 128])
    pu_d = din("peer_u", [16384, D])
    pv_d = din("peer_v", [16384, D])
    fg_d = din("final_g", [1, D])
    out_d = nc.dram_tensor("out", [NS, SEQ, D], F32, kind="ExternalOutput").ap()
    modscr = nc.dram_tensor("modscr", [NS + 1, 6 * D], F32, kind="Internal").ap()
    wps_d = nc.dram_tensor("wps", [128, 8, 2048], BF16, kind="Internal").ap()
    puv_d = nc.dram_tensor("puv_bf", [16384, 2 * D], BF16, kind="Internal").ap()
    dbg_out = {}
    for name, shape, dt in dbg:
        dbg_out[name] = nc.dram_tensor("dbg_" + name, list(shape), dt, kind="ExternalOutput").ap()

    ARENA = 212000
    arena = nc.alloc_sbuf_tensor("arena", [128, ARENA // 4], F32)
    base = nc.lookup_mloc(arena).addr
    cur = {"p": base, "limit": base + ARENA}

    def _sz(shape, dt):
        n = int(np.prod(shape[1:])) * (2 if dt == BF16 else 4)
        return (n + 63) // 64 * 64

    def alloc(name, shape, dt, at=None):
        n = _sz(shape, dt)
        if at is None:
            off = cur["p"]
            cur["p"] += n
            assert cur["p"] <= cur["limit"], ("SBUF overflow", name, cur["p"] - base)
        else:
            off = at[0]
            at[0] += n
            assert at[0] <= cur["limit"], ("SBUF phase overflow", name, at[0] - base)
        return nc.alloc_sbuf_tensor_at(name, list(shape), dt, offset=off)

    identF = alloc("identF", [128, 128], F32)
    identB = alloc("identB", [128, 128], BF16)
    TRI = [alloc("TRIf", [128, 128], F32), alloc("TRIb", [128, 128], F32)]
    M1 = [alloc("M1f", [128, 128], F32), alloc("M1b", [128, 128], F32)]
    UU = [alloc("Uf", [128, 128], F32), alloc("Ub", [128, 128], F32)]
    hgg_bc = alloc("hgg_bc", [128, 128], F32)
    glg_bc = alloc("glg_bc", [128, 256], F32)
    modsm = alloc("modsm", [128, 6, 8], F32)
    small = alloc("small", [128, 16], F32)
    yThg = alloc("yThg", [128, 8, SEQ], BF16)
    wgr = alloc("wgr", [128, 8, 64], BF16)
    gkw = alloc("gkw", [64, 512], F32)
    grT = alloc("grT", [64, TOK], F32)
    iota16 = alloc("iota16", [128, 16], F32)
    PH = cur["p"]

    a = [PH]
    hT = alloc("hT", [128, 8, TOK], BF16, a)
    yTgl = alloc("yTgl", [128, 8, SEQ], BF16, a)
    MX = a[0]
    a = [MX]
    qT = alloc("qT", [128, SEQ], F32, a)
    vst = alloc("vst", [128, NT, 256], BF16, a)
    ofw = alloc("ofw", [128, NLT, 256], F32, a)
    whd = [alloc("whd0", [128, 8, 768], BF16, a), alloc("whd1", [128, 8, 768], BF16, a)]
    oml = alloc("oml", [128, 2, 128], F32, a)
    omt = alloc("omt", [128, 2, 2, 128], F32, a)
    t_e = alloc("t_e", [128, 256], F32, a)
    t_r = alloc("t_r", [128, 256], F32, a)
    t_k = alloc("t_k", [128, 128], F32, a)
    t_f = alloc("t_f", [128, 128], F32, a)
    t_lg = alloc("t_lg", [128, 128], F32, a)
    t_P = alloc("t_P", [128, 4, 128], F32, a)
    t_qd = alloc("t_qd", [128, 128], BF16, a)
    t_kdT = alloc("t_kdT", [128, 128], BF16, a)
    t_qdec = alloc("t_qdec", [128, 128], BF16, a)
    t_kdec = alloc("t_kdec", [128, 128], BF16, a)
    t_scm = [alloc("t_scm0", [128, 128], BF16, a), alloc("t_scm1", [128, 128], BF16, a)]
    St = alloc("St", [128, 256], F32, a)
    Sbf = alloc("Sbf", [128, 256], BF16, a)
    t_o = alloc("t_o", [128, 256], F32, a)
    t_gate = alloc("t_gate", [128, 256], F32, a)
    t_junk = alloc("t_junk", [128, 256], F32, a)
    t_y = alloc("t_y", [128, 256], BF16, a)
    a = [MX]
    xin = [alloc("xin0", [128, D], F32, a), alloc("xin1", [128, D], F32, a)]
    xs = alloc("xs", [128, D], F32, a)
    a = [MX]
    wbh = alloc("wbh", [128, 8, D], BF16, a)
    wbg = alloc("wbg", [128, 8, D], BF16, a)
    wm = alloc("wm", [128, 8, 2 * D], BF16, a)
    c_e = [alloc("c_e0", [128, 512], F32, a), alloc("c_e1", [128, 512], F32, a)]
    c_t = [alloc("c_t0", [128, 512], F32, a), alloc("c_t1", [128, 512], F32, a)]
    c_ym = alloc("c_ym", [128, D], BF16, a)
    a = [PH]
    s_l = alloc("s_l", [128, 2, 2 * D], F32, a)
    adaw = [alloc("adaw0", [128, 8, 512], F32, a), alloc("adaw1", [128, 8, 512], F32, a)]
    cT = alloc("cT", [128, 8, NS + 1], F32, a)
    scT = alloc("scT", [128, 8, NS + 1], F32, a)
    s_tmp = alloc("s_tmp", [128, 8, NS + 1], F32, a)
    modrows = alloc("modrows", [NS + 1, 6 * D], F32, a)
    adab_sb = alloc("adab_sb", [NS + 1, 6 * D], F32, a)
    wqb = [alloc("wqb0", [128, 8, 128], F32, a), alloc("wqb1", [128, 8, 128], F32, a)]
    kraw = alloc("kraw", [128, 2, 128], F32, a)
    kT = alloc("kT", [128, 2, 128], F32, a)
    wqT = [alloc("wqT0", [128, 128], F32, a), alloc("wqT1", [128, 128], F32, a)]
    wpb = [alloc("wpb0", [128, 8, 128], BF16, a), alloc("wpb1", [128, 8, 128], BF16, a)]
    a = [PH]
    cst_f = [alloc("cst_f0", [128, 4096], F32, a), alloc("cst_f1", [128, 4096], F32, a)]
    cst_b = [alloc("cst_b0", [128, 4096], BF16, a), alloc("cst_b1", [128, 4096], BF16, a)]
    a = [PH]
    wps = alloc("wps_sb", [128, 8, 2048], BF16, a)
    wo = alloc("wo", [128, 8, D], BF16, a)
    G1 = alloc("G1", [128, D], F32, a)
    A2 = alloc("A2", [128, D], F32, a)
    B2 = alloc("B2", [128, D], F32, a)
    G2 = alloc("G2", [128, D], F32, a)
    FG = alloc("FG", [128, D], F32, a)
    NGB = 14
    gb = [alloc("gb%d" % i, [128, 2 * D], BF16, a) for i in range(8)]
    NDG = 4
    dg = [alloc("dg%d" % i, [128, 128], BF16, a) for i in range(NDG)]
    sc = alloc("sc", [128, 2048], F32, a)
    scw = alloc("scw", [128, 256], F32, a)
    dxin = alloc("dxin", [128, D], F32, a)
    x1 = alloc("x1", [128, D], F32, a)
    h2 = alloc("h2", [128, D], F32, a)
    h2T = alloc("h2T", [128, 8, 128], BF16, a)
    acc = alloc("acc", [128, D], F32, a)
    djunk = alloc("djunk", [128, D], F32, a)
    v16 = alloc("v16", [128, 16, 16], F32, a)
    i16 = alloc("i16", [128, 16, 16], U32, a)
    i16f = alloc("i16f", [128, 16, 16], F32, a)
    cand = alloc("cand", [128, 8, 256], F32, a)
    ts = alloc("ts", [128, 8, 16], F32, a)
    pos = alloc("pos", [128, 8, 16], U32, a)
    pa = alloc("pa", [128, 2, 8, 16], I32, a)
    paf = alloc("paf", [128, 2, 8, 16], F32, a)
    oh = alloc("oh", [128, 8, 256], F32, a)
    isel = alloc("isel", [128, 2, 128], F32, a)
    eidx = alloc("eidx", [128, 128], I32, a)
    pgate = alloc("pgate", [128, 128], F32, a)
    pact = alloc("pact", [128, 128], F32, a)
    pex = alloc("pex", [128, 128], F32, a)
    psm = alloc("psm", [128, 16], F32, a)
    for t_ in (sc, cand, oh):
        o_ = [nc.lookup_mloc(t_).addr] if False else None
    def _off(t_):
        return t_.manual_sbuf_range[0]
    for t_ in (sc, cand, oh):
        aa = [_off(t_)]
        gb.append(alloc("gb%d" % len(gb), [128, 2 * D], BF16, aa))
        gb.append(alloc("gb%d" % len(gb), [128, 2 * D], BF16, aa))

    psA = nc.alloc_psum_tensor("psA", [128, 1024], F32)
    psB = nc.alloc_psum_tensor("psB", [128, 1024], F32)
    psC = nc.alloc_psum_tensor("psC", [128, 1024], F32)
    psD = nc.alloc_psum_tensor("psD", [128, 512], F32)
    psE = nc.alloc_psum_tensor("psE", [128, 1024], BF16)

    S = Sched(nc)

    def MM(out, lhsT, rhs, start=True, stop=True, r=(), w=()):
        S.op("pe", lambda e: e.matmul(out, lhsT=lhsT, rhs=rhs, start=start, stop=stop), r, w)

    def TR(out, in_, ident, r=(), w=()):
        S.op("pe", lambda e: e.transpose(out=out, in_=in_, identity=ident), r, w)

    def ACT(out, in_, func, r=(), w=(), bias=0.0, scale=1.0, accum=None):
        if accum is None:
            S.op("act", lambda e: e.activation(out=out, in_=in_, func=func, bias=bias, scale=scale), r, w)
        else:
            S.op("act", lambda e: e.activation(out=out, in_=in_, func=func, bias=bias, scale=scale, accum_out=accum), r, w)

    def ACP(out, in_, r=(), w=()):
        S.op("act", lambda e: e.copy(out=out, in_=in_), r, w)

    def AMUL(out, in_, mul, r=(), w=()):
        S.op("act", lambda e: e.mul(out=out, in_=in_, mul=mul), r, w)

    def TT(eng, out, in0, in1, op, r=(), w=()):
        S.op(eng, lambda e: e.tensor_tensor(out=out, in0=in0, in1=in1, op=op), r, w)

    def TS(eng, out, in0, s1, s2, op0, op1=None, r=(), w=()):
        if op1 is None:
            S.op(eng, lambda e: e.tensor_scalar(out=out, in0=in0, scalar1=s1, scalar2=None, op0=op0), r, w)
        else:
            S.op(eng, lambda e: e.tensor_scalar(out=out, in0=in0, scalar1=s1, scalar2=s2, op0=op0, op1=op1), r, w)

    def STT(out, in0, scalar, in1, op0, op1, r=(), w=(), accum=None):
        if accum is None:
            S.op("dve", lambda e: e.scalar_tensor_tensor(out=out, in0=in0, scalar=scalar, in1=in1, op0=op0, op1=op1), r, w)
        else:
            S.op("dve", lambda e: e.scalar_tensor_tensor(out=out, in0=in0, scalar=scalar, in1=in1, op0=op0, op1=op1, accum_out=accum), r, w)

    def CP(eng, out, in_, r=(), w=()):
        S.op(eng, lambda e: e.tensor_copy(out=out, in_=in_), r, w)

    def RCP(out, in_, r=(), w=()):
        S.op("dve", lambda e: e.reciprocal(out=out, in_=in_), r, w)

    def MSET(eng, ap, val, w=()):
        S.op(eng, lambda e: e.memset(ap, val), (), w)

    def DMA(eng, out, in_, key, r=(), w=(), slow=False):
        if slow:
            S.dma(eng, lambda e: e.dma_start(out=out, in_=in_, allow_slow_non_contiguous=True), key, r, w)
        else:
            S.dma(eng, lambda e: e.dma_start(out=out, in_=in_), key, r, w)

    def ASEL(out, pattern, cmp, base_, cm, r=(), w=()):
        S.op("pool", lambda e: e.affine_select(out=out, in_=out, pattern=pattern, compare_op=cmp, fill=0.0,
                                               base=base_, channel_multiplier=cm), r, w)

    def rstd_from_ss(ss, out, n, keyr, keyw):
        ACT(out, ss, AF.Ln, r=keyr, w=[keyw], bias=EPS, scale=1.0 / n)
        ACT(out, out, AF.Exp, r=[keyw], w=[keyw], scale=-0.5)

    def tri_const(t, pattern, cmp, base_, cm, key):
        MSET("pool", t[:], 1.0, w=[key])
        ASEL(t[:], pattern, cmp, base_, cm, r=[key], w=[key])

    tri_const(identF, [[-1, 128]], ALU.is_equal, 0, 1, "identF")
    CP("dve", identB[:], identF[:], r=["identF"], w=["identB"])
    tri_const(TRI[0], [[1, 128]], ALU.is_ge, 0, -1, "TRI0")
    tri_const(TRI[1], [[-1, 128]], ALU.is_ge, 0, 1, "TRI1")
    tri_const(UU[0], [[-1, 128]], ALU.is_gt, 0, 1, "UU0")
    tri_const(UU[1], [[1, 128]], ALU.is_gt, 0, -1, "UU1")
    tri_const(M1[0], [[0, 128]], ALU.is_ge, 64, -1, "M10")
    tri_const(M1[1], [[0, 128]], ALU.is_ge, -63, 1, "M11")
    TT("dve", M1[0][:], TRI[0][:], M1[0][:], ALU.subtract, r=["TRI0", "M10"], w=["M10"])
    TT("dve", M1[1][:], TRI[1][:], M1[1][:], ALU.subtract, r=["TRI1", "M11"], w=["M11"])
    S.op("pool", lambda e: e.iota(iota16[:], pattern=[[1, 16]], base=0, channel_multiplier=0,
                                  allow_small_or_imprecise_dtypes=True), (), ["iota16"])
    if stop == "s1":
        S.emit(final_wait_keys=[])
        return nc
    DMA("sp", hgg_bc[:], hgg_d.partition_broadcast(128), "su", w=["hgg_bc"])
    DMA("sp", glg_bc[:], glg_d.partition_broadcast(128), "su", w=["glg_bc"])
    MSET("pool", wgr[:], 0.0, w=["wgr"])
    winv = win_d.rearrange("(c p) n -> p c n", p=128)
    DMA("pool", wgr[:, :, 0:16], winv[:, :, 8192:8208], "pw", r=["wgr"], w=["wgr"])
    DMA("pool", wgr[:, :, 32:48], winv[:, :, 8208:8224], "pw", r=["wgr"], w=["wgr"])
    MSET("dve", grT[:], 1.0, w=["grT"])
    MSET("dve", gkw[:], 0.0, w=["gkw"])
    DMA("sp", gkw[0:16, :], gkw_d[0], "su", r=["gkw"], w=["gkw"])
    DMA("sp", gkw[16:17, :], gkb_d[0:1, :], "su", w=["gkw"])
    DMA("sp", gkw[32:48, :], gkw_d[1], "su", w=["gkw"])
    DMA("sp", gkw[48:49, :], gkb_d[1:2, :], "su", w=["gkw"])
    if stop == "s2":
        S.emit(final_wait_keys=[])
        return nc
    for b_ in range(NS):
        DMA("sp", cT[:, :, b_:b_ + 1], c_d[b_:b_ + 1, :].rearrange("b (c p) -> p c b", p=128), "su", w=["cT"], slow=True)
    DMA("sp", cT[:, :, NS:NS + 1], cctx_d.rearrange("b (c p) -> p c b", p=128), "su", w=["cT"], slow=True)
    DMA("sp", adab_sb[:], adab_d.partition_broadcast(NS + 1), "su", w=["adab"])
    ACT(s_tmp[:], cT[:], AF.Exp, r=["cT"], w=["s_tmp"], scale=-1.0)
    TS("dve", s_tmp[:], s_tmp[:], 1.0, None, ALU.add, r=["s_tmp"], w=["s_tmp"])
    RCP(s_tmp[:], s_tmp[:], r=["s_tmp"], w=["s_tmp"])
    TT("dve", scT[:], cT[:], s_tmp[:], ALU.mult, r=["cT", "s_tmp"], w=["scT"])
    if stop == "s3":
        S.emit(final_wait_keys=[])
        return nc
    adawv = adaw_d.rearrange("(c p) n -> p c n", p=128)
    for n in range(12):
        ab = adaw[n % 2]
        DMA("sp", ab[:], adawv[:, :, n * 512:(n + 1) * 512], "aw", w=["adaw%d" % (n % 2)])
        for c in range(8):
            MM(psA[0:NS + 1, 0:512], scT[:, c, :], ab[:, c, :], start=(c == 0), stop=(c == 7),
               r=["scT", "adaw%d" % (n % 2)], w=["psA0"])
        TT("dve", modrows[:, n * 512:(n + 1) * 512], psA[0:NS + 1, 0:512], adab_sb[:, n * 512:(n + 1) * 512],
           ALU.add, r=["psA0", "adab"], w=["modrows"])
    if stop == "s4":
        S.emit(final_wait_keys=[])
        return nc
    DMA("sp", modscr, modrows[:], "ms", r=["modrows"], w=["modscr"])
    DMA("sp", modsm[:, 0, :], gmix_d.rearrange("o (c p) -> p (o c)", p=128), "ms2", w=["gmixT"], slow=True)
    DMA("sp", modsm[:, 2, :], modscr[NS:NS + 1, 0:D].rearrange("o (c p) -> p (o c)", p=128), "ms2",
        r=["modscr"], w=["B1Tc"], slow=True)
    DMA("sp", modsm[:, 5, :], modscr[NS:NS + 1, D:2 * D].rearrange("o (c p) -> p (o c)", p=128), "ms2",
        r=["modscr"], w=["mtmp"], slow=True)
    STT(modsm[:, 1, :], modsm[:, 5, :], 1.0, modsm[:, 0, :], ALU.add, ALU.mult, r=["mtmp", "gmixT"], w=["A1Tc"])
    if stop == "s5":
        S.emit(final_wait_keys=[])
        return nc
    DMA("sp", kraw[:, 0, :], k1_d, "su", w=["kraw"])
    DMA("sp", kraw[:, 1, :], k2_d, "su", w=["kraw"])
    for hf in range(2):
        TR(psB[:, hf * 128:(hf + 1) * 128], kraw[:, hf, :], identF[:], r=["kraw", "identF"], w=["psB0"])
    CP("dve", kT[:].rearrange("p a b -> p (a b)"), psB[:, 0:256], r=["psB0"], w=["kT"])
    wqv = wq_d.rearrange("(c p) n -> p c n", p=128)
    if stop == "s6":
        S.emit(final_wait_keys=[])
        return nc
    import os
    WST = int(os.environ.get("WST", "9"))
    for g in range(16 if stop != "s7" else 1):
        wb_ = wqb[g % 2]
        DMA("sp", wb_[:], wqv[:, :, g * 128:(g + 1) * 128], "wq", w=["wqb%d" % (g % 2)])
        for c in range(8):
            i = (g * 8 + c) % 2
            TR(psC[:, i * 512:i * 512 + 128], wb_[:, c, :], identF[:], r=["wqb%d" % (g % 2), "identF"], w=["psC%d" % i])
            if i == 0:
                CP("dve", wqT[i][:], psC[:, i * 512:i * 512 + 128], r=["psC%d" % i], w=["wqT%d" % i])
            else:
                ACP(wqT[i][:], psC[:, i * 512:i * 512 + 128], r=["psC%d" % i], w=["wqT%d" % i])
            MM(psA[:, i * 512:i * 512 + 128], wqT[i][:], kT[:, g % 2, :], r=["wqT%d" % i, "kT"], w=["psA%d" % i])
            CP("dve", wpb[g % 2][:, c, :], psA[:, i * 512:i * 512 + 128], r=["psA%d" % i], w=["wpb%d" % (g % 2)])
        DMA("sp", wps_d[:, :, g * 128:(g + 1) * 128], wpb[g % 2][:], "wpo", r=["wpb%d" % (g % 2)], w=["wps_d"])
    S.barrier()
    if stop == "s7":
        S.emit(final_wait_keys=[])
        return nc
    ci = 0
    for ti, src_t in enumerate((pu_d, pv_d)):
        sv = src_t.rearrange("(n p r) d -> n p (r d)", p=128, r=4)
        dv = puv_d[:, ti * D:(ti + 1) * D].rearrange("(n p r) d -> n p r d", p=128, r=4)
        for n in range(32):
            i = ci % 2
            ci += 1
            DMA("sp", cst_f[i][:], sv[n], "cin", w=["cst_f%d" % i])
            if i == 0:
                ACP(cst_b[i][:], cst_f[i][:], r=["cst_f%d" % i], w=["cst_b%d" % i])
            else:
                CP("dve", cst_b[i][:], cst_f[i][:], r=["cst_f%d" % i], w=["cst_b%d" % i])
            DMA("sp", dv[n], cst_b[i][:].rearrange("p (r d) -> p r d", r=4), "cout", r=["cst_b%d" % i], w=["ptab"])
    S.barrier()
    if stop == "setup":
        if "modrows" in dbg_out:
            DMA("sp", dbg_out["modrows"], modscr, "dbg", r=["modscr"])
            DMA("sp", dbg_out["wps"], wps_d, "dbg", r=["wps_d"])
        S.emit(final_wait_keys=["dbg"] if dbg_out else [])
        return nc

    def phase_A(s):
        for t in range(NT):
            xb = xin[t % 2]
            xk = "xin%d" % (t % 2)
            src = ctx_d[s, t * 128:(t + 1) * 128, :] if t < NCT else x_d[s, (t - NCT) * 128:(t - NCT + 1) * 128, :]
            DMA("sp", xb[:], src, "xin", w=[xk])
            ACT(xs[:], xb[:], AF.Square, r=[xk], w=["xs", "ssA"], accum=small[:, 0:1])
            rstd_from_ss(small[:, 0:1], small[:, 1:2], D, ["ssA"], "rsA")
            AMUL(xs[:], xb[:], small[:, 1:2], r=[xk, "rsA"], w=["xs"])
            for c in range(8):
                TR(psA[:, c * 128:(c + 1) * 128], xs[:, c * 128:(c + 1) * 128], identF[:], r=["xs", "identF"], w=["psA%d" % (c // 4)])
            ai, bi = (1, 2) if t < NCT else (3, 4)
            an, bn = ("A1Tc", "B1Tc") if t < NCT else ("A1Ts", "B1Ts")
            for c in range(8):
                TS("dve", hT[:, c, t * 128:(t + 1) * 128], psA[:, c * 128:(c + 1) * 128], modsm[:, ai, c:c + 1], modsm[:, bi, c:c + 1],
                   ALU.mult, ALU.add, r=["psA%d" % (c // 4), an, bn], w=[("hT", t)])

    def load_seq_mod(s):
        DMA("sp", modsm[:, 4, :], modscr[s:s + 1, 0:D].rearrange("o (c p) -> p (o c)", p=128), "ms2",
            r=["modscr"], w=["B1Ts"], slow=True)
        DMA("sp", modsm[:, 5, :], modscr[s:s + 1, D:2 * D].rearrange("o (c p) -> p (o c)", p=128), "ms2",
            r=["modscr"], w=["mtmp"], slow=True)
        STT(modsm[:, 3, :], modsm[:, 5, :], 1.0, modsm[:, 0, :], ALU.add, ALU.mult, r=["mtmp", "gmixT"], w=["A1Ts"])

    def gr_prepass():
        for g0 in range(0, TOK, 512):
            n = min(512, TOK - g0)
            for c in range(8):
                MM(psC[0:64, 512:512 + n], wgr[:, c, :], hT[:, c, g0:g0 + n], start=(c == 0), stop=(c == 7),
                   r=["wgr"] + [("hT", t) for t in range(g0 // 128, (g0 + n) // 128)], w=["psC1"])
            CP("dve", grT[0:16, g0:g0 + n], psC[0:16, 512:512 + n], r=["psC1"], w=["grT"])
            ACP(grT[32:48, g0:g0 + n], psC[32:48, 512:512 + n], r=["psC1"], w=["grT"])

    def load_head_w(kind, hh, buf):
        wt = whd[buf]
        k = "whd%d" % buf
        if kind == "hg":
            cols = [(hh * 128, 128), (D + hh * 128, 128), (3 * D + hh * 128, 128), (2 * D + hh * 128, 128), (4 * D + hh * 128, 128)]
        else:
            cols = [(5120 + hh * 128, 128), (6144 + hh * 256, 256), (5632 + hh * 128, 128), (7168 + hh * 256, 256)]
        o = 0
        for c0, n in cols:
            DMA("pool", wt[:, :, o:o + n], winv[:, :, c0:c0 + n], "pw", w=[k])
            o += n

    def scan_tile(kind, hh, buf, d, t, V, escale):
        import os
        wt = whd[buf]
        wk = "whd%d" % buf
        is_ctx = t < NCT
        lt = t - NCT
        tok = slice(t * 128, (t + 1) * 128)
        if kind == "hg":
            c0, ncol = (128, 256) if d == 0 else (384, 256)
        else:
            c0, ncol = (128, 384) if d == 0 else (384, 384)
        for c in range(8):
            MM(psB[:, 0:ncol], hT[:, c, tok], wt[:, c, c0:c0 + ncol], start=(c == 0), stop=(c == 7),
               r=[("hT", t), wk], w=["psB0"])
        if kind == "hg":
            z = psB[:, 0:128]
            ACT(t_e[:, 0:128], z, AF.Exp, r=["psB0"], w=["t_e"], scale=-1.0)
            TS("dve", t_e[:, 0:128], t_e[:, 0:128], 1.0, None, ALU.add, r=["t_e"], w=["t_e"])
            RCP(t_r[:, 0:128], t_e[:, 0:128], r=["t_e"], w=["t_r"])
            TS("dve", t_r[:, 0:128], t_r[:, 0:128], -1.0, 1.0, ALU.mult, ALU.add, r=["t_r"], w=["t_r"])
            TT("dve", t_k[:], t_r[:, 0:128], oml[:, d, :], ALU.mult, r=["t_r", "oml"], w=["t_k"])
            TS("dve", t_f[:], t_k[:], -1.0, 1.0, ALU.mult, ALU.add, r=["t_k"], w=["t_f"])
            ACT(t_lg[:], t_f[:], AF.Ln, r=["t_f"], w=["t_lg"])
            if d == 0 and not (os.environ.get("NOV17") and t == 17):
                ACP(vst[:, t, 0:128], psB[:, 128:256], r=["psB0"], w=[("vst", t)])
            gsrc = psB[:, 128:256]
        else:
            if d == 0:
                ACP(vst[:, t, :], psB[:, 0:256], r=["psB0"], w=[("vst", t)])
                CP("dve", t_k[:], psB[:, 256:384], r=["psB0"], w=["t_k"])
            else:
                CP("dve", t_k[:], psB[:, 0:128], r=["psB0"], w=["t_k"])
            gsrc = psB[:, 128:384]
            pb = 32 * d
            MM(psC[:, 640:768], grT[pb:pb + 32, tok], gkw[pb:pb + 32, hh * 128:(hh + 1) * 128], r=["grT", "gkw"], w=["psC1"])
            ACT(t_e[:, 0:128], psC[:, 640:768], AF.Exp, r=["psC1"], w=["t_e"], scale=-1.0)
            ACT(t_lg[:], t_e[:, 0:128], AF.Ln, r=["t_e"], w=["t_lg"], bias=1.0)
        if d == 1 and not is_ctx:
            ACT(t_e[:, 0:V], gsrc, AF.Exp, r=["psB0"], w=["t_e"], scale=-1.0)
            TS("dve", t_e[:, 0:V], t_e[:, 0:V], 1.0, None, ALU.add, r=["t_e"], w=["t_e"])
            RCP(t_r[:, 0:V], t_e[:, 0:V], r=["t_e"], w=["t_r"])
            TT("dve", t_gate[:, 0:V], gsrc, t_r[:, 0:V], ALU.mult, r=["psB0", "t_r"], w=["t_gate"])
            gbc = hgg_bc if kind == "hg" else glg_bc
            TT("dve", t_gate[:, 0:V], t_gate[:, 0:V], gbc[:, 0:V], ALU.mult, r=["t_gate", "hgg_bc", "glg_bc"], w=["t_gate"])
        HST = int(os.environ.get("HST", "9"))
        if HST < 2:
            return
        HSK = os.environ.get("HSK", "").split(",")
        if not is_ctx and "a" not in HSK:
            MM(psC[:, 0:128], t_lg[:], M1[d][:], r=["t_lg", "M1%d" % d], w=["psC0"])
        if "b" not in HSK:
            MM(psC[:, 128:256], t_lg[:], TRI[d][:], r=["t_lg", "TRI%d" % d], w=["psC0"])
        if "c" not in HSK:
            MM(psC[:, 256:384], UU[d][:], t_lg[:], r=["t_lg", "UU%d" % d], w=["psC0"])
        if not is_ctx:
            if "d" not in HSK:
                TR(psC[:, 384:512], t_k[:], identF[:], r=["t_k", "identF"], w=["psC0"])
            if "e" not in HSK:
                ACT(t_P[:, 0, :], psC[:, 0:128], AF.Exp, r=["psC0"], w=["P1"], scale=escale)
                ACT(t_P[:, 1, :], psC[:, 0:128], AF.Exp, r=["psC0"], w=["P2"], scale=-escale)
        if "e" not in HSK:
            ACT(t_P[:, 2, :], psC[:, 128:256], AF.Exp, r=["psC0"], w=["P3"], scale=escale)
            ACT(t_P[:, 3, :], psC[:, 256:384], AF.Exp, r=["psC0"], w=["P4"], scale=escale)
        if HST < 3:
            return
        PEN = os.environ.get("PEN", "dve")
        if "f" not in HSK:
            TT(PEN, t_kdec[:], t_k[:], t_P[:, 3, :], ALU.mult, r=["t_k", "P4"], w=["t_kdec"])
        if not is_ctx:
            qs = qT[:, lt * 128:(lt + 1) * 128]
            if "g" not in HSK:
                TT("dve", t_qd[:], qs, t_P[:, 0, :], ALU.mult, r=["qT", "P1"], w=["t_qd"])
            if "h" not in HSK:
                TT("dve", t_kdT[:], psC[:, 384:512], t_P[:, 1, :], ALU.mult, r=["psC0", "P2"], w=["t_kdT"])
            if "i" not in HSK:
                TT(PEN, t_qdec[:], qs, t_P[:, 2, :], ALU.mult, r=["qT", "P3"], w=["t_qdec"])
            if "j" not in HSK:
                MM(psC[:, 512:640], t_kdT[:], t_qd[:], r=["t_kdT", "t_qd"], w=["psC1"])
            if "k" not in HSK:
                S.op("dve", lambda e, d=d: e.copy_predicated(out=t_scm[d][:], mask=TRI[d][:].bitcast(U32), data=psC[:, 512:640]),
                     ["psC1", "TRI%d" % d], ["t_scm%d" % d])
            if "l" not in HSK:
                MM(psD[:, 0:V], t_scm[d][:], vst[:, t, 0:V], start=True, stop=False, r=["t_scm%d" % d, ("vst", t)], w=["psD"])
                MM(psD[:, 0:V], t_qdec[:], Sbf[:, 0:V], start=False, stop=True, r=["t_qdec", "Sbf"], w=["psD"])
        if HST < 4:
            return
        MM(psD[:, 256:256 + V], t_kdec[:], vst[:, t, 0:V], r=["t_kdec", ("vst", t)], w=["psD"])
        dcol = 127 if d == 0 else 0
        STT(St[:, 0:V], St[:, 0:V], t_P[:, 2, dcol:dcol + 1], psD[:, 256:256 + V], ALU.mult, ALU.add,
            r=["St", "P3", "psD"], w=["St"])
        ACP(Sbf[:, 0:V], St[:, 0:V], r=["St"], w=["Sbf"])
        if is_ctx:
            return
        if d == 0:
            ACP(ofw[:, lt, 0:V], psD[:, 0:V], r=["psD"], w=[("ofw", lt)])
            return
        if HST < 5:
            return
        TT("dve", t_o[:, 0:V], psD[:, 0:V], ofw[:, lt, 0:V], ALU.add, r=["psD", ("ofw", lt)], w=["t_o"])
        ACT(t_junk[:, 0:V], t_o[:, 0:V], AF.Square, r=["t_o"], w=["t_junk", "ssH"], accum=small[:, 2:3])
        rstd_from_ss(small[:, 2:3], small[:, 3:4], V, ["ssH"], "rsH")
        STT(t_y[:, 0:V], t_o[:, 0:V], small[:, 3:4], t_gate[:, 0:V], ALU.mult, ALU.mult, r=["t_o", "rsH", "t_gate"], w=["t_y"])
        dst = yThg if kind == "hg" else yTgl
        dk = "yThg" if kind == "hg" else "yTgl"
        for j in range(V // 128):
            TR(psE[:, j * 128:(j + 1) * 128], t_y[:, j * 128:(j + 1) * 128], identB[:], r=["t_y", "identB"], w=["psE"])
        for j in range(V // 128):
            ch = hh * (V // 128) + j
            ACP(dst[:, ch, lt * 128:(lt + 1) * 128], psE[:, j * 128:(j + 1) * 128], r=["psE"], w=[(dk, lt)])

    def head(kind, hh, buf, s):
        V = 128 if kind == "hg" else 256
        escale = 1.0 if kind == "hg" else -1.0 / 16.0
        wt = whd[buf]
        wk = "whd%d" % buf
        if kind == "hg":
            for d in range(2):
                for l in range(2):
                    DMA("sp", omt[:, d, l, :], lbl_d[l:l + 1, d * D + hh * 128:d * D + (hh + 1) * 128].partition_broadcast(128),
                        "su", w=["omt"])
            TT("dve", omt[:, :, 0, :], omt[:, :, 1, :], omt[:, :, 0, :], ALU.subtract, r=["omt"], w=["omt"])
            ACT(omt[:, :, 0, :], omt[:, :, 0, :], AF.Exp, r=["omt"], w=["omt"])
            TS("dve", omt[:, :, 1, :], omt[:, :, 0, :], 1.0, None, ALU.add, r=["omt"], w=["omt"])
            RCP(omt[:, :, 1, :], omt[:, :, 1, :], r=["omt"], w=["omt"])
            TT("dve", oml[:], omt[:, :, 0, :], omt[:, :, 1, :], ALU.mult, r=["omt"], w=["oml"])
        for g in range(4):
            for c in range(8):
                MM(psB[:, 512:1024], wt[:, c, 0:128], hT[:, c, CTX + g * 512:CTX + (g + 1) * 512], start=(c == 0), stop=(c == 7),
                   r=[wk] + [("hT", NCT + g * 4 + i) for i in range(4)], w=["psB1"])
            AMUL(qT[:, g * 512:(g + 1) * 512], psB[:, 512:1024], QSCALE, r=["psB1"], w=["qT"])
        import os
        if int(os.environ.get("HST", "9")) < 1:
            return
        for d in range(2):
            MSET("dve", t_scm[d][:], 0.0, w=["t_scm%d" % d])
            MSET("dve", St[:], 0.0, w=["St"])
            MSET("dve", Sbf[:], 0.0, w=["Sbf"])
            order = list(range(NT)) if d == 0 else [1, 0] + list(range(NT - 1, NCT - 1, -1))
            order = order[:int(os.environ.get("HTL%d" % d, "99"))]
            for t in order:
                scan_tile(kind, hh, buf, d, t, V, escale)

    def phase_C1(s):
        DMA("pool", wbh[:], wbh_d.rearrange("(c p) n -> p c n", p=128), "pw", w=["wbh"])
        DMA("pool", wbg[:], wbg_d.rearrange("(c p) n -> p c n", p=128), "pw", w=["wbg"])
        DMA("pool", wm[:], winv[:, :, 8224:8224 + 2 * D], "pw", w=["wm"])
        for lt in range(NLT):
            t = lt + NCT
            tok = slice(lt * 128, (lt + 1) * 128)
            for hf in range(2):
                fs = slice(hf * 512, (hf + 1) * 512)
                for c in range(8):
                    MM(psA[:, 0:512], yThg[:, c, tok], wbh[:, c, fs], start=(c == 0), stop=(c == 7), r=[("yThg", lt), "wbh"], w=["psA0"])
                for c in range(8):
                    MM(psA[:, 512:1024], yTgl[:, c, tok], wbg[:, c, fs], start=(c == 0), stop=(c == 7), r=[("yTgl", lt), "wbg"], w=["psA1"])
                for c in range(8):
                    MM(psB[:, 0:512], hT[:, c, t * 128:(t + 1) * 128], wm[:, c, fs], start=(c == 0), stop=(c == 7), r=[("hT", t), "wm"], w=["psB0"])
                for c in range(8):
                    MM(psB[:, 512:1024], hT[:, c, t * 128:(t + 1) * 128], wm[:, c, D + hf * 512:D + (hf + 1) * 512],
                       start=(c == 0), stop=(c == 7), r=[("hT", t), "wm"], w=["psB1"])
                for i, (pm, py, pmk, pyk) in enumerate(((psB[:, 0:512], psA[:, 0:512], "psB0", "psA0"),
                                                        (psB[:, 512:1024], psA[:, 512:1024], "psB1", "psA1"))):
                    ACT(c_e[i][:], pm, AF.Exp, r=[pmk], w=["c_e%d" % i], scale=-1.0)
                    TS("dve", c_e[i][:], c_e[i][:], 1.0, None, ALU.add, r=["c_e%d" % i], w=["c_e%d" % i])
                    RCP(c_e[i][:], c_e[i][:], r=["c_e%d" % i], w=["c_e%d" % i])
                    TT("dve", c_t[i][:], py, c_e[i][:], ALU.mult, r=[pyk, "c_e%d" % i], w=["c_t%d" % i])
                TT("dve", c_ym[:, fs], c_t[0][:], c_t[1][:], ALU.add, r=["c_t0", "c_t1"], w=["c_ym"])
            for c in range(8):
                TR(psE[:, c * 128:(c + 1) * 128], c_ym[:, c * 128:(c + 1) * 128], identB[:], r=["c_ym", "identB"], w=["psE"])
            ACP(yThg[:, :, tok], psE[:].rearrange("p (c t) -> p c t", c=8), r=["psE"], w=[("yThg", lt)])

    def peer_topk():
        for g in range(16):
            sg = sc[:, g * 128:(g + 1) * 128]
            S.op("dve", lambda e, g=g, sg=sg: e.max(out=v16[:, g, 0:8], in_=sg), ["sc"], [("v16", g)])
            S.op("dve", lambda e, g=g, sg=sg: e.max_index(out=i16[:, g, 0:8], in_max=v16[:, g, 0:8], in_values=sg), ["sc", ("v16", g)], [("i16", g)])
            S.op("dve", lambda e, g=g, sg=sg: e.match_replace(out=scw[:, 0:128], in_to_replace=v16[:, g, 0:8], in_values=sg, imm_value=NEG),
                 ["sc", ("v16", g)], ["scw"])
            S.op("dve", lambda e, g=g: e.max(out=v16[:, g, 8:16], in_=scw[:, 0:128]), ["scw"], [("v16b", g)])
            S.op("dve", lambda e, g=g: e.max_index(out=i16[:, g, 8:16], in_max=v16[:, g, 8:16], in_values=scw[:, 0:128]),
                 ["scw", ("v16b", g)], [("i16b", g)])
        allv = [("v16", g) for g in range(16)] + [("v16b", g) for g in range(16)]
        alli = [("i16", g) for g in range(16)] + [("i16b", g) for g in range(16)]
        v16v = v16[:].rearrange("p (h f) k -> p h f k", f=2)
        for h in range(8):
            TT("dve", cand[:, h, :].rearrange("p (a b) -> p a b", a=16),
               v16[:, 2 * h, :].unsqueeze(2).to_broadcast([128, 16, 16]),
               v16[:, 2 * h + 1, :].unsqueeze(1).to_broadcast([128, 16, 16]), ALU.add, r=allv, w=[("cand", h)])
        for h in range(8):
            ch = cand[:, h, :]
            S.op("dve", lambda e, h=h, ch=ch: e.max(out=ts[:, h, 0:8], in_=ch), [("cand", h)], [("ts", h)])
            S.op("dve", lambda e, h=h, ch=ch: e.max_index(out=pos[:, h, 0:8], in_max=ts[:, h, 0:8], in_values=ch), [("cand", h), ("ts", h)], [("pos", h)])
            S.op("dve", lambda e, h=h, ch=ch: e.match_replace(out=scw[:, 0:256], in_to_replace=ts[:, h, 0:8], in_values=ch, imm_value=NEG),
                 [("cand", h), ("ts", h)], ["scw"])
            S.op("dve", lambda e, h=h: e.max(out=ts[:, h, 8:16], in_=scw[:, 0:256]), ["scw"], [("tsb", h)])
            S.op("dve", lambda e, h=h: e.max_index(out=pos[:, h, 8:16], in_max=ts[:, h, 8:16], in_values=scw[:, 0:256]),
                 ["scw", ("tsb", h)], [("posb", h)])
        allts = [("ts", h) for h in range(8)] + [("tsb", h) for h in range(8)]
        allpos = [("pos", h) for h in range(8)] + [("posb", h) for h in range(8)]
        pex3 = pex[:].rearrange("p (h k) -> p h k", h=8)
        TT("dve", pex3, ts[:], ts[:, :, 0:1].to_broadcast([128, 8, 16]), ALU.subtract, r=allts, w=["pex"])
        ACT(pex[:], pex[:], AF.Exp, r=["pex"], w=["pex"])
        S.op("dve", lambda e: e.tensor_reduce(out=psm[:, 0:8], in_=pex3, axis=AX.X, op=ALU.add), ["pex"], ["psm"])
        RCP(psm[:, 8:16], psm[:, 0:8], r=["psm"], w=["psm"])
        TT("dve", pgate[:].rearrange("p (h k) -> p h k", h=8), pex3, psm[:, 8:16].unsqueeze(2).to_broadcast([128, 8, 16]),
           ALU.mult, r=["pex", "psm"], w=["pgate"])
        posi = pos[:].bitcast(I32)
        S.op("dve", lambda e: e.tensor_single_scalar(out=pa[:, 0], in_=posi, scalar=4, op=ALU.logical_shift_right), allpos, ["pa0"])
        S.op("dve", lambda e: e.tensor_single_scalar(out=pa[:, 1], in_=posi, scalar=15, op=ALU.bitwise_and), allpos, ["pa1"])
        CP("dve", paf[:], pa[:], r=["pa0", "pa1"], w=["paf"])
        CP("dve", i16f[:], i16[:], r=alli, w=["i16f"])
        i16fv = i16f[:].rearrange("p (h f) k -> p h f k", f=2)
        for f in range(2):
            for h in range(8):
                ohh = oh[:, h, :].rearrange("p (r a) -> p r a", r=16)
                TT("dve", ohh, paf[:, f, h, :].unsqueeze(2).to_broadcast([128, 16, 16]),
                   iota16[:].unsqueeze(1).to_broadcast([128, 16, 16]), ALU.is_equal, r=["paf", "iota16"], w=[("oh", h)])
                TT("dve", ohh, ohh, i16f[:, 2 * h + f, :].unsqueeze(1).to_broadcast([128, 16, 16]), ALU.mult,
                   r=[("oh", h), "i16f"], w=[("oh", h)])
            S.op("dve", lambda e, f=f: e.tensor_reduce(out=isel[:, f, :], in_=oh[:].rearrange("p h (r a) -> p (h r) a", r=16),
                                                       axis=AX.X, op=ALU.add), [("oh", h) for h in range(8)], [("isel", f)])
        STT(eidx[:], isel[:, 0, :], 128.0, isel[:, 1, :], ALU.mult, ALU.add, r=[("isel", 0), ("isel", 1)], w=["eidx"])

    gctr = [0]

    def phase_D(s):
        DMA("sp", wps[:], wps_d, "dw", r=["wps_d"], w=["wps"])
        DMA("pool", wo[:], wo_d.rearrange("(c p) n -> p c n", p=128), "pw", w=["wo"])
        DMA("sp", G1[:], modscr[s:s + 1, 2 * D:3 * D].partition_broadcast(128), "dw", r=["modscr"], w=["G1"])
        DMA("sp", B2[:], modscr[s:s + 1, 3 * D:4 * D].partition_broadcast(128), "dw", r=["modscr"], w=["B2"])
        DMA("sp", A2[:], modscr[s:s + 1, 4 * D:5 * D].partition_broadcast(128), "dw", r=["modscr"], w=["A2"])
        DMA("sp", G2[:], modscr[s:s + 1, 5 * D:6 * D].partition_broadcast(128), "dw", r=["modscr"], w=["G2"])
        DMA("sp", FG[:], gffn_d.partition_broadcast(128), "dw", w=["FG"])
        STT(A2[:], A2[:], 1.0, FG[:], ALU.add, ALU.mult, r=["A2", "FG"], w=["A2"])
        DMA("sp", FG[:], fg_d.partition_broadcast(128), "dw", r=["A2"], w=["FG"])
        for lt in range(NLT):
            tok = slice(lt * 128, (lt + 1) * 128)
            DMA("sp", dxin[:], x_d[s, tok, :], "dx", w=["dxin"])
            for hf in range(2):
                fs = slice(hf * 512, (hf + 1) * 512)
                for c in range(8):
                    MM(psA[:, fs], yThg[:, c, tok], wo[:, c, fs], start=(c == 0), stop=(c == 7), r=[("yThg", lt), "wo"], w=["psA%d" % hf])
                TT("dve", x1[:, fs], psA[:, fs], G1[:, fs], ALU.mult, r=["psA%d" % hf, "G1"], w=[("x1", hf)])
                TT("dve", x1[:, fs], x1[:, fs], dxin[:, fs], ALU.add, r=[("x1", hf), "dxin"], w=[("x1", hf)])
            x1k = [("x1", 0), ("x1", 1)]
            if "x1" in dbg_out:
                DMA("sp", dbg_out["x1"][lt * 128:(lt + 1) * 128, :], x1[:], "dbg", r=x1k)
            ACT(djunk[:], x1[:], AF.Square, r=x1k, w=["djunk", "ssD"], accum=small[:, 4:5])
            rstd_from_ss(small[:, 4:5], small[:, 5:6], D, ["ssD"], "rsD")
            STT(h2[:], x1[:], small[:, 5:6], A2[:], ALU.mult, ALU.mult, r=x1k + ["rsD", "A2"], w=["h2"])
            TT("dve", h2[:], h2[:], B2[:], ALU.add, r=["h2", "B2"], w=["h2"])
            for c in range(8):
                TR(psA[:, c * 128:(c + 1) * 128], h2[:, c * 128:(c + 1) * 128], identF[:], r=["h2", "identF"], w=["psA%d" % (c // 4)])
            ACP(h2T[:].rearrange("p c t -> p (c t)"), psA[:], r=["psA0", "psA1"], w=["h2T"])
            for q in range(4):
                pt = (psB, psC)[q // 2]
                pk = ("psB0", "psB1", "psC0", "psC1")[q]
                for c in range(8):
                    MM(pt[:, (q % 2) * 512:(q % 2 + 1) * 512], h2T[:, c, :], wps[:, c, q * 512:(q + 1) * 512], start=(c == 0), stop=(c == 7),
                       r=["h2T", "wps"], w=[pk])
                if q % 2 == 0:
                    ACP(sc[:, q * 512:(q + 1) * 512], pt[:, 0:512], r=[pk], w=["sc"])
                else:
                    CP("dve", sc[:, q * 512:(q + 1) * 512], pt[:, 512:1024], r=[pk], w=["sc"])
            peer_topk()
            GRP = 4
            S.barrier()
            for g0 in range(0, 128, GRP):
                ks = []
                for j in range(g0, g0 + GRP):
                    k = gctr[0] % NGB
                    gctr[0] += 1
                    ks.append(k)
                    S.dma("pool", lambda e, k=k, j=j: e.indirect_dma_start(out=gb[k][:], out_offset=None, in_=puv_d,
                                                                           in_offset=bass.IndirectOffsetOnAxis(ap=eidx[:, j:j + 1], axis=0)),
                          "gb%d" % (k % 8), ["eidx", "ptab"], [("gb", k)])
                    STT(djunk[:], gb[k][:, 0:D], 1.0, h2[:], ALU.mult, ALU.mult, r=[("gb", k), "h2"], w=["djunk", ("pact", g0)],
                        accum=pact[:, j:j + 1])
                ACT(pact[:, g0:g0 + GRP], pact[:, g0:g0 + GRP], AF.Gelu, r=[("pact", g0)], w=[("pact", g0)])
                TT("dve", pact[:, g0:g0 + GRP], pact[:, g0:g0 + GRP], pgate[:, g0:g0 + GRP], ALU.mult, r=[("pact", g0), "pgate"], w=[("pact", g0)])
                for j, k in zip(range(g0, g0 + GRP), ks):
                    dk = j % NDG
                    AMUL(dg[dk][:], identF[:], pact[:, j:j + 1], r=["identF", ("pact", g0)], w=[("dg", dk)])
                    for hf in range(2):
                        MM(psA[:, hf * 512:(hf + 1) * 512], dg[dk][:], gb[k][:, D + hf * 512:D + (hf + 1) * 512], start=(j == 0), stop=(j == 127),
                           r=[("dg", dk), ("gb", k)], w=["psA%d" % hf])
            S.barrier()
            for hf in range(2):
                fs = slice(hf * 512, (hf + 1) * 512)
                TT("dve", acc[:, fs], psA[:, fs], G2[:, fs], ALU.mult, r=["psA%d" % hf, "G2", "acc"], w=["acc"])
                TT("dve", acc[:, fs], acc[:, fs], x1[:, fs], ALU.add, r=["acc", ("x1", hf)], w=["acc"])
            ACT(djunk[:], acc[:], AF.Square, r=["acc"], w=["djunk", "ssF"], accum=small[:, 6:7])
            rstd_from_ss(small[:, 6:7], small[:, 7:8], D, ["ssF"], "rsF")
            STT(h2[:], acc[:], small[:, 7:8], FG[:], ALU.mult, ALU.mult, r=["acc", "rsF", "FG"], w=["h2"])
            DMA("sp", out_d[s, tok, :], h2[:], "out", r=["h2"])

    def dump_mixer():
        S.barrier()
        if "yThg" in dbg_out:
            DMA("sp", dbg_out["yThg"], yThg[:], "dbg", r=[("yThg", lt) for lt in range(NLT)])
            DMA("sp", dbg_out["yTgl"], yTgl[:], "dbg", r=[("yTgl", lt) for lt in range(NLT)])
            DMA("sp", dbg_out["hT"], hT[:], "dbg", r=[("hT", t) for t in range(NT)])
        if "grT" in dbg_out:
            DMA("sp", dbg_out["grT"], grT[:], "dbg", r=["grT"])

    def finish():
        S.emit(final_wait_keys=["out"] + (["dbg"] if dbg_out else []))
        return nc

    for s in range(NS):
        load_seq_mod(s)
        phase_A(s)
        S.barrier()
        if stop == "A":
            dump_mixer()
            S.emit(final_wait_keys=["dbg"])
            return nc
        gr_prepass()
        S.barrier()
        if stop == "gr":
            dump_mixer()
            S.emit(final_wait_keys=["dbg"])
            return nc
        hi = 0
        heads = [("hg", h) for h in range(8)] + [("gl", h) for h in range(4)]
        if stop == "head0":
            heads = [("hg", 0)]
        if stop == "head8":
            heads = [("gl", 0)]
        load_head_w(heads[0][0], heads[0][1], 0)
        for hi, (kind, hh) in enumerate(heads):
            if hi + 1 < len(heads):
                load_head_w(heads[hi + 1][0], heads[hi + 1][1], (hi + 1) % 2)
            head(kind, hh, hi % 2, s)
        if s == 0 and (dbg_out or stop in ("head0", "head8", "heads")):
            dump_mixer()
        if stop in ("head0", "head8", "heads"):
            S.emit(final_wait_keys=["dbg"])
            return nc
        S.barrier()
        phase_C1(s)
        S.barrier()
        phase_D(s)
        S.barrier()
    S.emit(final_wait_keys=["out"] + (["dbg"] if dbg_out else []))
    return nc


_CACHE = {}


def _in_maps(inputs, NS, cores):
    f = lambda a: np.ascontiguousarray(a, dtype=np.float32)
    shared = {
        "c_ctx": f(inputs["c_ctx"]).reshape(1, D),
        "ada_w": f(inputs["ada_w"]).reshape(D, 6 * D),
        "ada_b": f(inputs["ada_b"]).reshape(1, 6 * D),
        "norm_mix_g": f(inputs["norm_mix_g"]).reshape(1, D),
        "w_in": f(inputs["w_in"]).reshape(D, D_IN),
        "lb_logits": f(inputs["hgrn_lb_logits"]).reshape(2, 2 * D),
        "hgrn_norm_g": f(inputs["hgrn_norm_g"]).reshape(1, 128),
        "gla_gk_w": f(inputs["gla_gk_w"]).reshape(2, 16, 512),
        "gla_gk_b": f(inputs["gla_gk_b"]).reshape(2, 512),
        "gla_norm_g": f(inputs["gla_norm_g"]).reshape(1, 256),
        "w_branch_hgrn": f(inputs["w_branch_hgrn"]).reshape(D, D),
        "w_branch_gla": f(inputs["w_branch_gla"]).reshape(D, D),
        "w_out": f(inputs["w_out"]).reshape(D, D),
        "norm_ffn_g": f(inputs["norm_ffn_g"]).reshape(1, D),
        "peer_wq": f(inputs["peer_wq"]).reshape(D, 2048),
        "peer_k1": f(inputs["peer_k1"]).reshape(128, 128),
        "peer_k2": f(inputs["peer_k2"]).reshape(128, 128),
        "peer_u": f(inputs["peer_u"]).reshape(16384, D),
        "peer_v": f(inputs["peer_v"]).reshape(16384, D),
        "final_g": f(inputs["final_g"]).reshape(1, D),
    }
    x = f(inputs["x"])
    ctx = f(inputs["ctx"])
    c = f(inputs["c"])
    maps = []
    for i in range(cores):
        m = dict(shared)
        m["x"] = x[i * NS:(i + 1) * NS]
        m["ctx"] = ctx[i * NS:(i + 1) * NS]
        m["c"] = c[i * NS:(i + 1) * NS]
        maps.append(m)
    return maps


def kernel(**inputs):
    B = inputs["x"].shape[0]
    NS = B // N_CORES
    if NS not in _CACHE:
        _CACHE[NS] = build_program(NS)
    nc = _CACHE[NS]
    maps = _in_maps(inputs, NS, N_CORES)
    res = run_bass_kernel_spmd(nc, maps, core_ids=list(range(N_CORES)))
    return np.concatenate([r["out"] for r in res.results], axis=0).astype(np.float32)
```

```python
import contextlib
import numpy as np
import concourse.bass as bass
import concourse.mybir as mybir
from concourse.bass_utils import run_bass_kernel_spmd

F32 = mybir.dt.float32
BF16 = mybir.dt.bfloat16
I32 = mybir.dt.int32
U32 = mybir.dt.uint32
AF = mybir.ActivationFunctionType
ALU = mybir.AluOpType
AX = mybir.AxisListType

N_CORES = 8
D = 1024
SEQ = 2048
CTX = 256
NLT = SEQ // 128
NCT = CTX // 128
NT = NLT + NCT
TOK = SEQ + CTX
D_IN = 10272
EPS = 1e-6
QSCALE = 128 ** -0.5
NEG = -1e30


class _Op:
    __slots__ = ("eng", "fn", "deps", "signal", "tok_sem", "tok_val", "is_dma", "dma_key")

    def __init__(self, eng, fn, is_dma=False, dma_key=None):
        self.eng = eng
        self.fn = fn
        self.deps = []
        self.signal = False
        self.tok_sem = None
        self.tok_val = 0
        self.is_dma = is_dma
        self.dma_key = dma_key


class Sched:
    ENGS = ("pe", "act", "dve", "pool", "sp")

    def __init__(self, nc):
        self.nc = nc
        self.ops = {e: [] for e in self.ENGS}
        self.last_writer = {}
        self.readers = {}
        self.last_eng = {}
        self.last_key = {}
        self.bar_deps = []
        self.bar_need = set()

    def barrier(self):
        self.bar_deps = list(self.last_eng.values()) + list(self.last_key.values())
        self.bar_need = set(self.ENGS)

    def _add(self, op, reads, writes):
        excl = [k for k in reads if isinstance(k, str) and k.startswith("ps")]
        if excl:
            reads = [k for k in reads if k not in excl]
            writes = list(writes) + [k for k in excl if k not in writes]
        deps = []
        if op.eng in self.bar_need:
            deps.extend(self.bar_deps)
            self.bar_need.discard(op.eng)
        for r in reads:
            w = self.last_writer.get(r)
            if w is not None:
                deps.append(w)
        for wkey in writes:
            w = self.last_writer.get(wkey)
            if w is not None:
                deps.append(w)
            deps.extend(self.readers.get(wkey, ()))
        seen = set()
        for d in deps:
            if d is op or id(d) in seen:
                continue
            seen.add(id(d))
            if (not d.is_dma) and (not op.is_dma) and d.eng == op.eng and op.eng == "pe":
                continue
            op.deps.append(d)
            d.signal = True
        for r in reads:
            self.readers.setdefault(r, []).append(op)
        for wkey in writes:
            self.last_writer[wkey] = op
            self.readers[wkey] = []
        self.ops[op.eng].append(op)
        if op.is_dma:
            self.last_key[op.dma_key] = op
        else:
            self.last_eng[op.eng] = op
        return op

    def op(self, eng, fn, reads=(), writes=()):
        return self._add(_Op(eng, fn), list(reads), list(writes))

    def dma(self, eng, fn, key, reads=(), writes=()):
        o = _Op(eng, fn, is_dma=True, dma_key=key)
        o.signal = True
        return self._add(o, list(reads), list(writes))

    def emit(self, final_wait_keys=()):
        nc = self.nc
        dma_keys = []
        for e in self.ENGS:
            for o in self.ops[e]:
                if o.is_dma and o.dma_key not in dma_keys:
                    dma_keys.append(o.dma_key)
        with contextlib.ExitStack() as st:
            eng_sem = {e: st.enter_context(nc.semaphore("S_" + e)) for e in self.ENGS}
            key_sem = {k: st.enter_context(nc.semaphore("D_%d" % i)) for i, k in enumerate(dma_keys)}
            key_cnt = {k: 0 for k in dma_keys}
            key_eng = {}
            for e in self.ENGS:
                cnt = 0
                for o in self.ops[e]:
                    if o.is_dma:
                        assert key_eng.setdefault(o.dma_key, e) == e, "dma key used from two queues"
                        key_cnt[o.dma_key] += 16
                        o.tok_sem = key_sem[o.dma_key]
                        o.tok_val = key_cnt[o.dma_key]
                    elif o.signal:
                        cnt += 1
                        o.tok_sem = eng_sem[e]
                        o.tok_val = cnt
            blk = st.enter_context(nc.Block())
            engobj = {"pe": "tensor", "act": "scalar", "dve": "vector", "pool": "gpsimd", "sp": "sync"}

            self.stats = {}

            def make(e):
                def body(eng):
                    waited = {}
                    nw = 0
                    for o in self.ops[e]:
                        for d in o.deps:
                            s = d.tok_sem
                            if waited.get(id(s), 0) >= d.tok_val:
                                continue
                            waited[id(s)] = d.tok_val
                            eng.wait_ge(s, d.tok_val)
                            nw += 1
                        ins = o.fn(eng)
                        if o.is_dma:
                            ins.then_inc(o.tok_sem, 16)
                        elif o.signal:
                            ins.then_inc(o.tok_sem, 1)
                    self.stats[e] = (len(self.ops[e]), nw)
                    if e == "sp":
                        for k in final_wait_keys:
                            if k in key_sem:
                                eng.wait_ge(key_sem[k], key_cnt[k])
                return body

            for e in self.ENGS:
                getattr(blk, engobj[e])(make(e))


def build_program(NS, dbg=(), stop=None):
    nc = bass.Bass("TRN2", target_bir_lowering=False)

    def din(name, shape, dt=F32):
        return nc.dram_tensor(name, list(shape), dt, kind="ExternalInput").ap()

    x_d = din("x", [NS, SEQ, D])
    ctx_d = din("ctx", [NS, CTX, D])
    c_d = din("c", [NS, D])
    cctx_d = din("c_ctx", [1, D])
    adaw_d = din("ada_w", [D, 6 * D])
    adab_d = din("ada_b", [1, 6 * D])
    gmix_d = din("norm_mix_g", [1, D])
    win_d = din("w_in", [D, D_IN])
    lbl_d = din("lb_logits", [2, 2 * D])
    hgg_d = din("hgrn_norm_g", [1, 128])
    gkw_d = din("gla_gk_w", [2, 16, 512])
    gkb_d = din("gla_gk_b", [2, 512])
    glg_d = din("gla_norm_g", [1, 256])
    wbh_d = din("w_branch_hgrn", [D, D])
    wbg_d = din("w_branch_gla", [D, D])
    wo_d = din("w_out", [D, D])
    gffn_d = din("norm_ffn_g", [1, D])
    wq_d = din("peer_wq", [D, 2048])
    k1_d = din("peer_k1", [128, 128])
    k2_d = din("peer_k2", [128, 128])
    pu_d = din("peer_u", [16384, D])
    pv_d = din("peer_v", [16384, D])
    fg_d = din("final_g", [1, D])
    out_d = nc.dram_tensor("out", [NS, SEQ, D], F32, kind="ExternalOutput").ap()
    modscr = nc.dram_tensor("modscr", [NS + 1, 6 * D], F32, kind="Internal").ap()
    wps_d = nc.dram_tensor("wps", [128, 8, 2048], BF16, kind="Internal").ap()
    puv_d = nc.dram_tensor("puv_bf", [16384, 2 * D], BF16, kind="Internal").ap()
    dbg_out = {}
    for name, shape, dt in dbg:
        dbg_out[name] = nc.dram_tensor("dbg_" + name, list(shape), dt, kind="ExternalOutput").ap()

    ARENA = 212000
    arena = nc.alloc_sbuf_tensor("arena", [128, ARENA // 4], F32)
    base = nc.lookup_mloc(arena).addr
    cur = {"p": base, "limit": base + ARENA}

    def _sz(shape, dt):
        n = int(np.prod(shape[1:])) * (2 if dt == BF16 else 4)
        return (n + 63) // 64 * 64

    def alloc(name, shape, dt, at=None):
        n = _sz(shape, dt)
        if at is None:
            off = cur["p"]
            cur["p"] += n
            assert cur["p"] <= cur["limit"], ("SBUF overflow", name, cur["p"] - base)
        else:
            off = at[0]
            at[0] += n
            assert at[0] <= cur["limit"], ("SBUF phase overflow", name, at[0] - base)
        return nc.alloc_sbuf_tensor_at(name, list(shape), dt, offset=off)

    identF = alloc("identF", [128, 128], F32)
    identB = alloc("identB", [128, 128], BF16)
    TRI = [alloc("TRIf", [128, 128], F32), alloc("TRIb", [128, 128], F32)]
    M1 = [alloc("M1f", [128, 128], F32), alloc("M1b", [128, 128], F32)]
    UU = [alloc("Uf", [128, 128], F32), alloc("Ub", [128, 128], F32)]
    hgg_bc = alloc("hgg_bc", [128, 128], F32)
    glg_bc = alloc("glg_bc", [128, 256], F32)
    modsm = alloc("modsm", [128, 6, 8], F32)
    small = alloc("small", [128, 16], F32)
    yThg = alloc("yThg", [128, 8, SEQ], BF16)
    wgr = alloc("wgr", [128, 8, 64], BF16)
    gkw = alloc("gkw", [64, 512], F32)
    grT = alloc("grT", [64, TOK], F32)
    iota16 = alloc("iota16", [128, 16], F32)
    PH = cur["p"]

    a = [PH]
    hT = alloc("hT", [128, 8, TOK], BF16, a)
    yTgl = alloc("yTgl", [128, 8, SEQ], BF16, a)
    MX = a[0]
    a = [MX]
    qT = alloc("qT", [128, SEQ], F32, a)
    vst = alloc("vst", [128, NT, 256], BF16, a)
    ofw = alloc("ofw", [128, NLT, 256], F32, a)
    whd = [alloc("whd0", [128, 8, 768], BF16, a), alloc("whd1", [128, 8, 768], BF16, a)]
    oml = alloc("oml", [128, 2, 128], F32, a)
    omt = alloc("omt", [128, 2, 2, 128], F32, a)
    t_e = alloc("t_e", [128, 256], F32, a)
    t_r = alloc("t_r", [128, 256], F32, a)
    t_k = [alloc("t_k%d" % i, [128, 128], F32, a) for i in range(3)]
    t_f = alloc("t_f", [128, 128], F32, a)
    t_lg = [alloc("t_lg%d" % i, [128, 128], F32, a) for i in range(3)]
    t_P = [alloc("t_P%d" % i, [128, 4, 128], F32, a) for i in range(2)]
    t_qd = [alloc("t_qd%d" % i, [128, 128], BF16, a) for i in range(2)]
    t_kdT = [alloc("t_kdT%d" % i, [128, 128], BF16, a) for i in range(2)]
    t_qdec = [alloc("t_qdec%d" % i, [128, 128], BF16, a) for i in range(2)]
    t_kdec = [alloc("t_kdec%d" % i, [128, 128], BF16, a) for i in range(2)]
    t_scm = [[alloc("t_scm%d%d" % (d_, i), [128, 128], BF16, a) for i in range(2)] for d_ in range(2)]
    St = alloc("St", [128, 256], F32, a)
    Sbf = alloc("Sbf", [128, 256], BF16, a)
    t_o = alloc("t_o", [128, 256], F32, a)
    t_gate = [alloc("t_gate%d" % i, [128, 256], F32, a) for i in range(3)]
    t_junk = alloc("t_junk", [128, 256], F32, a)
    t_y = alloc("t_y", [128, 256], BF16, a)
    a = [MX]
    xin = [alloc("xin0", [128, D], F32, a), alloc("xin1", [128, D], F32, a)]
    xs = alloc("xs", [128, D], F32, a)
    a = [MX]
    wbh = alloc("wbh", [128, 8, D], BF16, a)
    wbg = alloc("wbg", [128, 8, D], BF16, a)
    wm = alloc("wm", [128, 8, 2 * D], BF16, a)
    c_e = [alloc("c_e0", [128, 512], F32, a), alloc("c_e1", [128, 512], F32, a)]
    c_t = [alloc("c_t0", [128, 512], F32, a), alloc("c_t1", [128, 512], F32, a)]
    c_ym = alloc("c_ym", [128, D], BF16, a)
    a = [PH]
    s_l = alloc("s_l", [128, 2, 2 * D], F32, a)
    adaw = [alloc("adaw0", [128, 8, 512], F32, a), alloc("adaw1", [128, 8, 512], F32, a)]
    cT = alloc("cT", [128, 8, NS + 1], F32, a)
    scT = alloc("scT", [128, 8, NS + 1], F32, a)
    s_tmp = alloc("s_tmp", [128, 8, NS + 1], F32, a)
    modrows = alloc("modrows", [NS + 1, 6 * D], F32, a)
    adab_sb = alloc("adab_sb", [NS + 1, 6 * D], F32, a)
    wqb = [alloc("wqb0", [128, 8, 128], F32, a), alloc("wqb1", [128, 8, 128], F32, a)]
    kraw = alloc("kraw", [128, 2, 128], F32, a)
    kT = alloc("kT", [128, 2, 128], F32, a)
    wqT = [alloc("wqT0", [128, 128], F32, a), alloc("wqT1", [128, 128], F32, a)]
    wpb = [alloc("wpb0", [128, 8, 128], BF16, a), alloc("wpb1", [128, 8, 128], BF16, a)]
    a = [PH]
    cst_f = [alloc("cst_f0", [128, 4096], F32, a), alloc("cst_f1", [128, 4096], F32, a)]
    cst_b = [alloc("cst_b0", [128, 4096], BF16, a), alloc("cst_b1", [128, 4096], BF16, a)]
    a = [PH]
    wps = alloc("wps_sb", [128, 8, 2048], BF16, a)
    wo = alloc("wo", [128, 8, D], BF16, a)
    G1 = alloc("G1", [128, D], F32, a)
    A2 = alloc("A2", [128, D], F32, a)
    B2 = alloc("B2", [128, D], F32, a)
    G2 = alloc("G2", [128, D], F32, a)
    FG = alloc("FG", [128, D], F32, a)
    NGB = 14
    gb = [alloc("gb%d" % i, [128, 2 * D], BF16, a) for i in range(8)]
    NDG = 4
    dg = [alloc("dg%d" % i, [128, 128], BF16, a) for i in range(NDG)]
    sc = alloc("sc", [128, 2048], F32, a)
    scw = alloc("scw", [128, 256], F32, a)
    dxin = alloc("dxin", [128, D], F32, a)
    x1 = alloc("x1", [128, D], F32, a)
    h2 = alloc("h2", [128, D], F32, a)
    h2T = alloc("h2T", [128, 8, 128], BF16, a)
    acc = alloc("acc", [128, D], F32, a)
    djunk = alloc("djunk", [128, D], F32, a)
    v16 = alloc("v16", [128, 16, 16], F32, a)
    i16 = alloc("i16", [128, 16, 16], U32, a)
    i16f = alloc("i16f", [128, 16, 16], F32, a)
    cand = alloc("cand", [128, 8, 256], F32, a)
    ts = alloc("ts", [128, 8, 16], F32, a)
    pos = alloc("pos", [128, 8, 16], U32, a)
    pa = alloc("pa", [128, 2, 8, 16], I32, a)
    paf = alloc("paf", [128, 2, 8, 16], F32, a)
    oh = alloc("oh", [128, 8, 256], F32, a)
    isel = alloc("isel", [128, 2, 128], F32, a)
    eidx = alloc("eidx", [128, 128], I32, a)
    pgate = alloc("pgate", [128, 128], F32, a)
    pact = alloc("pact", [128, 128], F32, a)
    pex = alloc("pex", [128, 128], F32, a)
    psm = alloc("psm", [128, 16], F32, a)
    for t_ in (sc, cand, oh):
        o_ = [nc.lookup_mloc(t_).addr] if False else None
    def _off(t_):
        return t_.manual_sbuf_range[0]
    for t_ in (sc, cand, oh):
        aa = [_off(t_)]
        gb.append(alloc("gb%d" % len(gb), [128, 2 * D], BF16, aa))
        gb.append(alloc("gb%d" % len(gb), [128, 2 * D], BF16, aa))

    psA = nc.alloc_psum_tensor("psA", [128, 1024], F32)
    psB = nc.alloc_psum_tensor("psB", [128, 1024], F32)
    psC = nc.alloc_psum_tensor("psC", [128, 1024], F32)
    psD = nc.alloc_psum_tensor("psD", [128, 512], F32)
    psE = nc.alloc_psum_tensor("psE", [128, 1024], BF16)

    S = Sched(nc)

    def MM(out, lhsT, rhs, start=True, stop=True, r=(), w=()):
        S.op("pe", lambda e: e.matmul(out, lhsT=lhsT, rhs=rhs, start=start, stop=stop), r, w)

    def TR(out, in_, ident, r=(), w=()):
        S.op("pe", lambda e: e.transpose(out=out, in_=in_, identity=ident), r, w)

    def ACT(out, in_, func, r=(), w=(), bias=0.0, scale=1.0, accum=None):
        if accum is None:
            S.op("act", lambda e: e.activation(out=out, in_=in_, func=func, bias=bias, scale=scale), r, w)
        else:
            S.op("act", lambda e: e.activation(out=out, in_=in_, func=func, bias=bias, scale=scale, accum_out=accum), r, w)

    def ACP(out, in_, r=(), w=()):
        S.op("act", lambda e: e.copy(out=out, in_=in_), r, w)

    def AMUL(out, in_, mul, r=(), w=()):
        S.op("act", lambda e: e.mul(out=out, in_=in_, mul=mul), r, w)

    def TT(eng, out, in0, in1, op, r=(), w=()):
        S.op(eng, lambda e: e.tensor_tensor(out=out, in0=in0, in1=in1, op=op), r, w)

    def TS(eng, out, in0, s1, s2, op0, op1=None, r=(), w=()):
        if op1 is None:
            S.op(eng, lambda e: e.tensor_scalar(out=out, in0=in0, scalar1=s1, scalar2=None, op0=op0), r, w)
        else:
            S.op(eng, lambda e: e.tensor_scalar(out=out, in0=in0, scalar1=s1, scalar2=s2, op0=op0, op1=op1), r, w)

    def STT(out, in0, scalar, in1, op0, op1, r=(), w=(), accum=None):
        if accum is None:
            S.op("dve", lambda e: e.scalar_tensor_tensor(out=out, in0=in0, scalar=scalar, in1=in1, op0=op0, op1=op1), r, w)
        else:
            S.op("dve", lambda e: e.scalar_tensor_tensor(out=out, in0=in0, scalar=scalar, in1=in1, op0=op0, op1=op1, accum_out=accum), r, w)

    def CP(eng, out, in_, r=(), w=()):
        S.op(eng, lambda e: e.tensor_copy(out=out, in_=in_), r, w)

    def RCP(out, in_, r=(), w=()):
        S.op("dve", lambda e: e.reciprocal(out=out, in_=in_), r, w)

    def MSET(eng, ap, val, w=()):
        S.op(eng, lambda e: e.memset(ap, val), (), w)

    def DMA(eng, out, in_, key, r=(), w=(), slow=False):
        if slow:
            S.dma(eng, lambda e: e.dma_start(out=out, in_=in_, allow_slow_non_contiguous=True), key, r, w)
        else:
            S.dma(eng, lambda e: e.dma_start(out=out, in_=in_), key, r, w)

    def ASEL(out, pattern, cmp, base_, cm, r=(), w=()):
        S.op("pool", lambda e: e.affine_select(out=out, in_=out, pattern=pattern, compare_op=cmp, fill=0.0,
                                               base=base_, channel_multiplier=cm), r, w)

    def rstd_from_ss(ss, out, n, keyr, keyw):
        ACT(out, ss, AF.Ln, r=keyr, w=[keyw], bias=EPS, scale=1.0 / n)
        ACT(out, out, AF.Exp, r=[keyw], w=[keyw], scale=-0.5)

    def tri_const(t, pattern, cmp, base_, cm, key):
        MSET("pool", t[:], 1.0, w=[key])
        ASEL(t[:], pattern, cmp, base_, cm, r=[key], w=[key])

    tri_const(identF, [[-1, 128]], ALU.is_equal, 0, 1, "identF")
    CP("dve", identB[:], identF[:], r=["identF"], w=["identB"])
    tri_const(TRI[0], [[1, 128]], ALU.is_ge, 0, -1, "TRI0")
    tri_const(TRI[1], [[-1, 128]], ALU.is_ge, 0, 1, "TRI1")
    tri_const(UU[0], [[-1, 128]], ALU.is_gt, 0, 1, "UU0")
    tri_const(UU[1], [[1, 128]], ALU.is_gt, 0, -1, "UU1")
    tri_const(M1[0], [[0, 128]], ALU.is_ge, 64, -1, "M10")
    tri_const(M1[1], [[0, 128]], ALU.is_ge, -63, 1, "M11")
    TT("dve", M1[0][:], TRI[0][:], M1[0][:], ALU.subtract, r=["TRI0", "M10"], w=["M10"])
    TT("dve", M1[1][:], TRI[1][:], M1[1][:], ALU.subtract, r=["TRI1", "M11"], w=["M11"])
    S.op("pool", lambda e: e.iota(iota16[:], pattern=[[1, 16]], base=0, channel_multiplier=0,
                                  allow_small_or_imprecise_dtypes=True), (), ["iota16"])
    if stop == "s1":
        S.emit(final_wait_keys=[])
        return nc
    DMA("sp", hgg_bc[:], hgg_d.partition_broadcast(128), "su", w=["hgg_bc"])
    DMA("sp", glg_bc[:], glg_d.partition_broadcast(128), "su", w=["glg_bc"])
    MSET("pool", wgr[:], 0.0, w=["wgr"])
    winv = win_d.rearrange("(c p) n -> p c n", p=128)
    DMA("pool", wgr[:, :, 0:16], winv[:, :, 8192:8208], "pw", r=["wgr"], w=["wgr"])
    DMA("pool", wgr[:, :, 32:48], winv[:, :, 8208:8224], "pw", r=["wgr"], w=["wgr"])
    MSET("dve", grT[:], 1.0, w=["grT"])
    MSET("dve", gkw[:], 0.0, w=["gkw"])
    DMA("sp", gkw[0:16, :], gkw_d[0], "su", r=["gkw"], w=["gkw"])
    DMA("sp", gkw[16:17, :], gkb_d[0:1, :], "su", w=["gkw"])
    DMA("sp", gkw[32:48, :], gkw_d[1], "su", w=["gkw"])
    DMA("sp", gkw[48:49, :], gkb_d[1:2, :], "su", w=["gkw"])
    if stop == "s2":
        S.emit(final_wait_keys=[])
        return nc
    for b_ in range(NS):
        DMA("sp", cT[:, :, b_:b_ + 1], c_d[b_:b_ + 1, :].rearrange("b (c p) -> p c b", p=128), "su", w=["cT"], slow=True)
    DMA("sp", cT[:, :, NS:NS + 1], cctx_d.rearrange("b (c p) -> p c b", p=128), "su", w=["cT"], slow=True)
    DMA("sp", adab_sb[:], adab_d.partition_broadcast(NS + 1), "su", w=["adab"])
    ACT(s_tmp[:], cT[:], AF.Exp, r=["cT"], w=["s_tmp"], scale=-1.0)
    TS("dve", s_tmp[:], s_tmp[:], 1.0, None, ALU.add, r=["s_tmp"], w=["s_tmp"])
    RCP(s_tmp[:], s_tmp[:], r=["s_tmp"], w=["s_tmp"])
    TT("dve", scT[:], cT[:], s_tmp[:], ALU.mult, r=["cT", "s_tmp"], w=["scT"])
    if stop == "s3":
        S.emit(final_wait_keys=[])
        return nc
    adawv = adaw_d.rearrange("(c p) n -> p c n", p=128)
    for n in range(12):
        ab = adaw[n % 2]
        DMA("sp", ab[:], adawv[:, :, n * 512:(n + 1) * 512], "aw", w=["adaw%d" % (n % 2)])
        for c in range(8):
            MM(psA[0:NS + 1, 0:512], scT[:, c, :], ab[:, c, :], start=(c == 0), stop=(c == 7),
               r=["scT", "adaw%d" % (n % 2)], w=["psA0"])
        TT("dve", modrows[:, n * 512:(n + 1) * 512], psA[0:NS + 1, 0:512], adab_sb[:, n * 512:(n + 1) * 512],
           ALU.add, r=["psA0", "adab"], w=["modrows"])
    if stop == "s4":
        S.emit(final_wait_keys=[])
        return nc
    DMA("sp", modscr, modrows[:], "ms", r=["modrows"], w=["modscr"])
    DMA("sp", modsm[:, 0, :], gmix_d.rearrange("o (c p) -> p (o c)", p=128), "ms2", w=["gmixT"], slow=True)
    DMA("sp", modsm[:, 2, :], modscr[NS:NS + 1, 0:D].rearrange("o (c p) -> p (o c)", p=128), "ms2",
        r=["modscr"], w=["B1Tc"], slow=True)
    DMA("sp", modsm[:, 5, :], modscr[NS:NS + 1, D:2 * D].rearrange("o (c p) -> p (o c)", p=128), "ms2",
        r=["modscr"], w=["mtmp"], slow=True)
    STT(modsm[:, 1, :], modsm[:, 5, :], 1.0, modsm[:, 0, :], ALU.add, ALU.mult, r=["mtmp", "gmixT"], w=["A1Tc"])
    if stop == "s5":
        S.emit(final_wait_keys=[])
        return nc
    DMA("sp", kraw[:, 0, :], k1_d, "su", w=["kraw"])
    DMA("sp", kraw[:, 1, :], k2_d, "su", w=["kraw"])
    for hf in range(2):
        TR(psB[:, hf * 128:(hf + 1) * 128], kraw[:, hf, :], identF[:], r=["kraw", "identF"], w=["psB0"])
    CP("dve", kT[:].rearrange("p a b -> p (a b)"), psB[:, 0:256], r=["psB0"], w=["kT"])
    wqv = wq_d.rearrange("(c p) n -> p c n", p=128)
    if stop == "s6":
        S.emit(final_wait_keys=[])
        return nc
    for g in range(16 if stop != "s7" else 1):
        wb_ = wqb[g % 2]
        DMA("sp", wb_[:], wqv[:, :, g * 128:(g + 1) * 128], "wq", w=["wqb%d" % (g % 2)])
        for c in range(8):
            i = (g * 8 + c) % 2
            TR(psC[:, i * 512:i * 512 + 128], wb_[:, c, :], identF[:], r=["wqb%d" % (g % 2), "identF"], w=["psC%d" % i])
            if i == 0:
                CP("dve", wqT[i][:], psC[:, i * 512:i * 512 + 128], r=["psC%d" % i], w=["wqT%d" % i])
            else:
                ACP(wqT[i][:], psC[:, i * 512:i * 512 + 128], r=["psC%d" % i], w=["wqT%d" % i])
            MM(psA[:, i * 512:i * 512 + 128], wqT[i][:], kT[:, g % 2, :], r=["wqT%d" % i, "kT"], w=["psA%d" % i])
            CP("dve", wpb[g % 2][:, c, :], psA[:, i * 512:i * 512 + 128], r=["psA%d" % i], w=["wpb%d" % (g % 2)])
        DMA("sp", wps_d[:, :, g * 128:(g + 1) * 128], wpb[g % 2][:], "wpo", r=["wpb%d" % (g % 2)], w=["wps_d"])
    S.barrier()
    if stop == "s7":
        S.emit(final_wait_keys=[])
        return nc
    ci = 0
    for ti, src_t in enumerate((pu_d, pv_d)):
        sv = src_t.rearrange("(n p r) d -> n p (r d)", p=128, r=4)
        dv = puv_d[:, ti * D:(ti + 1) * D].rearrange("(n p r) d -> n p r d", p=128, r=4)
        for n in range(32):
            i = ci % 2
            ci += 1
            DMA("sp", cst_f[i][:], sv[n], "cin", w=["cst_f%d" % i])
            if i == 0:
                ACP(cst_b[i][:], cst_f[i][:], r=["cst_f%d" % i], w=["cst_b%d" % i])
            else:
                CP("dve", cst_b[i][:], cst_f[i][:], r=["cst_f%d" % i], w=["cst_b%d" % i])
            DMA("sp", dv[n], cst_b[i][:].rearrange("p (r d) -> p r d", r=4), "cout", r=["cst_b%d" % i], w=["ptab"])
    S.barrier()
    if stop == "setup":
        if "modrows" in dbg_out:
            DMA("sp", dbg_out["modrows"], modscr, "dbg", r=["modscr"])
            DMA("sp", dbg_out["wps"], wps_d, "dbg", r=["wps_d"])
        S.emit(final_wait_keys=["dbg"] if dbg_out else [])
        return nc

    def phase_A(s):
        for t in range(NT):
            xb = xin[t % 2]
            xk = "xin%d" % (t % 2)
            src = ctx_d[s, t * 128:(t + 1) * 128, :] if t < NCT else x_d[s, (t - NCT) * 128:(t - NCT + 1) * 128, :]
            DMA("sp", xb[:], src, "xin", w=[xk])
            ACT(xs[:], xb[:], AF.Square, r=[xk], w=["xs", "ssA"], accum=small[:, 0:1])
            rstd_from_ss(small[:, 0:1], small[:, 1:2], D, ["ssA"], "rsA")
            AMUL(xs[:], xb[:], small[:, 1:2], r=[xk, "rsA"], w=["xs"])
            for c in range(8):
                TR(psA[:, c * 128:(c + 1) * 128], xs[:, c * 128:(c + 1) * 128], identF[:], r=["xs", "identF"], w=["psA%d" % (c // 4)])
            ai, bi = (1, 2) if t < NCT else (3, 4)
            an, bn = ("A1Tc", "B1Tc") if t < NCT else ("A1Ts", "B1Ts")
            for c in range(8):
                TS("dve", hT[:, c, t * 128:(t + 1) * 128], psA[:, c * 128:(c + 1) * 128], modsm[:, ai, c:c + 1], modsm[:, bi, c:c + 1],
                   ALU.mult, ALU.add, r=["psA%d" % (c // 4), an, bn], w=[("hT", t)])

    def load_seq_mod(s):
        DMA("sp", modsm[:, 4, :], modscr[s:s + 1, 0:D].rearrange("o (c p) -> p (o c)", p=128), "ms2",
            r=["modscr"], w=["B1Ts"], slow=True)
        DMA("sp", modsm[:, 5, :], modscr[s:s + 1, D:2 * D].rearrange("o (c p) -> p (o c)", p=128), "ms2",
            r=["modscr"], w=["mtmp"], slow=True)
        STT(modsm[:, 3, :], modsm[:, 5, :], 1.0, modsm[:, 0, :], ALU.add, ALU.mult, r=["mtmp", "gmixT"], w=["A1Ts"])

    def gr_prepass():
        for g0 in range(0, TOK, 512):
            n = min(512, TOK - g0)
            for c in range(8):
                MM(psC[0:64, 512:512 + n], wgr[:, c, :], hT[:, c, g0:g0 + n], start=(c == 0), stop=(c == 7),
                   r=["wgr"] + [("hT", t) for t in range(g0 // 128, (g0 + n) // 128)], w=["psC1"])
            CP("dve", grT[0:16, g0:g0 + n], psC[0:16, 512:512 + n], r=["psC1"], w=["grT"])
            ACP(grT[32:48, g0:g0 + n], psC[32:48, 512:512 + n], r=["psC1"], w=["grT"])

    def load_head_w(kind, hh, buf):
        wt = whd[buf]
        k = "whd%d" % buf
        if kind == "hg":
            cols = [(hh * 128, 128), (D + hh * 128, 128), (3 * D + hh * 128, 128), (2 * D + hh * 128, 128), (4 * D + hh * 128, 128)]
        else:
            cols = [(5120 + hh * 128, 128), (6144 + hh * 256, 256), (5632 + hh * 128, 128), (7168 + hh * 256, 256)]
        o = 0
        for c0, n in cols:
            DMA("pool", wt[:, :, o:o + n], winv[:, :, c0:c0 + n], "pw", w=[k])
            o += n

    def stage_F(kind, hh, buf, d, t, V, sl):
        wt = whd[buf]
        wk = "whd%d" % buf
        is_ctx = t < NCT
        tok = slice(t * 128, (t + 1) * 128)
        tk, tlg, tg = t_k[sl % 3], t_lg[sl % 3], t_gate[sl % 3]
        kk, kl, kg = "t_k%d" % (sl % 3), "t_lg%d" % (sl % 3), "t_gate%d" % (sl % 3)
        if kind == "hg":
            c0, ncol = (128, 256) if d == 0 else (384, 256)
        else:
            c0, ncol = (128, 384) if d == 0 else (384, 384)
        for c in range(8):
            MM(psB[:, 0:ncol], hT[:, c, tok], wt[:, c, c0:c0 + ncol], start=(c == 0), stop=(c == 7),
               r=[("hT", t), wk], w=["psB0"])
        if kind == "hg":
            ACT(t_e[:, 0:128], psB[:, 0:128], AF.Exp, r=["psB0"], w=["t_e"], scale=-1.0)
            if d == 0:
                ACP(vst[:, t, 0:128], psB[:, 128:256], r=["psB0"], w=[("vst", t)])
            TS("dve", t_e[:, 0:128], t_e[:, 0:128], 1.0, None, ALU.add, r=["t_e"], w=["t_e"])
            RCP(t_r[:, 0:128], t_e[:, 0:128], r=["t_e"], w=["t_r"])
            TS("dve", t_r[:, 0:128], t_r[:, 0:128], -1.0, 1.0, ALU.mult, ALU.add, r=["t_r"], w=["t_r"])
            TT("dve", tk[:], t_r[:, 0:128], oml[:, d, :], ALU.mult, r=["t_r", "oml"], w=[kk])
            TS("dve", t_f[:], tk[:], -1.0, 1.0, ALU.mult, ALU.add, r=[kk], w=["t_f"])
            ACT(tlg[:], t_f[:], AF.Ln, r=["t_f"], w=[kl])
            gsrc = psB[:, 128:256]
        else:
            if d == 0:
                ACP(vst[:, t, :], psB[:, 0:256], r=["psB0"], w=[("vst", t)])
                CP("dve", tk[:], psB[:, 256:384], r=["psB0"], w=[kk])
            else:
                CP("dve", tk[:], psB[:, 0:128], r=["psB0"], w=[kk])
            gsrc = psB[:, 128:384]
            pb = 32 * d
            MM(psC[:, 640:768], grT[pb:pb + 32, tok], gkw[pb:pb + 32, hh * 128:(hh + 1) * 128], r=["grT", "gkw"], w=["psC1"])
            ACT(t_e[:, 0:128], psC[:, 640:768], AF.Exp, r=["psC1"], w=["t_e"], scale=-1.0)
            ACT(tlg[:], t_e[:, 0:128], AF.Ln, r=["t_e"], w=[kl], bias=1.0)
        if d == 1 and not is_ctx:
            ACT(t_e[:, 0:V], gsrc, AF.Exp, r=["psB0"], w=["t_e"], scale=-1.0)
            TS("dve", t_e[:, 0:V], t_e[:, 0:V], 1.0, None, ALU.add, r=["t_e"], w=["t_e"])
            RCP(t_r[:, 0:V], t_e[:, 0:V], r=["t_e"], w=["t_r"])
            TT("dve", tg[:, 0:V], gsrc, t_r[:, 0:V], ALU.mult, r=["psB0", "t_r"], w=[kg])
            gbc = hgg_bc if kind == "hg" else glg_bc
            TT("dve", tg[:, 0:V], tg[:, 0:V], gbc[:, 0:V], ALU.mult, r=[kg, "hgg_bc", "glg_bc"], w=[kg])

    def stage_B1(kind, d, t, V, escale, sl):
        is_ctx = t < NCT
        lt = t - NCT
        tk, tlg = t_k[sl % 3], t_lg[sl % 3]
        kk, kl = "t_k%d" % (sl % 3), "t_lg%d" % (sl % 3)
        b = sl % 2
        P = t_P[b]
        pk = "P%d" % b
        if not is_ctx:
            MM(psC[:, 0:128], tlg[:], M1[d][:], r=[kl, "M1%d" % d], w=["psC0"])
        MM(psC[:, 128:256], tlg[:], TRI[d][:], r=[kl, "TRI%d" % d], w=["psC0"])
        MM(psC[:, 256:384], UU[d][:], tlg[:], r=[kl, "UU%d" % d], w=["psC0"])
        if not is_ctx:
            TR(psC[:, 384:512], tk[:], identF[:], r=[kk, "identF"], w=["psC0"])
            ACT(P[:, 0, :], psC[:, 0:128], AF.Exp, r=["psC0"], w=[pk + "a"], scale=escale)
            ACT(P[:, 1, :], psC[:, 0:128], AF.Exp, r=["psC0"], w=[pk + "b"], scale=-escale)
        ACT(P[:, 2, :], psC[:, 128:256], AF.Exp, r=["psC0"], w=[pk + "c"], scale=escale)
        ACT(P[:, 3, :], psC[:, 256:384], AF.Exp, r=["psC0"], w=[pk + "d"], scale=escale)
        if not is_ctx:
            qs = qT[:, lt * 128:(lt + 1) * 128]
            TT("dve", t_kdT[b][:], psC[:, 384:512], P[:, 1, :], ALU.mult, r=["psC0", pk + "b"], w=["t_kdT%d" % b])
            TT("dve", t_qd[b][:], qs, P[:, 0, :], ALU.mult, r=["qT", pk + "a"], w=["t_qd%d" % b])
            MM(psC[:, 512:640], t_kdT[b][:], t_qd[b][:], r=["t_kdT%d" % b, "t_qd%d" % b], w=["psC1"])
            TT("dve", t_qdec[b][:], qs, P[:, 2, :], ALU.mult, r=["qT", pk + "c"], w=["t_qdec%d" % b])
        TT("dve", t_kdec[b][:], tk[:], P[:, 3, :], ALU.mult, r=[kk, pk + "d"], w=["t_kdec%d" % b])
        if not is_ctx:
            S.op("dve", lambda e, d=d, b=b: e.copy_predicated(out=t_scm[d][b][:], mask=TRI[d][:].bitcast(U32), data=psC[:, 512:640]),
                 ["psC1", "TRI%d" % d], ["t_scm%d%d" % (d, b)])

    def stage_B2(kind, hh, d, t, V, sl):
        is_ctx = t < NCT
        lt = t - NCT
        b = sl % 2
        P = t_P[b]
        pk = "P%d" % b
        tg = t_gate[sl % 3]
        kg = "t_gate%d" % (sl % 3)
        if not is_ctx:
            MM(psD[:, 0:V], t_scm[d][b][:], vst[:, t, 0:V], start=True, stop=False, r=["t_scm%d%d" % (d, b), ("vst", t)], w=["psD"])
            MM(psD[:, 0:V], t_qdec[b][:], Sbf[:, 0:V], start=False, stop=True, r=["t_qdec%d" % b, "Sbf"], w=["psD"])
        MM(psD[:, 256:256 + V], t_kdec[b][:], vst[:, t, 0:V], r=["t_kdec%d" % b, ("vst", t)], w=["psD"])
        dcol = 127 if d == 0 else 0
        STT(St[:, 0:V], St[:, 0:V], P[:, 2, dcol:dcol + 1], psD[:, 256:256 + V], ALU.mult, ALU.add,
            r=["St", pk + "c", "psD"], w=["St"])
        if is_ctx:
            ACP(Sbf[:, 0:V], St[:, 0:V], r=["St"], w=["Sbf"])
            return
        if d == 0:
            ACP(ofw[:, lt, 0:V], psD[:, 0:V], r=["psD"], w=[("ofw", lt)])
            ACP(Sbf[:, 0:V], St[:, 0:V], r=["St"], w=["Sbf"])
            return
        TT("dve", t_o[:, 0:V], psD[:, 0:V], ofw[:, lt, 0:V], ALU.add, r=["psD", ("ofw", lt)], w=["t_o"])
        ACP(Sbf[:, 0:V], St[:, 0:V], r=["St"], w=["Sbf"])
        ACT(t_junk[:, 0:V], t_o[:, 0:V], AF.Square, r=["t_o"], w=["t_junk", "ssH"], accum=small[:, 2:3])
        rstd_from_ss(small[:, 2:3], small[:, 3:4], V, ["ssH"], "rsH")
        STT(t_y[:, 0:V], t_o[:, 0:V], small[:, 3:4], tg[:, 0:V], ALU.mult, ALU.mult, r=["t_o", "rsH", kg], w=["t_y"])
        dst = yThg if kind == "hg" else yTgl
        dk = "yThg" if kind == "hg" else "yTgl"
        for j in range(V // 128):
            TR(psE[:, j * 128:(j + 1) * 128], t_y[:, j * 128:(j + 1) * 128], identB[:], r=["t_y", "identB"], w=["psE"])
        for j in range(V // 128):
            ch = hh * (V // 128) + j
            ACP(dst[:, ch, lt * 128:(lt + 1) * 128], psE[:, j * 128:(j + 1) * 128], r=["psE"], w=[(dk, lt)])

    def head(kind, hh, buf, s):
        V = 128 if kind == "hg" else 256
        escale = 1.0 if kind == "hg" else -1.0 / 16.0
        wt = whd[buf]
        wk = "whd%d" % buf
        if kind == "hg":
            for d in range(2):
                for l in range(2):
                    DMA("sp", omt[:, d, l, :], lbl_d[l:l + 1, d * D + hh * 128:d * D + (hh + 1) * 128].partition_broadcast(128),
                        "su", w=["omt"])
            TT("dve", omt[:, :, 0, :], omt[:, :, 1, :], omt[:, :, 0, :], ALU.subtract, r=["omt"], w=["omt"])
            ACT(omt[:, :, 0, :], omt[:, :, 0, :], AF.Exp, r=["omt"], w=["omt"])
            TS("dve", omt[:, :, 1, :], omt[:, :, 0, :], 1.0, None, ALU.add, r=["omt"], w=["omt"])
            RCP(omt[:, :, 1, :], omt[:, :, 1, :], r=["omt"], w=["omt"])
            TT("dve", oml[:], omt[:, :, 0, :], omt[:, :, 1, :], ALU.mult, r=["omt"], w=["oml"])
        for g in range(4):
            for c in range(8):
                MM(psB[:, 512:1024], wt[:, c, 0:128], hT[:, c, CTX + g * 512:CTX + (g + 1) * 512], start=(c == 0), stop=(c == 7),
                   r=[wk] + [("hT", NCT + g * 4 + i) for i in range(4)], w=["psB1"])
            AMUL(qT[:, g * 512:(g + 1) * 512], psB[:, 512:1024], QSCALE, r=["psB1"], w=["qT"])
        for d in range(2):
            for b in range(2):
                MSET("dve", t_scm[d][b][:], 0.0, w=["t_scm%d%d" % (d, b)])
        for d in range(2):
            MSET("dve", St[:], 0.0, w=["St"])
            MSET("dve", Sbf[:], 0.0, w=["Sbf"])
            order = list(range(NT)) if d == 0 else [1, 0] + list(range(NT - 1, NCT - 1, -1))
            n = len(order)
            for i in range(n + 2):
                if i < n:
                    stage_F(kind, hh, buf, d, order[i], V, i)
                if 0 <= i - 1 < n:
                    stage_B1(kind, d, order[i - 1], V, escale, i - 1)
                if 0 <= i - 2 < n:
                    stage_B2(kind, hh, d, order[i - 2], V, i - 2)

    def phase_C1(s):
        DMA("pool", wbh[:], wbh_d.rearrange("(c p) n -> p c n", p=128), "pw", w=["wbh"])
        DMA("pool", wbg[:], wbg_d.rearrange("(c p) n -> p c n", p=128), "pw", w=["wbg"])
        DMA("pool", wm[:], winv[:, :, 8224:8224 + 2 * D], "pw", w=["wm"])
        for lt in range(NLT):
            t = lt + NCT
            tok = slice(lt * 128, (lt + 1) * 128)
            for hf in range(2):
                fs = slice(hf * 512, (hf + 1) * 512)
                for c in range(8):
                    MM(psA[:, 0:512], yThg[:, c, tok], wbh[:, c, fs], start=(c == 0), stop=(c == 7), r=[("yThg", lt), "wbh"], w=["psA0"])
                for c in range(8):
                    MM(psA[:, 512:1024], yTgl[:, c, tok], wbg[:, c, fs], start=(c == 0), stop=(c == 7), r=[("yTgl", lt), "wbg"], w=["psA1"])
                for c in range(8):
                    MM(psB[:, 0:512], hT[:, c, t * 128:(t + 1) * 128], wm[:, c, fs], start=(c == 0), stop=(c == 7), r=[("hT", t), "wm"], w=["psB0"])
                for c in range(8):
                    MM(psB[:, 512:1024], hT[:, c, t * 128:(t + 1) * 128], wm[:, c, D + hf * 512:D + (hf + 1) * 512],
                       start=(c == 0), stop=(c == 7), r=[("hT", t), "wm"], w=["psB1"])
                for i, (pm, py, pmk, pyk) in enumerate(((psB[:, 0:512], psA[:, 0:512], "psB0", "psA0"),
                                                        (psB[:, 512:1024], psA[:, 512:1024], "psB1", "psA1"))):
                    ACT(c_e[i][:], pm, AF.Exp, r=[pmk], w=["c_e%d" % i], scale=-1.0)
                    TS("dve", c_e[i][:], c_e[i][:], 1.0, None, ALU.add, r=["c_e%d" % i], w=["c_e%d" % i])
                    RCP(c_e[i][:], c_e[i][:], r=["c_e%d" % i], w=["c_e%d" % i])
                    TT("dve", c_t[i][:], py, c_e[i][:], ALU.mult, r=[pyk, "c_e%d" % i], w=["c_t%d" % i])
                TT("dve", c_ym[:, fs], c_t[0][:], c_t[1][:], ALU.add, r=["c_t0", "c_t1"], w=["c_ym"])
            for c in range(8):
                TR(psE[:, c * 128:(c + 1) * 128], c_ym[:, c * 128:(c + 1) * 128], identB[:], r=["c_ym", "identB"], w=["psE"])
            ACP(yThg[:, :, tok], psE[:].rearrange("p (c t) -> p c t", c=8), r=["psE"], w=[("yThg", lt)])

    def peer_topk():
        for g in range(16):
            sg = sc[:, g * 128:(g + 1) * 128]
            S.op("dve", lambda e, g=g, sg=sg: e.max(out=v16[:, g, 0:8], in_=sg), ["sc"], [("v16", g)])
            S.op("dve", lambda e, g=g, sg=sg: e.max_index(out=i16[:, g, 0:8], in_max=v16[:, g, 0:8], in_values=sg), ["sc", ("v16", g)], [("i16", g)])
            S.op("dve", lambda e, g=g, sg=sg: e.match_replace(out=scw[:, 0:128], in_to_replace=v16[:, g, 0:8], in_values=sg, imm_value=NEG),
                 ["sc", ("v16", g)], ["scw"])
            S.op("dve", lambda e, g=g: e.max(out=v16[:, g, 8:16], in_=scw[:, 0:128]), ["scw"], [("v16b", g)])
            S.op("dve", lambda e, g=g: e.max_index(out=i16[:, g, 8:16], in_max=v16[:, g, 8:16], in_values=scw[:, 0:128]),
                 ["scw", ("v16b", g)], [("i16b", g)])
        allv = [("v16", g) for g in range(16)] + [("v16b", g) for g in range(16)]
        alli = [("i16", g) for g in range(16)] + [("i16b", g) for g in range(16)]
        v16v = v16[:].rearrange("p (h f) k -> p h f k", f=2)
        for h in range(8):
            TT("dve", cand[:, h, :].rearrange("p (a b) -> p a b", a=16),
               v16[:, 2 * h, :].unsqueeze(2).to_broadcast([128, 16, 16]),
               v16[:, 2 * h + 1, :].unsqueeze(1).to_broadcast([128, 16, 16]), ALU.add, r=allv, w=[("cand", h)])
        for h in range(8):
            ch = cand[:, h, :]
            S.op("dve", lambda e, h=h, ch=ch: e.max(out=ts[:, h, 0:8], in_=ch), [("cand", h)], [("ts", h)])
            S.op("dve", lambda e, h=h, ch=ch: e.max_index(out=pos[:, h, 0:8], in_max=ts[:, h, 0:8], in_values=ch), [("cand", h), ("ts", h)], [("pos", h)])
            S.op("dve", lambda e, h=h, ch=ch: e.match_replace(out=scw[:, 0:256], in_to_replace=ts[:, h, 0:8], in_values=ch, imm_value=NEG),
                 [("cand", h), ("ts", h)], ["scw"])
            S.op("dve", lambda e, h=h: e.max(out=ts[:, h, 8:16], in_=scw[:, 0:256]), ["scw"], [("tsb", h)])
            S.op("dve", lambda e, h=h: e.max_index(out=pos[:, h, 8:16], in_max=ts[:, h, 8:16], in_values=scw[:, 0:256]),
                 ["scw", ("tsb", h)], [("posb", h)])
        allts = [("ts", h) for h in range(8)] + [("tsb", h) for h in range(8)]
        allpos = [("pos", h) for h in range(8)] + [("posb", h) for h in range(8)]
        pex3 = pex[:].rearrange("p (h k) -> p h k", h=8)
        TT("dve", pex3, ts[:], ts[:, :, 0:1].to_broadcast([128, 8, 16]), ALU.subtract, r=allts, w=["pex"])
        ACT(pex[:], pex[:], AF.Exp, r=["pex"], w=["pex"])
        S.op("dve", lambda e: e.tensor_reduce(out=psm[:, 0:8], in_=pex3, axis=AX.X, op=ALU.add), ["pex"], ["psm"])
        RCP(psm[:, 8:16], psm[:, 0:8], r=["psm"], w=["psm"])
        TT("dve", pgate[:].rearrange("p (h k) -> p h k", h=8), pex3, psm[:, 8:16].unsqueeze(2).to_broadcast([128, 8, 16]),
           ALU.mult, r=["pex", "psm"], w=["pgate"])
        posi = pos[:].bitcast(I32)
        S.op("dve", lambda e: e.tensor_single_scalar(out=pa[:, 0], in_=posi, scalar=4, op=ALU.logical_shift_right), allpos, ["pa0"])
        S.op("dve", lambda e: e.tensor_single_scalar(out=pa[:, 1], in_=posi, scalar=15, op=ALU.bitwise_and), allpos, ["pa1"])
        CP("dve", paf[:], pa[:], r=["pa0", "pa1"], w=["paf"])
        CP("dve", i16f[:], i16[:], r=alli, w=["i16f"])
        i16fv = i16f[:].rearrange("p (h f) k -> p h f k", f=2)
        for f in range(2):
            for h in range(8):
                ohh = oh[:, h, :].rearrange("p (r a) -> p r a", r=16)
                TT("dve", ohh, paf[:, f, h, :].unsqueeze(2).to_broadcast([128, 16, 16]),
                   iota16[:].unsqueeze(1).to_broadcast([128, 16, 16]), ALU.is_equal, r=["paf", "iota16"], w=[("oh", h)])
                TT("dve", ohh, ohh, i16f[:, 2 * h + f, :].unsqueeze(1).to_broadcast([128, 16, 16]), ALU.mult,
                   r=[("oh", h), "i16f"], w=[("oh", h)])
            S.op("dve", lambda e, f=f: e.tensor_reduce(out=isel[:, f, :], in_=oh[:].rearrange("p h (r a) -> p (h r) a", r=16),
                                                       axis=AX.X, op=ALU.add), [("oh", h) for h in range(8)], [("isel", f)])
        STT(eidx[:], isel[:, 0, :], 128.0, isel[:, 1, :], ALU.mult, ALU.add, r=[("isel", 0), ("isel", 1)], w=["eidx"])

    gctr = [0]

    def phase_D(s):
        DMA("sp", wps[:], wps_d, "dw", r=["wps_d"], w=["wps"])
        DMA("pool", wo[:], wo_d.rearrange("(c p) n -> p c n", p=128), "pw", w=["wo"])
        DMA("sp", G1[:], modscr[s:s + 1, 2 * D:3 * D].partition_broadcast(128), "dw", r=["modscr"], w=["G1"])
        DMA("sp", B2[:], modscr[s:s + 1, 3 * D:4 * D].partition_broadcast(128), "dw", r=["modscr"], w=["B2"])
        DMA("sp", A2[:], modscr[s:s + 1, 4 * D:5 * D].partition_broadcast(128), "dw", r=["modscr"], w=["A2"])
        DMA("sp", G2[:], modscr[s:s + 1, 5 * D:6 * D].partition_broadcast(128), "dw", r=["modscr"], w=["G2"])
        DMA("sp", FG[:], gffn_d.partition_broadcast(128), "dw", w=["FG"])
        STT(A2[:], A2[:], 1.0, FG[:], ALU.add, ALU.mult, r=["A2", "FG"], w=["A2"])
        DMA("sp", FG[:], fg_d.partition_broadcast(128), "dw", r=["A2"], w=["FG"])
        for lt in range(NLT):
            tok = slice(lt * 128, (lt + 1) * 128)
            DMA("sp", dxin[:], x_d[s, tok, :], "dx", w=["dxin"])
            for hf in range(2):
                fs = slice(hf * 512, (hf + 1) * 512)
                for c in range(8):
                    MM(psA[:, fs], yThg[:, c, tok], wo[:, c, fs], start=(c == 0), stop=(c == 7), r=[("yThg", lt), "wo"], w=["psA%d" % hf])
                TT("dve", x1[:, fs], psA[:, fs], G1[:, fs], ALU.mult, r=["psA%d" % hf, "G1"], w=[("x1", hf)])
                TT("dve", x1[:, fs], x1[:, fs], dxin[:, fs], ALU.add, r=[("x1", hf), "dxin"], w=[("x1", hf)])
            x1k = [("x1", 0), ("x1", 1)]
            if "x1" in dbg_out:
                DMA("sp", dbg_out["x1"][lt * 128:(lt + 1) * 128, :], x1[:], "dbg", r=x1k)
            ACT(djunk[:], x1[:], AF.Square, r=x1k, w=["djunk", "ssD"], accum=small[:, 4:5])
            rstd_from_ss(small[:, 4:5], small[:, 5:6], D, ["ssD"], "rsD")
            STT(h2[:], x1[:], small[:, 5:6], A2[:], ALU.mult, ALU.mult, r=x1k + ["rsD", "A2"], w=["h2"])
            TT("dve", h2[:], h2[:], B2[:], ALU.add, r=["h2", "B2"], w=["h2"])
            for c in range(8):
                TR(psA[:, c * 128:(c + 1) * 128], h2[:, c * 128:(c + 1) * 128], identF[:], r=["h2", "identF"], w=["psA%d" % (c // 4)])
            ACP(h2T[:].rearrange("p c t -> p (c t)"), psA[:], r=["psA0", "psA1"], w=["h2T"])
            for q in range(4):
                pt = (psB, psC)[q // 2]
                pk = ("psB0", "psB1", "psC0", "psC1")[q]
                for c in range(8):
                    MM(pt[:, (q % 2) * 512:(q % 2 + 1) * 512], h2T[:, c, :], wps[:, c, q * 512:(q + 1) * 512], start=(c == 0), stop=(c == 7),
                       r=["h2T", "wps"], w=[pk])
                if q % 2 == 0:
                    ACP(sc[:, q * 512:(q + 1) * 512], pt[:, 0:512], r=[pk], w=["sc"])
                else:
                    CP("dve", sc[:, q * 512:(q + 1) * 512], pt[:, 512:1024], r=[pk], w=["sc"])
            peer_topk()
            GRP = 4
            S.barrier()
            for g0 in range(0, 128, GRP):
                ks = []
                for j in range(g0, g0 + GRP):
                    k = gctr[0] % NGB
                    gctr[0] += 1
                    ks.append(k)
                    S.dma("pool", lambda e, k=k, j=j: e.indirect_dma_start(out=gb[k][:], out_offset=None, in_=puv_d,
                                                                           in_offset=bass.IndirectOffsetOnAxis(ap=eidx[:, j:j + 1], axis=0)),
                          "gb%d" % (k % 8), ["eidx", "ptab"], [("gb", k)])
                    STT(djunk[:], gb[k][:, 0:D], 1.0, h2[:], ALU.mult, ALU.mult, r=[("gb", k), "h2"], w=["djunk", ("pact", g0)],
                        accum=pact[:, j:j + 1])
                ACT(pact[:, g0:g0 + GRP], pact[:, g0:g0 + GRP], AF.Gelu, r=[("pact", g0)], w=[("pact", g0)])
                TT("dve", pact[:, g0:g0 + GRP], pact[:, g0:g0 + GRP], pgate[:, g0:g0 + GRP], ALU.mult, r=[("pact", g0), "pgate"], w=[("pact", g0)])
                for j, k in zip(range(g0, g0 + GRP), ks):
                    dk = j % NDG
                    AMUL(dg[dk][:], identF[:], pact[:, j:j + 1], r=["identF", ("pact", g0)], w=[("dg", dk)])
                    for hf in range(2):
                        MM(psA[:, hf * 512:(hf + 1) * 512], dg[dk][:], gb[k][:, D + hf * 512:D + (hf + 1) * 512], start=(j == 0), stop=(j == 127),
                           r=[("dg", dk), ("gb", k)], w=["psA%d" % hf])
            S.barrier()
            for hf in range(2):
                fs = slice(hf * 512, (hf + 1) * 512)
                TT("dve", acc[:, fs], psA[:, fs], G2[:, fs], ALU.mult, r=["psA%d" % hf, "G2", "acc"], w=["acc"])
                TT("dve", acc[:, fs], acc[:, fs], x1[:, fs], ALU.add, r=["acc", ("x1", hf)], w=["acc"])
            ACT(djunk[:], acc[:], AF.Square, r=["acc"], w=["djunk", "ssF"], accum=small[:, 6:7])
            rstd_from_ss(small[:, 6:7], small[:, 7:8], D, ["ssF"], "rsF")
            STT(h2[:], acc[:], small[:, 7:8], FG[:], ALU.mult, ALU.mult, r=["acc", "rsF", "FG"], w=["h2"])
            DMA("sp", out_d[s, tok, :], h2[:], "out", r=["h2"])

    def dump_mixer():
        S.barrier()
        if "yThg" in dbg_out:
            DMA("sp", dbg_out["yThg"], yThg[:], "dbg", r=[("yThg", lt) for lt in range(NLT)])
            DMA("sp", dbg_out["yTgl"], yTgl[:], "dbg", r=[("yTgl", lt) for lt in range(NLT)])
            DMA("sp", dbg_out["hT"], hT[:], "dbg", r=[("hT", t) for t in range(NT)])
        if "grT" in dbg_out:
            DMA("sp", dbg_out["grT"], grT[:], "dbg", r=["grT"])

    def finish():
        S.emit(final_wait_keys=["out"] + (["dbg"] if dbg_out else []))
        return nc

    for s in range(NS):
        load_seq_mod(s)
        phase_A(s)
        S.barrier()
        if stop == "A":
            dump_mixer()
            S.emit(final_wait_keys=["dbg"])
            return nc
        gr_prepass()
        S.barrier()
        if stop == "gr":
            dump_mixer()
            S.emit(final_wait_keys=["dbg"])
            return nc
        hi = 0
        heads = [("hg", h) for h in range(8)] + [("gl", h) for h in range(4)]
        if stop == "head0":
            heads = [("hg", 0)]
        if stop == "head8":
            heads = [("gl", 0)]
        load_head_w(heads[0][0], heads[0][1], 0)
        for hi, (kind, hh) in enumerate(heads):
            if hi + 1 < len(heads):
                load_head_w(heads[hi + 1][0], heads[hi + 1][1], (hi + 1) % 2)
            head(kind, hh, hi % 2, s)
        if s == 0 and (dbg_out or stop in ("head0", "head8", "heads")):
            dump_mixer()
        if stop in ("head0", "head8", "heads"):
            S.emit(final_wait_keys=["dbg"])
            return nc
        S.barrier()
        phase_C1(s)
        S.barrier()
        phase_D(s)
        S.barrier()
    S.emit(final_wait_keys=["out"] + (["dbg"] if dbg_out else []))
    return nc


_CACHE = {}


def _in_maps(inputs, NS, cores):
    f = lambda a: np.ascontiguousarray(a, dtype=np.float32)
    shared = {
        "c_ctx": f(inputs["c_ctx"]).reshape(1, D),
        "ada_w": f(inputs["ada_w"]).reshape(D, 6 * D),
        "ada_b": f(inputs["ada_b"]).reshape(1, 6 * D),
        "norm_mix_g": f(inputs["norm_mix_g"]).reshape(1, D),
        "w_in": f(inputs["w_in"]).reshape(D, D_IN),
        "lb_logits": f(inputs["hgrn_lb_logits"]).reshape(2, 2 * D),
        "hgrn_norm_g": f(inputs["hgrn_norm_g"]).reshape(1, 128),
        "gla_gk_w": f(inputs["gla_gk_w"]).reshape(2, 16, 512),
        "gla_gk_b": f(inputs["gla_gk_b"]).reshape(2, 512),
        "gla_norm_g": f(inputs["gla_norm_g"]).reshape(1, 256),
        "w_branch_hgrn": f(inputs["w_branch_hgrn"]).reshape(D, D),
        "w_branch_gla": f(inputs["w_branch_gla"]).reshape(D, D),
        "w_out": f(inputs["w_out"]).reshape(D, D),
        "norm_ffn_g": f(inputs["norm_ffn_g"]).reshape(1, D),
        "peer_wq": f(inputs["peer_wq"]).reshape(D, 2048),
        "peer_k1": f(inputs["peer_k1"]).reshape(128, 128),
        "peer_k2": f(inputs["peer_k2"]).reshape(128, 128),
        "peer_u": f(inputs["peer_u"]).reshape(16384, D),
        "peer_v": f(inputs["peer_v"]).reshape(16384, D),
        "final_g": f(inputs["final_g"]).reshape(1, D),
    }
    x = f(inputs["x"])
    ctx = f(inputs["ctx"])
    c = f(inputs["c"])
    maps = []
    for i in range(cores):
        m = dict(shared)
        m["x"] = x[i * NS:(i + 1) * NS]
        m["ctx"] = ctx[i * NS:(i + 1) * NS]
        m["c"] = c[i * NS:(i + 1) * NS]
        maps.append(m)
    return maps


def kernel(**inputs):
    B = inputs["x"].shape[0]
    NS = B // N_CORES
    if NS not in _CACHE:
        _CACHE[NS] = build_program(NS)
    nc = _CACHE[NS]
    maps = _in_maps(inputs, NS, N_CORES)
    res = run_bass_kernel_spmd(nc, maps, core_ids=list(range(N_CORES)))
    return np.concatenate([r["out"] for r in res.results], axis=0).astype(np.float32)
```

```python
import contextlib
import numpy as np
import concourse.bass as bass
import concourse.mybir as mybir
from concourse.bass_utils import run_bass_kernel_spmd

F32 = mybir.dt.float32
BF16 = mybir.dt.bfloat16
I32 = mybir.dt.int32
U32 = mybir.dt.uint32
AF = mybir.ActivationFunctionType
ALU = mybir.AluOpType
AX = mybir.AxisListType

N_CORES = 8
D = 1024
SEQ = 2048
CTX = 256
NLT = SEQ // 128
NCT = CTX // 128
NT = NLT + NCT
TOK = SEQ + CTX
D_IN = 10272
EPS = 1e-6
QSCALE = 128 ** -0.5
NEG = -1e30


class _Op:
    __slots__ = ("eng", "fn", "deps", "signal", "tok_sem", "tok_val", "is_dma", "dma_key")

    def __init__(self, eng, fn, is_dma=False, dma_key=None):
        self.eng = eng
        self.fn = fn
        self.deps = []
        self.signal = False
        self.tok_sem = None
        self.tok_val = 0
        self.is_dma = is_dma
        self.dma_key = dma_key


class Sched:
    ENGS = ("pe", "act", "dve", "pool", "sp")

    def __init__(self, nc):
        self.nc = nc
        self.ops = {e: [] for e in self.ENGS}
        self.last_writer = {}
        self.readers = {}
        self.last_eng = {}
        self.last_key = {}
        self.bar_deps = []
        self.bar_need = set()

    def barrier(self):
        self.bar_deps = list(self.last_eng.values()) + list(self.last_key.values())
        self.bar_need = set(self.ENGS)

    def _add(self, op, reads, writes):
        excl = [k for k in reads if isinstance(k, str) and k.startswith("ps")]
        if excl:
            reads = [k for k in reads if k not in excl]
            writes = list(writes) + [k for k in excl if k not in writes]
        deps = []
        if op.eng in self.bar_need:
            deps.extend(self.bar_deps)
            self.bar_need.discard(op.eng)
        for r in reads:
            w = self.last_writer.get(r)
            if w is not None:
                deps.append(w)
        for wkey in writes:
            w = self.last_writer.get(wkey)
            if w is not None:
                deps.append(w)
            deps.extend(self.readers.get(wkey, ()))
        seen = set()
        for d in deps:
            if d is op or id(d) in seen:
                continue
            seen.add(id(d))
            if (not d.is_dma) and (not op.is_dma) and d.eng == op.eng and op.eng == "pe":
                continue
            op.deps.append(d)
            d.signal = True
        for r in reads:
            self.readers.setdefault(r, []).append(op)
        for wkey in writes:
            self.last_writer[wkey] = op
            self.readers[wkey] = []
        self.ops[op.eng].append(op)
        if op.is_dma:
            self.last_key[op.dma_key] = op
        else:
            self.last_eng[op.eng] = op
        return op

    def op(self, eng, fn, reads=(), writes=()):
        return self._add(_Op(eng, fn), list(reads), list(writes))

    def dma(self, eng, fn, key, reads=(), writes=()):
        o = _Op(eng, fn, is_dma=True, dma_key=key)
        o.signal = True
        return self._add(o, list(reads), list(writes))

    def emit(self, final_wait_keys=()):
        nc = self.nc
        dma_keys = []
        for e in self.ENGS:
            for o in self.ops[e]:
                if o.is_dma and o.dma_key not in dma_keys:
                    dma_keys.append(o.dma_key)
        with contextlib.ExitStack() as st:
            eng_sem = {e: st.enter_context(nc.semaphore("S_" + e)) for e in self.ENGS}
            key_sem = {k: st.enter_context(nc.semaphore("D_%d" % i)) for i, k in enumerate(dma_keys)}
            key_cnt = {k: 0 for k in dma_keys}
            key_eng = {}
            for e in self.ENGS:
                cnt = 0
                for o in self.ops[e]:
                    if o.is_dma:
                        assert key_eng.setdefault(o.dma_key, e) == e, "dma key used from two queues"
                        key_cnt[o.dma_key] += 16
                        o.tok_sem = key_sem[o.dma_key]
                        o.tok_val = key_cnt[o.dma_key]
                    elif o.signal:
                        cnt += 1
                        o.tok_sem = eng_sem[e]
                        o.tok_val = cnt
            blk = st.enter_context(nc.Block())
            engobj = {"pe": "tensor", "act": "scalar", "dve": "vector", "pool": "gpsimd", "sp": "sync"}

            self.stats = {}

            def make(e):
                def body(eng):
                    waited = {}
                    nw = 0
                    for o in self.ops[e]:
                        for d in o.deps:
                            s = d.tok_sem
                            if waited.get(id(s), 0) >= d.tok_val:
                                continue
                            waited[id(s)] = d.tok_val
                            eng.wait_ge(s, d.tok_val)
                            nw += 1
                        ins = o.fn(eng)
                        if o.is_dma:
                            ins.then_inc(o.tok_sem, 16)
                        elif o.signal:
                            ins.then_inc(o.tok_sem, 1)
                    self.stats[e] = (len(self.ops[e]), nw)
                    if e == "sp":
                        for k in final_wait_keys:
                            if k in key_sem:
                                eng.wait_ge(key_sem[k], key_cnt[k])
                return body

            for e in self.ENGS:
                getattr(blk, engobj[e])(make(e))


def build_program(NS, dbg=(), stop=None):
    nc = bass.Bass("TRN2", target_bir_lowering=False)

    def din(name, shape, dt=F32):
        return nc.dram_tensor(name, list(shape), dt, kind="ExternalInput").ap()

    x_d = din("x", [NS, SEQ, D])
    ctx_d = din("ctx", [NS, CTX, D])
    c_d = din("c", [NS, D])
    cctx_d = din("c_ctx", [1, D])
    adaw_d = din("ada_w", [D, 6 * D])
    adab_d = din("ada_b", [1, 6 * D])
    gmix_d = din("norm_mix_g", [1, D])
    win_d = din("w_in", [D, D_IN])
    lbl_d = din("lb_logits", [2, 2 * D])
    hgg_d = din("hgrn_norm_g", [1, 128])
    gkw_d = din("gla_gk_w", [2, 16, 512])
    gkb_d = din("gla_gk_b", [2, 512])
    glg_d = din("gla_norm_g", [1, 256])
    wbh_d = din("w_branch_hgrn", [D, D])
    wbg_d = din("w_branch_gla", [D, D])
    wo_d = din("w_out", [D, D])
    gffn_d = din("norm_ffn_g", [1, D])
    wq_d = din("peer_wq", [D, 2048])
    k1_d = din("peer_k1", [128, 128])
    k2_d = din("peer_k2", [128, 128])
    pu_d = din("peer_u", [16384, D])
    pv_d = din("peer_v", [16384, D])
    fg_d = din("final_g", [1, D])
    out_d = nc.dram_tensor("out", [NS, SEQ, D], F32, kind="ExternalOutput").ap()
    modscr = nc.dram_tensor("modscr", [NS + 1, 6 * D], F32, kind="Internal").ap()
    wps_d = nc.dram_tensor("wps", [128, 8, 2048], BF16, kind="Internal").ap()
    puv_d = nc.dram_tensor("puv_bf", [16384, 2 * D], BF16, kind="Internal").ap()
    dbg_out = {}
    for name, shape, dt in dbg:
        dbg_out[name] = nc.dram_tensor("dbg_" + name, list(shape), dt, kind="ExternalOutput").ap()

    ARENA = 212000
    arena = nc.alloc_sbuf_tensor("arena", [128, ARENA // 4], F32)
    base = nc.lookup_mloc(arena).addr
    cur = {"p": base, "limit": base + ARENA}

    def _sz(shape, dt):
        n = int(np.prod(shape[1:])) * (2 if dt == BF16 else 4)
        return (n + 63) // 64 * 64

    def alloc(name, shape, dt, at=None):
        n = _sz(shape, dt)
        if at is None:
            off = cur["p"]
            cur["p"] += n
            assert cur["p"] <= cur["limit"], ("SBUF overflow", name, cur["p"] - base)
        else:
            off = at[0]
            at[0] += n
            assert at[0] <= cur["limit"], ("SBUF phase overflow", name, at[0] - base)
        return nc.alloc_sbuf_tensor_at(name, list(shape), dt, offset=off)

    identF = alloc("identF", [128, 128], F32)
    identB = alloc("identB", [128, 128], BF16)
    TRI = [alloc("TRIf", [128, 128], F32), alloc("TRIb", [128, 128], F32)]
    M1 = [alloc("M1f", [128, 128], F32), alloc("M1b", [128, 128], F32)]
    UU = [alloc("Uf", [128, 128], F32), alloc("Ub", [128, 128], F32)]
    hgg_bc = alloc("hgg_bc", [128, 128], F32)
    glg_bc = alloc("glg_bc", [128, 256], F32)
    modsm = alloc("modsm", [128, 6, 8], F32)
    small = alloc("small", [128, 16], F32)
    yThg = alloc("yThg", [128, 8, SEQ], BF16)
    wgr = alloc("wgr", [128, 8, 64], BF16)
    gkw = alloc("gkw", [64, 512], F32)
    grT = alloc("grT", [64, TOK], F32)
    iota16 = alloc("iota16", [128, 16], F32)
    PH = cur["p"]

    a = [PH]
    hT = alloc("hT", [128, 8, TOK], BF16, a)
    yTgl = alloc("yTgl", [128, 8, SEQ], BF16, a)
    MX = a[0]
    a = [MX]
    qT = alloc("qT", [128, SEQ], F32, a)
    vst = alloc("vst", [128, NT, 256], BF16, a)
    ofw = alloc("ofw", [128, NLT, 256], F32, a)
    whd = [alloc("whd0", [128, 8, 768], BF16, a), alloc("whd1", [128, 8, 768], BF16, a)]
    oml = alloc("oml", [128, 2, 128], F32, a)
    omt = alloc("omt", [128, 2, 2, 128], F32, a)
    t_e = alloc("t_e", [128, 256], F32, a)
    t_r = alloc("t_r", [128, 256], F32, a)
    t_k = [alloc("t_k%d" % i, [128, 128], F32, a) for i in range(3)]
    t_f = alloc("t_f", [128, 128], F32, a)
    t_lg = [alloc("t_lg%d" % i, [128, 128], F32, a) for i in range(3)]
    t_P = [alloc("t_P%d" % i, [128, 4, 128], F32, a) for i in range(2)]
    t_qd = [alloc("t_qd%d" % i, [128, 128], BF16, a) for i in range(2)]
    t_kdT = [alloc("t_kdT%d" % i, [128, 128], BF16, a) for i in range(2)]
    t_qdec = [alloc("t_qdec%d" % i, [128, 128], BF16, a) for i in range(2)]
    t_kdec = [alloc("t_kdec%d" % i, [128, 128], BF16, a) for i in range(2)]
    t_scm = [[alloc("t_scm%d%d" % (d_, i), [128, 128], BF16, a) for i in range(2)] for d_ in range(2)]
    St = alloc("St", [128, 256], F32, a)
    Sbf = alloc("Sbf", [128, 256], BF16, a)
    t_o = alloc("t_o", [128, 256], F32, a)
    t_gate = [alloc("t_gate%d" % i, [128, 256], F32, a) for i in range(3)]
    t_junk = alloc("t_junk", [128, 256], F32, a)
    t_y = alloc("t_y", [128, 256], BF16, a)
    a = [MX]
    xin = [alloc("xin0", [128, D], F32, a), alloc("xin1", [128, D], F32, a)]
    xs = alloc("xs", [128, D], F32, a)
    a = [MX]
    wbh = alloc("wbh", [128, 8, D], BF16, a)
    wbg = alloc("wbg", [128, 8, D], BF16, a)
    wm = alloc("wm", [128, 8, 2 * D], BF16, a)
    c_e = [alloc("c_e0", [128, 512], F32, a), alloc("c_e1", [128, 512], F32, a)]
    c_t = [alloc("c_t0", [128, 512], F32, a), alloc("c_t1", [128, 512], F32, a)]
    c_ym = alloc("c_ym", [128, D], BF16, a)
    a = [PH]
    s_l = alloc("s_l", [128, 2, 2 * D], F32, a)
    adaw = [alloc("adaw0", [128, 8, 512], F32, a), alloc("adaw1", [128, 8, 512], F32, a)]
    cT = alloc("cT", [128, 8, NS + 1], F32, a)
    scT = alloc("scT", [128, 8, NS + 1], F32, a)
    s_tmp = alloc("s_tmp", [128, 8, NS + 1], F32, a)
    modrows = alloc("modrows", [NS + 1, 6 * D], F32, a)
    adab_sb = alloc("adab_sb", [NS + 1, 6 * D], F32, a)
    wqb = [alloc("wqb0", [128, 8, 128], F32, a), alloc("wqb1", [128, 8, 128], F32, a)]
    kraw = alloc("kraw", [128, 2, 128], F32, a)
    kT = alloc("kT", [128, 2, 128], F32, a)
    wqT = [alloc("wqT0", [128, 128], F32, a), alloc("wqT1", [128, 128], F32, a)]
    wpb = [alloc("wpb0", [128, 8, 128], BF16, a), alloc("wpb1", [128, 8, 128], BF16, a)]
    a = [PH]
    cst_f = [alloc("cst_f0", [128, 4096], F32, a), alloc("cst_f1", [128, 4096], F32, a)]
    cst_b = [alloc("cst_b0", [128, 4096], BF16, a), alloc("cst_b1", [128, 4096], BF16, a)]
    a = [PH]
    wps = alloc("wps_sb", [128, 8, 2048], BF16, a)
    wo = alloc("wo", [128, 8, D], BF16, a)
    G1 = alloc("G1", [128, D], F32, a)
    A2 = alloc("A2", [128, D], F32, a)
    B2 = alloc("B2", [128, D], F32, a)
    G2 = alloc("G2", [128, D], F32, a)
    FG = alloc("FG", [128, D], F32, a)
    NGB = 14
    gb = [alloc("gb%d" % i, [128, 2 * D], BF16, a) for i in range(8)]
    NDG = 4
    dg = [alloc("dg%d" % i, [128, 128], BF16, a) for i in range(NDG)]
    sc = alloc("sc", [128, 2048], F32, a)
    scw = alloc("scw", [128, 256], F32, a)
    dxin = alloc("dxin", [128, D], F32, a)
    x1 = alloc("x1", [128, D], F32, a)
    h2 = alloc("h2", [128, D], F32, a)
    h2T = alloc("h2T", [128, 8, 128], BF16, a)
    acc = alloc("acc", [128, D], F32, a)
    djunk = alloc("djunk", [128, D], F32, a)
    v16 = alloc("v16", [128, 16, 16], F32, a)
    i16 = alloc("i16", [128, 16, 16], U32, a)
    i16f = alloc("i16f", [128, 16, 16], F32, a)
    cand = alloc("cand", [128, 8, 256], F32, a)
    ts = alloc("ts", [128, 8, 16], F32, a)
    pos = alloc("pos", [128, 8, 16], U32, a)
    pa = alloc("pa", [128, 2, 8, 16], I32, a)
    paf = alloc("paf", [128, 2, 8, 16], F32, a)
    oh = alloc("oh", [128, 8, 256], F32, a)
    isel = alloc("isel", [128, 2, 128], F32, a)
    eidx = alloc("eidx", [128, 128], I32, a)
    pgate = alloc("pgate", [128, 128], F32, a)
    pact = alloc("pact", [128, 128], F32, a)
    pex = alloc("pex", [128, 128], F32, a)
    psm = alloc("psm", [128, 16], F32, a)
    for t_ in (sc, cand, oh):
        o_ = [nc.lookup_mloc(t_).addr] if False else None
    def _off(t_):
        return t_.manual_sbuf_range[0]
    for t_ in (sc, cand, oh):
        aa = [_off(t_)]
        gb.append(alloc("gb%d" % len(gb), [128, 2 * D], BF16, aa))
        gb.append(alloc("gb%d" % len(gb), [128, 2 * D], BF16, aa))

    psA = nc.alloc_psum_tensor("psA", [128, 1024], F32)
    psB = nc.alloc_psum_tensor("psB", [128, 1024], F32)
    psC = nc.alloc_psum_tensor("psC", [128, 1024], F32)
    psD = nc.alloc_psum_tensor("psD", [128, 512], F32)
    psE = nc.alloc_psum_tensor("psE", [128, 1024], BF16)

    S = Sched(nc)

    def MM(out, lhsT, rhs, start=True, stop=True, r=(), w=()):
        S.op("pe", lambda e: e.matmul(out, lhsT=lhsT, rhs=rhs, start=start, stop=stop), r, w)

    def TR(out, in_, ident, r=(), w=()):
        S.op("pe", lambda e: e.transpose(out=out, in_=in_, identity=ident), r, w)

    def ACT(out, in_, func, r=(), w=(), bias=0.0, scale=1.0, accum=None):
        if accum is None:
            S.op("act", lambda e: e.activation(out=out, in_=in_, func=func, bias=bias, scale=scale), r, w)
        else:
            S.op("act", lambda e: e.activation(out=out, in_=in_, func=func, bias=bias, scale=scale, accum_out=accum), r, w)

    def ACP(out, in_, r=(), w=()):
        S.op("act", lambda e: e.copy(out=out, in_=in_), r, w)

    def AMUL(out, in_, mul, r=(), w=()):
        S.op("act", lambda e: e.mul(out=out, in_=in_, mul=mul), r, w)

    def TT(eng, out, in0, in1, op, r=(), w=()):
        S.op(eng, lambda e: e.tensor_tensor(out=out, in0=in0, in1=in1, op=op), r, w)

    def TS(eng, out, in0, s1, s2, op0, op1=None, r=(), w=()):
        if op1 is None:
            S.op(eng, lambda e: e.tensor_scalar(out=out, in0=in0, scalar1=s1, scalar2=None, op0=op0), r, w)
        else:
            S.op(eng, lambda e: e.tensor_scalar(out=out, in0=in0, scalar1=s1, scalar2=s2, op0=op0, op1=op1), r, w)

    def STT(out, in0, scalar, in1, op0, op1, r=(), w=(), accum=None):
        if accum is None:
            S.op("dve", lambda e: e.scalar_tensor_tensor(out=out, in0=in0, scalar=scalar, in1=in1, op0=op0, op1=op1), r, w)
        else:
            S.op("dve", lambda e: e.scalar_tensor_tensor(out=out, in0=in0, scalar=scalar, in1=in1, op0=op0, op1=op1, accum_out=accum), r, w)

    def CP(eng, out, in_, r=(), w=()):
        S.op(eng, lambda e: e.tensor_copy(out=out, in_=in_), r, w)

    def RCP(out, in_, r=(), w=()):
        S.op("dve", lambda e: e.reciprocal(out=out, in_=in_), r, w)

    def MSET(eng, ap, val, w=()):
        S.op(eng, lambda e: e.memset(ap, val), (), w)

    def DMA(eng, out, in_, key, r=(), w=(), slow=False):
        if slow:
            S.dma(eng, lambda e: e.dma_start(out=out, in_=in_, allow_slow_non_contiguous=True), key, r, w)
        else:
            S.dma(eng, lambda e: e.dma_start(out=out, in_=in_), key, r, w)

    def ASEL(out, pattern, cmp, base_, cm, r=(), w=()):
        S.op("pool", lambda e: e.affine_select(out=out, in_=out, pattern=pattern, compare_op=cmp, fill=0.0,
                                               base=base_, channel_multiplier=cm), r, w)

    def rstd_from_ss(ss, out, n, keyr, keyw):
        ACT(out, ss, AF.Ln, r=keyr, w=[keyw], bias=EPS, scale=1.0 / n)
        ACT(out, out, AF.Exp, r=[keyw], w=[keyw], scale=-0.5)

    def tri_const(t, pattern, cmp, base_, cm, key):
        MSET("pool", t[:], 1.0, w=[key])
        ASEL(t[:], pattern, cmp, base_, cm, r=[key], w=[key])

    tri_const(identF, [[-1, 128]], ALU.is_equal, 0, 1, "identF")
    CP("dve", identB[:], identF[:], r=["identF"], w=["identB"])
    tri_const(TRI[0], [[1, 128]], ALU.is_ge, 0, -1, "TRI0")
    tri_const(TRI[1], [[-1, 128]], ALU.is_ge, 0, 1, "TRI1")
    tri_const(UU[0], [[-1, 128]], ALU.is_gt, 0, 1, "UU0")
    tri_const(UU[1], [[1, 128]], ALU.is_gt, 0, -1, "UU1")
    tri_const(M1[0], [[0, 128]], ALU.is_ge, 64, -1, "M10")
    tri_const(M1[1], [[0, 128]], ALU.is_ge, -63, 1, "M11")
    TT("dve", M1[0][:], TRI[0][:], M1[0][:], ALU.subtract, r=["TRI0", "M10"], w=["M10"])
    TT("dve", M1[1][:], TRI[1][:], M1[1][:], ALU.subtract, r=["TRI1", "M11"], w=["M11"])
    S.op("pool", lambda e: e.iota(iota16[:], pattern=[[1, 16]], base=0, channel_multiplier=0,
                                  allow_small_or_imprecise_dtypes=True), (), ["iota16"])
    if stop == "s1":
        S.emit(final_wait_keys=[])
        return nc
    DMA("sp", hgg_bc[:], hgg_d.partition_broadcast(128), "su", w=["hgg_bc"])
    DMA("sp", glg_bc[:], glg_d.partition_broadcast(128), "su", w=["glg_bc"])
    MSET("pool", wgr[:], 0.0, w=["wgr"])
    winv = win_d.rearrange("(c p) n -> p c n", p=128)
    DMA("pool", wgr[:, :, 0:16], winv[:, :, 8192:8208], "pw", r=["wgr"], w=["wgr"])
    DMA("pool", wgr[:, :, 32:48], winv[:, :, 8208:8224], "pw", r=["wgr"], w=["wgr"])
    MSET("dve", grT[:], 1.0, w=["grT"])
    MSET("dve", gkw[:], 0.0, w=["gkw"])
    DMA("sp", gkw[0:16, :], gkw_d[0], "su", r=["gkw"], w=["gkw"])
    DMA("sp", gkw[16:17, :], gkb_d[0:1, :], "su", w=["gkw"])
    DMA("sp", gkw[32:48, :], gkw_d[1], "su", w=["gkw"])
    DMA("sp", gkw[48:49, :], gkb_d[1:2, :], "su", w=["gkw"])
    if stop == "s2":
        S.emit(final_wait_keys=[])
        return nc
    for b_ in range(NS):
        DMA("sp", cT[:, :, b_:b_ + 1], c_d[b_:b_ + 1, :].rearrange("b (c p) -> p c b", p=128), "su", w=["cT"], slow=True)
    DMA("sp", cT[:, :, NS:NS + 1], cctx_d.rearrange("b (c p) -> p c b", p=128), "su", w=["cT"], slow=True)
    DMA("sp", adab_sb[:], adab_d.partition_broadcast(NS + 1), "su", w=["adab"])
    ACT(s_tmp[:], cT[:], AF.Exp, r=["cT"], w=["s_tmp"], scale=-1.0)
    TS("dve", s_tmp[:], s_tmp[:], 1.0, None, ALU.add, r=["s_tmp"], w=["s_tmp"])
    RCP(s_tmp[:], s_tmp[:], r=["s_tmp"], w=["s_tmp"])
    TT("dve", scT[:], cT[:], s_tmp[:], ALU.mult, r=["cT", "s_tmp"], w=["scT"])
    if stop == "s3":
        S.emit(final_wait_keys=[])
        return nc
    adawv = adaw_d.rearrange("(c p) n -> p c n", p=128)
    for n in range(12):
        ab = adaw[n % 2]
        DMA("sp", ab[:], adawv[:, :, n * 512:(n + 1) * 512], "aw", w=["adaw%d" % (n % 2)])
        for c in range(8):
            MM(psA[0:NS + 1, 0:512], scT[:, c, :], ab[:, c, :], start=(c == 0), stop=(c == 7),
               r=["scT", "adaw%d" % (n % 2)], w=["psA0"])
        TT("dve", modrows[:, n * 512:(n + 1) * 512], psA[0:NS + 1, 0:512], adab_sb[:, n * 512:(n + 1) * 512],
           ALU.add, r=["psA0", "adab"], w=["modrows"])
    if stop == "s4":
        S.emit(final_wait_keys=[])
        return nc
    DMA("sp", modscr, modrows[:], "ms", r=["modrows"], w=["modscr"])
    DMA("sp", modsm[:, 0, :], gmix_d.rearrange("o (c p) -> p (o c)", p=128), "ms2", w=["gmixT"], slow=True)
    DMA("sp", modsm[:, 2, :], modscr[NS:NS + 1, 0:D].rearrange("o (c p) -> p (o c)", p=128), "ms2",
        r=["modscr"], w=["B1Tc"], slow=True)
    DMA("sp", modsm[:, 5, :], modscr[NS:NS + 1, D:2 * D].rearrange("o (c p) -> p (o c)", p=128), "ms2",
        r=["modscr"], w=["mtmp"], slow=True)
    STT(modsm[:, 1, :], modsm[:, 5, :], 1.0, modsm[:, 0, :], ALU.add, ALU.mult, r=["mtmp", "gmixT"], w=["A1Tc"])
    if stop == "s5":
        S.emit(final_wait_keys=[])
        return nc
    DMA("sp", kraw[:, 0, :], k1_d, "su", w=["kraw"])
    DMA("sp", kraw[:, 1, :], k2_d, "su", w=["kraw"])
    for hf in range(2):
        TR(psB[:, hf * 128:(hf + 1) * 128], kraw[:, hf, :], identF[:], r=["kraw", "identF"], w=["psB0"])
    CP("dve", kT[:].rearrange("p a b -> p (a b)"), psB[:, 0:256], r=["psB0"], w=["kT"])
    wqv = wq_d.rearrange("(c p) n -> p c n", p=128)
    if stop == "s6":
        S.emit(final_wait_keys=[])
        return nc
    for g in range(16 if stop != "s7" else 1):
        wb_ = wqb[g % 2]
        DMA("sp", wb_[:], wqv[:, :, g * 128:(g + 1) * 128], "wq", w=["wqb%d" % (g % 2)])
        for c in range(8):
            i = (g * 8 + c) % 2
            TR(psC[:, i * 512:i * 512 + 128], wb_[:, c, :], identF[:], r=["wqb%d" % (g % 2), "identF"], w=["psC%d" % i])
            if i == 0:
                CP("dve", wqT[i][:], psC[:, i * 512:i * 512 + 128], r=["psC%d" % i], w=["wqT%d" % i])
            else:
                ACP(wqT[i][:], psC[:, i * 512:i * 512 + 128], r=["psC%d" % i], w=["wqT%d" % i])
            MM(psA[:, i * 512:i * 512 + 128], wqT[i][:], kT[:, g % 2, :], r=["wqT%d" % i, "kT"], w=["psA%d" % i])
            CP("dve", wpb[g % 2][:, c, :], psA[:, i * 512:i * 512 + 128], r=["psA%d" % i], w=["wpb%d" % (g % 2)])
        DMA("sp", wps_d[:, :, g * 128:(g + 1) * 128], wpb[g % 2][:], "wpo", r=["wpb%d" % (g % 2)], w=["wps_d"])
    S.barrier()
    if stop == "s7":
        S.emit(final_wait_keys=[])
        return nc
    ci = 0
    for ti, src_t in enumerate((pu_d, pv_d)):
        sv = src_t.rearrange("(n p r) d -> n p (r d)", p=128, r=4)
        dv = puv_d[:, ti * D:(ti + 1) * D].rearrange("(n p r) d -> n p r d", p=128, r=4)
        for n in range(32):
            i = ci % 2
            ci += 1
            DMA("sp", cst_f[i][:], sv[n], "cin", w=["cst_f%d" % i])
            if i == 0:
                ACP(cst_b[i][:], cst_f[i][:], r=["cst_f%d" % i], w=["cst_b%d" % i])
            else:
                CP("dve", cst_b[i][:], cst_f[i][:], r=["cst_f%d" % i], w=["cst_b%d" % i])
            DMA("sp", dv[n], cst_b[i][:].rearrange("p (r d) -> p r d", r=4), "cout", r=["cst_b%d" % i], w=["ptab"])
    S.barrier()
    if stop == "setup":
        if "modrows" in dbg_out:
            DMA("sp", dbg_out["modrows"], modscr, "dbg", r=["modscr"])
            DMA("sp", dbg_out["wps"], wps_d, "dbg", r=["wps_d"])
        S.emit(final_wait_keys=["dbg"] if dbg_out else [])
        return nc

    def phase_A(s):
        for t in range(NT):
            xb = xin[t % 2]
            xk = "xin%d" % (t % 2)
            src = ctx_d[s, t * 128:(t + 1) * 128, :] if t < NCT else x_d[s, (t - NCT) * 128:(t - NCT + 1) * 128, :]
            DMA("sp", xb[:], src, "xin", w=[xk])
            ACT(xs[:], xb[:], AF.Square, r=[xk], w=["xs", "ssA"], accum=small[:, 0:1])
            rstd_from_ss(small[:, 0:1], small[:, 1:2], D, ["ssA"], "rsA")
            AMUL(xs[:], xb[:], small[:, 1:2], r=[xk, "rsA"], w=["xs"])
            for c in range(8):
                TR(psA[:, c * 128:(c + 1) * 128], xs[:, c * 128:(c + 1) * 128], identF[:], r=["xs", "identF"], w=["psA%d" % (c // 4)])
            ai, bi = (1, 2) if t < NCT else (3, 4)
            an, bn = ("A1Tc", "B1Tc") if t < NCT else ("A1Ts", "B1Ts")
            for c in range(8):
                TS("dve", hT[:, c, t * 128:(t + 1) * 128], psA[:, c * 128:(c + 1) * 128], modsm[:, ai, c:c + 1], modsm[:, bi, c:c + 1],
                   ALU.mult, ALU.add, r=["psA%d" % (c // 4), an, bn], w=[("hT", t)])

    def load_seq_mod(s):
        DMA("sp", modsm[:, 4, :], modscr[s:s + 1, 0:D].rearrange("o (c p) -> p (o c)", p=128), "ms2",
            r=["modscr"], w=["B1Ts"], slow=True)
        DMA("sp", modsm[:, 5, :], modscr[s:s + 1, D:2 * D].rearrange("o (c p) -> p (o c)", p=128), "ms2",
            r=["modscr"], w=["mtmp"], slow=True)
        STT(modsm[:, 3, :], modsm[:, 5, :], 1.0, modsm[:, 0, :], ALU.add, ALU.mult, r=["mtmp", "gmixT"], w=["A1Ts"])

    def gr_prepass():
        for g0 in range(0, TOK, 512):
            n = min(512, TOK - g0)
            for c in range(8):
                MM(psC[0:64, 512:512 + n], wgr[:, c, :], hT[:, c, g0:g0 + n], start=(c == 0), stop=(c == 7),
                   r=["wgr"] + [("hT", t) for t in range(g0 // 128, (g0 + n) // 128)], w=["psC1"])
            CP("dve", grT[0:16, g0:g0 + n], psC[0:16, 512:512 + n], r=["psC1"], w=["grT"])
            ACP(grT[32:48, g0:g0 + n], psC[32:48, 512:512 + n], r=["psC1"], w=["grT"])

    def load_head_w(kind, hh, buf):
        wt = whd[buf]
        k = "whd%d" % buf
        if kind == "hg":
            cols = [(hh * 128, 128), (D + hh * 128, 128), (3 * D + hh * 128, 128), (2 * D + hh * 128, 128), (4 * D + hh * 128, 128)]
        else:
            cols = [(5120 + hh * 128, 128), (6144 + hh * 256, 256), (5632 + hh * 128, 128), (7168 + hh * 256, 256)]
        o = 0
        for c0, n in cols:
            DMA("pool", wt[:, :, o:o + n], winv[:, :, c0:c0 + n], "pw", w=[k])
            o += n

    def stage_F(kind, hh, buf, d, t, V, sl):
        wt = whd[buf]
        wk = "whd%d" % buf
        is_ctx = t < NCT
        tok = slice(t * 128, (t + 1) * 128)
        tk, tlg, tg = t_k[sl % 3], t_lg[sl % 3], t_gate[sl % 3]
        kk, kl, kg = "t_k%d" % (sl % 3), "t_lg%d" % (sl % 3), "t_gate%d" % (sl % 3)
        if kind == "hg":
            c0, ncol = (128, 256) if d == 0 else (384, 256)
        else:
            c0, ncol = (128, 384) if d == 0 else (384, 384)
        for c in range(8):
            MM(psB[:, 0:ncol], hT[:, c, tok], wt[:, c, c0:c0 + ncol], start=(c == 0), stop=(c == 7),
               r=[("hT", t), wk], w=["psB0"])
        if kind == "hg":
            ACT(t_e[:, 0:128], psB[:, 0:128], AF.Exp, r=["psB0"], w=["t_e"], scale=-1.0)
            if d == 0:
                ACP(vst[:, t, 0:128], psB[:, 128:256], r=["psB0"], w=[("vst", t)])
            TS("dve", t_e[:, 0:128], t_e[:, 0:128], 1.0, None, ALU.add, r=["t_e"], w=["t_e"])
            RCP(t_r[:, 0:128], t_e[:, 0:128], r=["t_e"], w=["t_r"])
            TS("dve", t_r[:, 0:128], t_r[:, 0:128], -1.0, 1.0, ALU.mult, ALU.add, r=["t_r"], w=["t_r"])
            TT("dve", tk[:], t_r[:, 0:128], oml[:, d, :], ALU.mult, r=["t_r", "oml"], w=[kk])
            TS("dve", t_f[:], tk[:], -1.0, 1.0, ALU.mult, ALU.add, r=[kk], w=["t_f"])
            ACT(tlg[:], t_f[:], AF.Ln, r=["t_f"], w=[kl])
            gsrc = psB[:, 128:256]
        else:
            if d == 0:
                ACP(vst[:, t, :], psB[:, 0:256], r=["psB0"], w=[("vst", t)])
                CP("dve", tk[:], psB[:, 256:384], r=["psB0"], w=[kk])
            else:
                CP("dve", tk[:], psB[:, 0:128], r=["psB0"], w=[kk])
            gsrc = psB[:, 128:384]
            pb = 32 * d
            MM(psC[:, 640:768], grT[pb:pb + 32, tok], gkw[pb:pb + 32, hh * 128:(hh + 1) * 128], r=["grT", "gkw"], w=["psC1"])
            ACT(t_e[:, 0:128], psC[:, 640:768], AF.Exp, r=["psC1"], w=["t_e"], scale=-1.0)
            ACT(tlg[:], t_e[:, 0:128], AF.Ln, r=["t_e"], w=[kl], bias=1.0)
        if d == 1 and not is_ctx:
            ACT(t_e[:, 0:V], gsrc, AF.Exp, r=["psB0"], w=["t_e"], scale=-1.0)
            TS("dve", t_e[:, 0:V], t_e[:, 0:V], 1.0, None, ALU.add, r=["t_e"], w=["t_e"])
            RCP(t_r[:, 0:V], t_e[:, 0:V], r=["t_e"], w=["t_r"])
            TT("dve", tg[:, 0:V], gsrc, t_r[:, 0:V], ALU.mult, r=["psB0", "t_r"], w=[kg])
            gbc = hgg_bc if kind == "hg" else glg_bc
            TT("dve", tg[:, 0:V], tg[:, 0:V], gbc[:, 0:V], ALU.mult, r=[kg, "hgg_bc", "glg_bc"], w=[kg])

    def stage_B1(kind, d, t, V, escale, sl):
        is_ctx = t < NCT
        lt = t - NCT
        tk, tlg = t_k[sl % 3], t_lg[sl % 3]
        kk, kl = "t_k%d" % (sl % 3), "t_lg%d" % (sl % 3)
        b = sl % 2
        P = t_P[b]
        pk = "P%d" % b
        if not is_ctx:
            MM(psC[:, 0:128], tlg[:], M1[d][:], r=[kl, "M1%d" % d], w=["psC0"])
        MM(psC[:, 128:256], tlg[:], TRI[d][:], r=[kl, "TRI%d" % d], w=["psC0"])
        MM(psC[:, 256:384], UU[d][:], tlg[:], r=[kl, "UU%d" % d], w=["psC0"])
        if not is_ctx:
            TR(psC[:, 384:512], tk[:], identF[:], r=[kk, "identF"], w=["psC0"])
            ACT(P[:, 0, :], psC[:, 0:128], AF.Exp, r=["psC0"], w=[pk + "a"], scale=escale)
            ACT(P[:, 1, :], psC[:, 0:128], AF.Exp, r=["psC0"], w=[pk + "b"], scale=-escale)
        ACT(P[:, 2, :], psC[:, 128:256], AF.Exp, r=["psC0"], w=[pk + "c"], scale=escale)
        ACT(P[:, 3, :], psC[:, 256:384], AF.Exp, r=["psC0"], w=[pk + "d"], scale=escale)
        if not is_ctx:
            qs = qT[:, lt * 128:(lt + 1) * 128]
            TT("dve", t_kdT[b][:], psC[:, 384:512], P[:, 1, :], ALU.mult, r=["psC0", pk + "b"], w=["t_kdT%d" % b])
            TT("dve", t_qd[b][:], qs, P[:, 0, :], ALU.mult, r=["qT", pk + "a"], w=["t_qd%d" % b])
            TT("dve", t_qdec[b][:], qs, P[:, 2, :], ALU.mult, r=["qT", pk + "c"], w=["t_qdec%d" % b])
        TT("dve", t_kdec[b][:], tk[:], P[:, 3, :], ALU.mult, r=[kk, pk + "d"], w=["t_kdec%d" % b])

    def stage_B1b(kind, d, t, V, escale, sl):
        is_ctx = t < NCT
        b = sl % 2
        if not is_ctx:
            MM(psC[:, 512:640], t_kdT[b][:], t_qd[b][:], r=["t_kdT%d" % b, "t_qd%d" % b], w=["psC1"])
            S.op("dve", lambda e, d=d, b=b: e.copy_predicated(out=t_scm[d][b][:], mask=TRI[d][:].bitcast(U32), data=psC[:, 512:640]),
                 ["psC1", "TRI%d" % d], ["t_scm%d%d" % (d, b)])

    def stage_B2(kind, hh, d, t, V, sl):
        is_ctx = t < NCT
        lt = t - NCT
        b = sl % 2
        P = t_P[b]
        pk = "P%d" % b
        tg = t_gate[sl % 3]
        kg = "t_gate%d" % (sl % 3)
        if not is_ctx:
            MM(psD[:, 0:V], t_scm[d][b][:], vst[:, t, 0:V], start=True, stop=False, r=["t_scm%d%d" % (d, b), ("vst", t)], w=["psD"])
            MM(psD[:, 0:V], t_qdec[b][:], Sbf[:, 0:V], start=False, stop=True, r=["t_qdec%d" % b, "Sbf"], w=["psD"])
        MM(psD[:, 256:256 + V], t_kdec[b][:], vst[:, t, 0:V], r=["t_kdec%d" % b, ("vst", t)], w=["psD"])
        dcol = 127 if d == 0 else 0
        STT(St[:, 0:V], St[:, 0:V], P[:, 2, dcol:dcol + 1], psD[:, 256:256 + V], ALU.mult, ALU.add,
            r=["St", pk + "c", "psD"], w=["St"])
        if is_ctx:
            ACP(Sbf[:, 0:V], St[:, 0:V], r=["St"], w=["Sbf"])
            return
        if d == 0:
            ACP(ofw[:, lt, 0:V], psD[:, 0:V], r=["psD"], w=[("ofw", lt)])
            ACP(Sbf[:, 0:V], St[:, 0:V], r=["St"], w=["Sbf"])
            return
        TT("dve", t_o[:, 0:V], psD[:, 0:V], ofw[:, lt, 0:V], ALU.add, r=["psD", ("ofw", lt)], w=["t_o"])
        ACP(Sbf[:, 0:V], St[:, 0:V], r=["St"], w=["Sbf"])
        ACT(t_junk[:, 0:V], t_o[:, 0:V], AF.Square, r=["t_o"], w=["t_junk", "ssH"], accum=small[:, 2:3])
        rstd_from_ss(small[:, 2:3], small[:, 3:4], V, ["ssH"], "rsH")
        STT(t_y[:, 0:V], t_o[:, 0:V], small[:, 3:4], tg[:, 0:V], ALU.mult, ALU.mult, r=["t_o", "rsH", kg], w=["t_y"])
        dst = yThg if kind == "hg" else yTgl
        dk = "yThg" if kind == "hg" else "yTgl"
        for j in range(V // 128):
            TR(psE[:, j * 128:(j + 1) * 128], t_y[:, j * 128:(j + 1) * 128], identB[:], r=["t_y", "identB"], w=["psE"])
        for j in range(V // 128):
            ch = hh * (V // 128) + j
            ACP(dst[:, ch, lt * 128:(lt + 1) * 128], psE[:, j * 128:(j + 1) * 128], r=["psE"], w=[(dk, lt)])

    def head(kind, hh, buf, s):
        V = 128 if kind == "hg" else 256
        escale = 1.0 if kind == "hg" else -1.0 / 16.0
        wt = whd[buf]
        wk = "whd%d" % buf
        if kind == "hg":
            for d in range(2):
                for l in range(2):
                    DMA("sp", omt[:, d, l, :], lbl_d[l:l + 1, d * D + hh * 128:d * D + (hh + 1) * 128].partition_broadcast(128),
                        "su", w=["omt"])
            TT("dve", omt[:, :, 0, :], omt[:, :, 1, :], omt[:, :, 0, :], ALU.subtract, r=["omt"], w=["omt"])
            ACT(omt[:, :, 0, :], omt[:, :, 0, :], AF.Exp, r=["omt"], w=["omt"])
            TS("dve", omt[:, :, 1, :], omt[:, :, 0, :], 1.0, None, ALU.add, r=["omt"], w=["omt"])
            RCP(omt[:, :, 1, :], omt[:, :, 1, :], r=["omt"], w=["omt"])
            TT("dve", oml[:], omt[:, :, 0, :], omt[:, :, 1, :], ALU.mult, r=["omt"], w=["oml"])
        for g in range(4):
            for c in range(8):
                MM(psB[:, 512:1024], wt[:, c, 0:128], hT[:, c, CTX + g * 512:CTX + (g + 1) * 512], start=(c == 0), stop=(c == 7),
                   r=[wk] + [("hT", NCT + g * 4 + i) for i in range(4)], w=["psB1"])
            AMUL(qT[:, g * 512:(g + 1) * 512], psB[:, 512:1024], QSCALE, r=["psB1"], w=["qT"])
        for d in range(2):
            for b in range(2):
                MSET("dve", t_scm[d][b][:], 0.0, w=["t_scm%d%d" % (d, b)])
        for d in range(2):
            MSET("dve", St[:], 0.0, w=["St"])
            MSET("dve", Sbf[:], 0.0, w=["Sbf"])
            order = list(range(NT)) if d == 0 else [1, 0] + list(range(NT - 1, NCT - 1, -1))
            n = len(order)
            for i in range(n + 3):
                if 0 <= i - 3 < n:
                    stage_B2(kind, hh, d, order[i - 3], V, i - 3)
                if 0 <= i - 2 < n:
                    stage_B1b(kind, d, order[i - 2], V, escale, i - 2)
                if 0 <= i - 1 < n:
                    stage_B1(kind, d, order[i - 1], V, escale, i - 1)
                if i < n:
                    stage_F(kind, hh, buf, d, order[i], V, i)

    def phase_C1(s):
        DMA("pool", wbh[:], wbh_d.rearrange("(c p) n -> p c n", p=128), "pw", w=["wbh"])
        DMA("pool", wbg[:], wbg_d.rearrange("(c p) n -> p c n", p=128), "pw", w=["wbg"])
        DMA("pool", wm[:], winv[:, :, 8224:8224 + 2 * D], "pw", w=["wm"])
        for lt in range(NLT):
            t = lt + NCT
            tok = slice(lt * 128, (lt + 1) * 128)
            for hf in range(2):
                fs = slice(hf * 512, (hf + 1) * 512)
                for c in range(8):
                    MM(psA[:, 0:512], yThg[:, c, tok], wbh[:, c, fs], start=(c == 0), stop=(c == 7), r=[("yThg", lt), "wbh"], w=["psA0"])
                for c in range(8):
                    MM(psA[:, 512:1024], yTgl[:, c, tok], wbg[:, c, fs], start=(c == 0), stop=(c == 7), r=[("yTgl", lt), "wbg"], w=["psA1"])
                for c in range(8):
                    MM(psB[:, 0:512], hT[:, c, t * 128:(t + 1) * 128], wm[:, c, fs], start=(c == 0), stop=(c == 7), r=[("hT", t), "wm"], w=["psB0"])
                for c in range(8):
                    MM(psB[:, 512:1024], hT[:, c, t * 128:(t + 1) * 128], wm[:, c, D + hf * 512:D + (hf + 1) * 512],
                       start=(c == 0), stop=(c == 7), r=[("hT", t), "wm"], w=["psB1"])
                for i, (pm, py, pmk, pyk) in enumerate(((psB[:, 0:512], psA[:, 0:512], "psB0", "psA0"),
                                                        (psB[:, 512:1024], psA[:, 512:1024], "psB1", "psA1"))):
                    ACT(c_e[i][:], pm, AF.Exp, r=[pmk], w=["c_e%d" % i], scale=-1.0)
                    TS("dve", c_e[i][:], c_e[i][:], 1.0, None, ALU.add, r=["c_e%d" % i], w=["c_e%d" % i])
                    RCP(c_e[i][:], c_e[i][:], r=["c_e%d" % i], w=["c_e%d" % i])
                    TT("dve", c_t[i][:], py, c_e[i][:], ALU.mult, r=[pyk, "c_e%d" % i], w=["c_t%d" % i])
                TT("dve", c_ym[:, fs], c_t[0][:], c_t[1][:], ALU.add, r=["c_t0", "c_t1"], w=["c_ym"])
            for c in range(8):
                TR(psE[:, c * 128:(c + 1) * 128], c_ym[:, c * 128:(c + 1) * 128], identB[:], r=["c_ym", "identB"], w=["psE"])
            ACP(yThg[:, :, tok], psE[:].rearrange("p (c t) -> p c t", c=8), r=["psE"], w=[("yThg", lt)])

    def peer_topk():
        for g in range(16):
            sg = sc[:, g * 128:(g + 1) * 128]
            S.op("dve", lambda e, g=g, sg=sg: e.max(out=v16[:, g, 0:8], in_=sg), ["sc"], [("v16", g)])
            S.op("dve", lambda e, g=g, sg=sg: e.max_index(out=i16[:, g, 0:8], in_max=v16[:, g, 0:8], in_values=sg), ["sc", ("v16", g)], [("i16", g)])
            S.op("dve", lambda e, g=g, sg=sg: e.match_replace(out=scw[:, 0:128], in_to_replace=v16[:, g, 0:8], in_values=sg, imm_value=NEG),
                 ["sc", ("v16", g)], ["scw"])
            S.op("dve", lambda e, g=g: e.max(out=v16[:, g, 8:16], in_=scw[:, 0:128]), ["scw"], [("v16b", g)])
            S.op("dve", lambda e, g=g: e.max_index(out=i16[:, g, 8:16], in_max=v16[:, g, 8:16], in_values=scw[:, 0:128]),
                 ["scw", ("v16b", g)], [("i16b", g)])
        allv = [("v16", g) for g in range(16)] + [("v16b", g) for g in range(16)]
        alli = [("i16", g) for g in range(16)] + [("i16b", g) for g in range(16)]
        v16v = v16[:].rearrange("p (h f) k -> p h f k", f=2)
        for h in range(8):
            TT("dve", cand[:, h, :].rearrange("p (a b) -> p a b", a=16),
               v16[:, 2 * h, :].unsqueeze(2).to_broadcast([128, 16, 16]),
               v16[:, 2 * h + 1, :].unsqueeze(1).to_broadcast([128, 16, 16]), ALU.add, r=allv, w=[("cand", h)])
        for h in range(8):
            ch = cand[:, h, :]
            S.op("dve", lambda e, h=h, ch=ch: e.max(out=ts[:, h, 0:8], in_=ch), [("cand", h)], [("ts", h)])
            S.op("dve", lambda e, h=h, ch=ch: e.max_index(out=pos[:, h, 0:8], in_max=ts[:, h, 0:8], in_values=ch), [("cand", h), ("ts", h)], [("pos", h)])
            S.op("dve", lambda e, h=h, ch=ch: e.match_replace(out=scw[:, 0:256], in_to_replace=ts[:, h, 0:8], in_values=ch, imm_value=NEG),
                 [("cand", h), ("ts", h)], ["scw"])
            S.op("dve", lambda e, h=h: e.max(out=ts[:, h, 8:16], in_=scw[:, 0:256]), ["scw"], [("tsb", h)])
            S.op("dve", lambda e, h=h: e.max_index(out=pos[:, h, 8:16], in_max=ts[:, h, 8:16], in_values=scw[:, 0:256]),
                 ["scw", ("tsb", h)], [("posb", h)])
        allts = [("ts", h) for h in range(8)] + [("tsb", h) for h in range(8)]
        allpos = [("pos", h) for h in range(8)] + [("posb", h) for h in range(8)]
        pex3 = pex[:].rearrange("p (h k) -> p h k", h=8)
        TT("dve", pex3, ts[:], ts[:, :, 0:1].to_broadcast([128, 8, 16]), ALU.subtract, r=allts, w=["pex"])
        ACT(pex[:], pex[:], AF.Exp, r=["pex"], w=["pex"])
        S.op("dve", lambda e: e.tensor_reduce(out=psm[:, 0:8], in_=pex3, axis=AX.X, op=ALU.add), ["pex"], ["psm"])
        RCP(psm[:, 8:16], psm[:, 0:8], r=["psm"], w=["psm"])
        TT("dve", pgate[:].rearrange("p (h k) -> p h k", h=8), pex3, psm[:, 8:16].unsqueeze(2).to_broadcast([128, 8, 16]),
           ALU.mult, r=["pex", "psm"], w=["pgate"])
        posi = pos[:].bitcast(I32)
        S.op("dve", lambda e: e.tensor_single_scalar(out=pa[:, 0], in_=posi, scalar=4, op=ALU.logical_shift_right), allpos, ["pa0"])
        S.op("dve", lambda e: e.tensor_single_scalar(out=pa[:, 1], in_=posi, scalar=15, op=ALU.bitwise_and), allpos, ["pa1"])
        CP("dve", paf[:], pa[:], r=["pa0", "pa1"], w=["paf"])
        CP("dve", i16f[:], i16[:], r=alli, w=["i16f"])
        i16fv = i16f[:].rearrange("p (h f) k -> p h f k", f=2)
        for f in range(2):
            for h in range(8):
                ohh = oh[:, h, :].rearrange("p (r a) -> p r a", r=16)
                TT("dve", ohh, paf[:, f, h, :].unsqueeze(2).to_broadcast([128, 16, 16]),
                   iota16[:].unsqueeze(1).to_broadcast([128, 16, 16]), ALU.is_equal, r=["paf", "iota16"], w=[("oh", h)])
                TT("dve", ohh, ohh, i16f[:, 2 * h + f, :].unsqueeze(1).to_broadcast([128, 16, 16]), ALU.mult,
                   r=[("oh", h), "i16f"], w=[("oh", h)])
            S.op("dve", lambda e, f=f: e.tensor_reduce(out=isel[:, f, :], in_=oh[:].rearrange("p h (r a) -> p (h r) a", r=16),
                                                       axis=AX.X, op=ALU.add), [("oh", h) for h in range(8)], [("isel", f)])
        STT(eidx[:], isel[:, 0, :], 128.0, isel[:, 1, :], ALU.mult, ALU.add, r=[("isel", 0), ("isel", 1)], w=["eidx"])

    gctr = [0]

    def phase_D(s):
        DMA("sp", wps[:], wps_d, "dw", r=["wps_d"], w=["wps"])
        DMA("pool", wo[:], wo_d.rearrange("(c p) n -> p c n", p=128), "pw", w=["wo"])
        DMA("sp", G1[:], modscr[s:s + 1, 2 * D:3 * D].partition_broadcast(128), "dw", r=["modscr"], w=["G1"])
        DMA("sp", B2[:], modscr[s:s + 1, 3 * D:4 * D].partition_broadcast(128), "dw", r=["modscr"], w=["B2"])
        DMA("sp", A2[:], modscr[s:s + 1, 4 * D:5 * D].partition_broadcast(128), "dw", r=["modscr"], w=["A2"])
        DMA("sp", G2[:], modscr[s:s + 1, 5 * D:6 * D].partition_broadcast(128), "dw", r=["modscr"], w=["G2"])
        DMA("sp", FG[:], gffn_d.partition_broadcast(128), "dw", w=["FG"])
        STT(A2[:], A2[:], 1.0, FG[:], ALU.add, ALU.mult, r=["A2", "FG"], w=["A2"])
        DMA("sp", FG[:], fg_d.partition_broadcast(128), "dw", r=["A2"], w=["FG"])
        for lt in range(NLT):
            tok = slice(lt * 128, (lt + 1) * 128)
            DMA("sp", dxin[:], x_d[s, tok, :], "dx", w=["dxin"])
            for hf in range(2):
                fs = slice(hf * 512, (hf + 1) * 512)
                for c in range(8):
                    MM(psA[:, fs], yThg[:, c, tok], wo[:, c, fs], start=(c == 0), stop=(c == 7), r=[("yThg", lt), "wo"], w=["psA%d" % hf])
                TT("dve", x1[:, fs], psA[:, fs], G1[:, fs], ALU.mult, r=["psA%d" % hf, "G1"], w=[("x1", hf)])
                TT("dve", x1[:, fs], x1[:, fs], dxin[:, fs], ALU.add, r=[("x1", hf), "dxin"], w=[("x1", hf)])
            x1k = [("x1", 0), ("x1", 1)]
            if "x1" in dbg_out:
                DMA("sp", dbg_out["x1"][lt * 128:(lt + 1) * 128, :], x1[:], "dbg", r=x1k)
            ACT(djunk[:], x1[:], AF.Square, r=x1k, w=["djunk", "ssD"], accum=small[:, 4:5])
            rstd_from_ss(small[:, 4:5], small[:, 5:6], D, ["ssD"], "rsD")
            STT(h2[:], x1[:], small[:, 5:6], A2[:], ALU.mult, ALU.mult, r=x1k + ["rsD", "A2"], w=["h2"])
            TT("dve", h2[:], h2[:], B2[:], ALU.add, r=["h2", "B2"], w=["h2"])
            for c in range(8):
                TR(psA[:, c * 128:(c + 1) * 128], h2[:, c * 128:(c + 1) * 128], identF[:], r=["h2", "identF"], w=["psA%d" % (c // 4)])
            ACP(h2T[:].rearrange("p c t -> p (c t)"), psA[:], r=["psA0", "psA1"], w=["h2T"])
            for q in range(4):
                pt = (psB, psC)[q // 2]
                pk = ("psB0", "psB1", "psC0", "psC1")[q]
                for c in range(8):
                    MM(pt[:, (q % 2) * 512:(q % 2 + 1) * 512], h2T[:, c, :], wps[:, c, q * 512:(q + 1) * 512], start=(c == 0), stop=(c == 7),
                       r=["h2T", "wps"], w=[pk])
                if q % 2 == 0:
                    ACP(sc[:, q * 512:(q + 1) * 512], pt[:, 0:512], r=[pk], w=["sc"])
                else:
                    CP("dve", sc[:, q * 512:(q + 1) * 512], pt[:, 512:1024], r=[pk], w=["sc"])
            peer_topk()
            GRP = 4
            S.barrier()
            for g0 in range(0, 128, GRP):
                ks = []
                for j in range(g0, g0 + GRP):
                    k = gctr[0] % NGB
                    gctr[0] += 1
                    ks.append(k)
                    S.dma("pool", lambda e, k=k, j=j: e.indirect_dma_start(out=gb[k][:], out_offset=None, in_=puv_d,
                                                                           in_offset=bass.IndirectOffsetOnAxis(ap=eidx[:, j:j + 1], axis=0)),
                          "gb%d" % (k % 8), ["eidx", "ptab"], [("gb", k)])
                    STT(djunk[:], gb[k][:, 0:D], 1.0, h2[:], ALU.mult, ALU.mult, r=[("gb", k), "h2"], w=["djunk", ("pact", g0)],
                        accum=pact[:, j:j + 1])
                ACT(pact[:, g0:g0 + GRP], pact[:, g0:g0 + GRP], AF.Gelu, r=[("pact", g0)], w=[("pact", g0)])
                TT("dve", pact[:, g0:g0 + GRP], pact[:, g0:g0 + GRP], pgate[:, g0:g0 + GRP], ALU.mult, r=[("pact", g0), "pgate"], w=[("pact", g0)])
                for j, k in zip(range(g0, g0 + GRP), ks):
                    dk = j % NDG
                    AMUL(dg[dk][:], identF[:], pact[:, j:j + 1], r=["identF", ("pact", g0)], w=[("dg", dk)])
                    for hf in range(2):
                        MM(psA[:, hf * 512:(hf + 1) * 512], dg[dk][:], gb[k][:, D + hf * 512:D + (hf + 1) * 512], start=(j == 0), stop=(j == 127),
                           r=[("dg", dk), ("gb", k)], w=["psA%d" % hf])
            S.barrier()
            for hf in range(2):
                fs = slice(hf * 512, (hf + 1) * 512)
                TT("dve", acc[:, fs], psA[:, fs], G2[:, fs], ALU.mult, r=["psA%d" % hf, "G2", "acc"], w=["acc"])
                TT("dve", acc[:, fs], acc[:, fs], x1[:, fs], ALU.add, r=["acc", ("x1", hf)], w=["acc"])
            ACT(djunk[:], acc[:], AF.Square, r=["acc"], w=["djunk", "ssF"], accum=small[:, 6:7])
            rstd_from_ss(small[:, 6:7], small[:, 7:8], D, ["ssF"], "rsF")
            STT(h2[:], acc[:], small[:, 7:8], FG[:], ALU.mult, ALU.mult, r=["acc", "rsF", "FG"], w=["h2"])
            DMA("sp", out_d[s, tok, :], h2[:], "out", r=["h2"])

    def dump_mixer():
        S.barrier()
        if "yThg" in dbg_out:
            DMA("sp", dbg_out["yThg"], yThg[:], "dbg", r=[("yThg", lt) for lt in range(NLT)])
            DMA("sp", dbg_out["yTgl"], yTgl[:], "dbg", r=[("yTgl", lt) for lt in range(NLT)])
            DMA("sp", dbg_out["hT"], hT[:], "dbg", r=[("hT", t) for t in range(NT)])
        if "grT" in dbg_out:
            DMA("sp", dbg_out["grT"], grT[:], "dbg", r=["grT"])

    def finish():
        S.emit(final_wait_keys=["out"] + (["dbg"] if dbg_out else []))
        return nc

    for s in range(NS):
        load_seq_mod(s)
        phase_A(s)
        S.barrier()
        if stop == "A":
            dump_mixer()
            S.emit(final_wait_keys=["dbg"])
            return nc
        gr_prepass()
        S.barrier()
        if stop == "gr":
            dump_mixer()
            S.emit(final_wait_keys=["dbg"])
            return nc
        hi = 0
        heads = [("hg", h) for h in range(8)] + [("gl", h) for h in range(4)]
        if stop == "head0":
            heads = [("hg", 0)]
        if stop == "head8":
            heads = [("gl", 0)]
        load_head_w(heads[0][0], heads[0][1], 0)
        for hi, (kind, hh) in enumerate(heads):
            if hi + 1 < len(heads):
                load_head_w(heads[hi + 1][0], heads[hi + 1][1], (hi + 1) % 2)
            head(kind, hh, hi % 2, s)
        if s == 0 and (dbg_out or stop in ("head0", "head8", "heads")):
            dump_mixer()
        if stop in ("head0", "head8", "heads"):
            S.emit(final_wait_keys=["dbg"])
            return nc
        S.barrier()
        phase_C1(s)
        S.barrier()
        phase_D(s)
        S.barrier()
    S.emit(final_wait_keys=["out"] + (["dbg"] if dbg_out else []))
    return nc


_CACHE = {}


def _in_maps(inputs, NS, cores):
    f = lambda a: np.ascontiguousarray(a, dtype=np.float32)
    shared = {
        "c_ctx": f(inputs["c_ctx"]).reshape(1, D),
        "ada_w": f(inputs["ada_w"]).reshape(D, 6 * D),
        "ada_b": f(inputs["ada_b"]).reshape(1, 6 * D),
        "norm_mix_g": f(inputs["norm_mix_g"]).reshape(1, D),
        "w_in": f(inputs["w_in"]).reshape(D, D_IN),
        "lb_logits": f(inputs["hgrn_lb_logits"]).reshape(2, 2 * D),
        "hgrn_norm_g": f(inputs["hgrn_norm_g"]).reshape(1, 128),
        "gla_gk_w": f(inputs["gla_gk_w"]).reshape(2, 16, 512),
        "gla_gk_b": f(inputs["gla_gk_b"]).reshape(2, 512),
        "gla_norm_g": f(inputs["gla_norm_g"]).reshape(1, 256),
        "w_branch_hgrn": f(inputs["w_branch_hgrn"]).reshape(D, D),
        "w_branch_gla": f(inputs["w_branch_gla"]).reshape(D, D),
        "w_out": f(inputs["w_out"]).reshape(D, D),
        "norm_ffn_g": f(inputs["norm_ffn_g"]).reshape(1, D),
        "peer_wq": f(inputs["peer_wq"]).reshape(D, 2048),
        "peer_k1": f(inputs["peer_k1"]).reshape(128, 128),
        "peer_k2": f(inputs["peer_k2"]).reshape(128, 128),
        "peer_u": f(inputs["peer_u"]).reshape(16384, D),
        "peer_v": f(inputs["peer_v"]).reshape(16384, D),
        "final_g": f(inputs["final_g"]).reshape(1, D),
    }
    x = f(inputs["x"])
    ctx = f(inputs["ctx"])
    c = f(inputs["c"])
    maps = []
    for i in range(cores):
        m = dict(shared)
        m["x"] = x[i * NS:(i + 1) * NS]
        m["ctx"] = ctx[i * NS:(i + 1) * NS]
        m["c"] = c[i * NS:(i + 1) * NS]
        maps.append(m)
    return maps


def kernel(**inputs):
    B = inputs["x"].shape[0]
    NS = B // N_CORES
    if NS not in _CACHE:
        _CACHE[NS] = build_program(NS)
    nc = _CACHE[NS]
    maps = _in_maps(inputs, NS, N_CORES)
    res = run_bass_kernel_spmd(nc, maps, core_ids=list(range(N_CORES)))
    return np.concatenate([r["out"] for r in res.results], axis=0).astype(np.float32)
```
